# Optimizing a Trainium2 kernel written in Bass

```python
import jax, jax.numpy as jnp
from jax import lax
import numpy as np

D_MODEL = 2048
BATCH = 4
SEQ = 8192
DEPTH = 4
DEC_BATCH = 8
DEC_SEQ = 32
PAST_LEN = 1024

CHUNK = 64
CONV_DIM = D_MODEL // 4
CONV_WIDTH = 3
ATT_HEAD_DIM = 128
ATT_DIM = D_MODEL // 2
ATT_HEADS = ATT_DIM // ATT_HEAD_DIM
BAND_CHUNKS = 8
BAND_PAST = BAND_CHUNKS * CHUNK
BAND_LEN = BAND_PAST + CHUNK
MAX_REL = 128
ML_HEAD_DIM = 128
ML_DIM = D_MODEL // 4
ML_HEADS = ML_DIM // ML_HEAD_DIM
D_FF = 4 * D_MODEL
EPS = 1e-6
NEG_INF = -1e30
IN_SPLITS = (CONV_DIM, CONV_DIM, CONV_DIM, ATT_DIM, ATT_DIM, ATT_DIM, ML_DIM, ML_DIM, ML_DIM, ML_DIM, ML_HEADS, ML_HEADS)
IN_DIM = sum(IN_SPLITS)

kernel_name = "hybrid_streaming_encoder_step"


def rmsnorm(x, g):
    xf = x.astype(jnp.float32)
    y = xf * lax.rsqrt(jnp.mean(xf * xf, axis=-1, keepdims=True) + EPS)
    return (y * g.astype(jnp.float32)).astype(x.dtype)


def split_proj(proj):
    idx = [int(i) for i in np.cumsum(IN_SPLITS)[:-1]]
    return jnp.split(proj, idx, axis=-1)


def short_conv(xa, gate_b, gate_c, conv_w, buf):
    u = gate_c * xa
    u_ext = jnp.concatenate([buf.astype(u.dtype), u], axis=1)
    L = u.shape[1]
    y = sum(conv_w[j] * u_ext[:, j:j + L] for j in range(CONV_WIDTH))
    return gate_b * y, u_ext[:, -(CONV_WIDTH - 1):]


def band_attention(q, k, v, q_pos, k_pos, k_valid, rel_bias):
    s = jnp.einsum('bqhd,bkhd->bhqk', q, k).astype(jnp.float32) * (ATT_HEAD_DIM ** -0.5)
    rel = jnp.clip(q_pos[:, None] - k_pos[None, :], -MAX_REL, MAX_REL) + MAX_REL
    s = s + rel_bias.astype(jnp.float32)[:, rel][None]
    s = jnp.where(k_valid[None, None, None, :], s, NEG_INF)
    p = jax.nn.softmax(s, axis=-1)
    return jnp.einsum('bhqk,bkhd->bqhd', p.astype(v.dtype), v)


def prompt_attention(q, k, v, rel_bias):
    B, L = q.shape[0], q.shape[1]
    nc = L // CHUNK
    pad = ((0, 0), (BAND_PAST, 0), (0, 0), (0, 0))
    kp = jnp.pad(k, pad)
    vp = jnp.pad(v, pad)

    def one_chunk(c):
        start = c * CHUNK
        qc = lax.dynamic_slice_in_dim(q, start, CHUNK, axis=1)
        kb = lax.dynamic_slice_in_dim(kp, start, BAND_LEN, axis=1)
        vb = lax.dynamic_slice_in_dim(vp, start, BAND_LEN, axis=1)
        q_pos = start + jnp.arange(CHUNK)
        k_pos = start - BAND_PAST + jnp.arange(BAND_LEN)
        return band_attention(qc, kb, vb, q_pos, k_pos, k_pos >= 0, rel_bias)

    out = lax.map(one_chunk, jnp.arange(nc))
    return out.transpose(1, 0, 2, 3, 4).reshape(B, L, ATT_HEADS, ATT_HEAD_DIM)


def mlstm_chunk(state, q, k, v, log_i, log_f):
    c0, n0, m0 = state
    L = q.shape[2]
    b = jnp.cumsum(log_f, axis=-1)
    causal = jnp.tril(jnp.ones((L, L), dtype=bool))
    d = jnp.where(causal, b[..., :, None] - b[..., None, :] + log_i[..., None, :], -jnp.inf)
    inter = b + m0[..., None]
    m = jnp.maximum(inter, jnp.max(d, axis=-1))
    w_intra = jnp.exp(d - m[..., None])
    w_inter = jnp.exp(inter - m)
    s = jnp.einsum('bhtd,bhsd->bhts', q, k) * w_intra
    num = w_inter[..., None] * jnp.einsum('bhtd,bhde->bhte', q, c0) + jnp.einsum('bhts,bhse->bhte', s, v)
    den = w_inter * jnp.einsum('bhtd,bhd->bht', q, n0) + jnp.sum(s, axis=-1)
    h = num / jnp.maximum(jnp.abs(den), jnp.exp(-m))[..., None]
    b_last = b[..., -1]
    g = b_last[..., None] - b + log_i
    m_new = jnp.maximum(b_last + m0, jnp.max(g, axis=-1))
    w_state = jnp.exp(g - m_new[..., None])
    decay = jnp.exp(b_last + m0 - m_new)
    kw = k * w_state[..., None]
    c_new = decay[..., None, None] * c0 + jnp.einsum('bhsd,bhse->bhde', kw, v)
    n_new = decay[..., None] * n0 + jnp.sum(kw, axis=2)
    return (c_new, n_new, m_new), h


def mlstm_prompt(q, k, v, li, lf):
    B, L = q.shape[0], q.shape[1]
    nc = L // CHUNK

    def to_chunks4(a):
        return a.reshape(B, nc, CHUNK, ML_HEADS, ML_HEAD_DIM).transpose(1, 0, 3, 2, 4)

    def to_chunks3(a):
        return a.reshape(B, nc, CHUNK, ML_HEADS).transpose(1, 0, 3, 2)

    init = (jnp.zeros((B, ML_HEADS, ML_HEAD_DIM, ML_HEAD_DIM), jnp.float32),
            jnp.zeros((B, ML_HEADS, ML_HEAD_DIM), jnp.float32),
            jnp.zeros((B, ML_HEADS), jnp.float32))

    def step(carry, xs):
        return mlstm_chunk(carry, *xs)

    xs = (to_chunks4(q), to_chunks4(k), to_chunks4(v), to_chunks3(li), to_chunks3(lf))
    final, h = lax.scan(step, init, xs)
    h = h.transpose(1, 0, 3, 2, 4).reshape(B, L, ML_HEADS, ML_HEAD_DIM)
    return h, final


def mlstm_sample(q, k, v, li, lf, state):
    st = tuple(a.astype(jnp.float32) for a in state)
    new_state, h = mlstm_chunk(st, q.transpose(0, 2, 1, 3), k.transpose(0, 2, 1, 3), v.transpose(0, 2, 1, 3),
                               li.transpose(0, 2, 1), lf.transpose(0, 2, 1))
    return h.transpose(0, 2, 1, 3), new_state


def trunk_layer(x, conv_buf, past_k, past_v, ml_state, norm_mix_g, w_in, conv_w, q_norm_g, k_norm_g, rel_bias,
                b_igate, b_fgate, mlstm_norm_g, w_out, norm_mlp_g, w_up, w_down):
    B, L, _ = x.shape
    h = rmsnorm(x, norm_mix_g)
    xa, gb, gc, q, k, v, mq, mk, mv, mo, ig, fg = split_proj(h @ w_in)
    ya, conv_new = short_conv(xa, gb, gc, conv_w, conv_buf)
    q = rmsnorm(q.reshape(B, L, ATT_HEADS, ATT_HEAD_DIM), q_norm_g)
    k = rmsnorm(k.reshape(B, L, ATT_HEADS, ATT_HEAD_DIM), k_norm_g)
    v = v.reshape(B, L, ATT_HEADS, ATT_HEAD_DIM)
    if past_k is None:
        att = prompt_attention(q, k, v, rel_bias)
        keep = min(BAND_PAST, L)
        k_rows, v_rows = k[:, L - keep:], v[:, L - keep:]
    else:
        lc = past_k.shape[1]
        kb = jnp.concatenate([past_k.astype(k.dtype), k], axis=1)
        vb = jnp.concatenate([past_v.astype(v.dtype), v], axis=1)
        k_pos = jnp.arange(-lc, L)
        att = band_attention(q, kb, vb, jnp.arange(L), k_pos, jnp.ones((lc + L,), dtype=bool), rel_bias)
        k_rows, v_rows = k, v
    mq = mq.reshape(B, L, ML_HEADS, ML_HEAD_DIM).astype(jnp.float32)
    mk = mk.reshape(B, L, ML_HEADS, ML_HEAD_DIM).astype(jnp.float32) * (ML_HEAD_DIM ** -0.5)
    mv = mv.reshape(B, L, ML_HEADS, ML_HEAD_DIM).astype(jnp.float32)
    li = (ig + b_igate).astype(jnp.float32)
    lf = jax.nn.log_sigmoid((fg + b_fgate).astype(jnp.float32))
    if ml_state is None:
        h_ml, ml_new = mlstm_prompt(mq, mk, mv, li, lf)
    else:
        h_ml, ml_new = mlstm_sample(mq, mk, mv, li, lf, ml_state)
    yc = rmsnorm(h_ml, mlstm_norm_g.reshape(ML_HEADS, ML_HEAD_DIM)).astype(x.dtype)
    yc = yc * jax.nn.sigmoid(mo).reshape(B, L, ML_HEADS, ML_HEAD_DIM)
    mixed = jnp.concatenate([ya, att.reshape(B, L, ATT_DIM), yc.reshape(B, L, ML_DIM)], axis=-1)
    x = x + mixed @ w_out
    h2 = rmsnorm(x, norm_mlp_g)
    x = x + jnp.square(jax.nn.relu(h2 @ w_up)) @ w_down
    return x, conv_new, k_rows, v_rows, ml_new


def setup_inputs(seed: int = 0) -> dict:
    key = jax.random.key(seed)
    ks = jax.random.split(key, 24)

    def nrm(k, shape, scale):
        return jax.random.normal(k, shape, jnp.float32) * scale

    att_rows = min(BAND_PAST, PAST_LEN)
    return {
        "x_prompt": nrm(ks[0], (BATCH, SEQ, D_MODEL), 1.0),
        "x_sample": nrm(ks[1], (DEC_BATCH, DEC_SEQ, D_MODEL), 1.0),
        "cache_att_k": nrm(ks[2], (DEPTH, DEC_BATCH, att_rows, ATT_HEADS, ATT_HEAD_DIM), 1.0),
        "cache_att_v": nrm(ks[3], (DEPTH, DEC_BATCH, att_rows, ATT_HEADS, ATT_HEAD_DIM), 1.0),
        "state_conv": nrm(ks[4], (DEPTH, DEC_BATCH, CONV_WIDTH - 1, CONV_DIM), 1.0),
        "state_mlstm_c": nrm(ks[5], (DEPTH, DEC_BATCH, ML_HEADS, ML_HEAD_DIM, ML_HEAD_DIM), 0.5),
        "state_mlstm_n": nrm(ks[6], (DEPTH, DEC_BATCH, ML_HEADS, ML_HEAD_DIM), 0.5),
        "state_mlstm_m": nrm(ks[7], (DEPTH, DEC_BATCH, ML_HEADS), 1.0),
        "norm_mix_g": 1.0 + nrm(ks[8], (DEPTH, D_MODEL), 0.05),
        "w_in": nrm(ks[9], (DEPTH, D_MODEL, IN_DIM), D_MODEL ** -0.5),
        "conv_w": nrm(ks[10], (DEPTH, CONV_WIDTH, CONV_DIM), CONV_WIDTH ** -0.5),
        "q_norm_g": 1.0 + nrm(ks[11], (DEPTH, ATT_HEAD_DIM), 0.05),
        "k_norm_g": 1.0 + nrm(ks[12], (DEPTH, ATT_HEAD_DIM), 0.05),
        "rel_bias": nrm(ks[13], (DEPTH, ATT_HEADS, 2 * MAX_REL + 1), 0.2),
        "b_igate": nrm(ks[14], (DEPTH, ML_HEADS), 0.1),
        "b_fgate": jnp.linspace(3.0, 6.0, ML_HEADS, dtype=jnp.float32)[None, :] + nrm(ks[15], (DEPTH, ML_HEADS), 0.1),
        "mlstm_norm_g": 1.0 + nrm(ks[16], (DEPTH, ML_DIM), 0.05),
        "w_out": nrm(ks[17], (DEPTH, D_MODEL, D_MODEL), D_MODEL ** -0.5),
        "norm_mlp_g": 1.0 + nrm(ks[18], (DEPTH, D_MODEL), 0.05),
        "w_up": nrm(ks[19], (DEPTH, D_MODEL, D_FF), D_MODEL ** -0.5),
        "w_down": nrm(ks[20], (DEPTH, D_FF, D_MODEL), D_FF ** -0.5),
    }


def reference(x_prompt, x_sample, cache_att_k, cache_att_v, state_conv, state_mlstm_c, state_mlstm_n, state_mlstm_m,
              norm_mix_g, w_in, conv_w, q_norm_g, k_norm_g, rel_bias, b_igate, b_fgate, mlstm_norm_g, w_out,
              norm_mlp_g, w_up, w_down):
    yp, ys = x_prompt, x_sample
    pc, pk, pv, pcc, pn, pm = [], [], [], [], [], []
    sc, sk, sv, scc, sn, sm = [], [], [], [], [], []
    for l in range(DEPTH):
        params = (norm_mix_g[l], w_in[l], conv_w[l], q_norm_g[l], k_norm_g[l], rel_bias[l], b_igate[l], b_fgate[l],
                  mlstm_norm_g[l], w_out[l], norm_mlp_g[l], w_up[l], w_down[l])
        zero_buf = jnp.zeros((yp.shape[0], CONV_WIDTH - 1, CONV_DIM), yp.dtype)
        yp, cb, kr, vr, (c_, n_, m_) = trunk_layer(yp, zero_buf, None, None, None, *params)
        pc.append(cb); pk.append(kr); pv.append(vr); pcc.append(c_); pn.append(n_); pm.append(m_)
        ys, cb, kr, vr, (c_, n_, m_) = trunk_layer(ys, state_conv[l], cache_att_k[l], cache_att_v[l],
                                                   (state_mlstm_c[l], state_mlstm_n[l], state_mlstm_m[l]), *params)
        sc.append(cb); sk.append(kr); sv.append(vr); scc.append(c_); sn.append(n_); sm.append(m_)
    p_conv, p_k, p_v = jnp.stack(pc), jnp.stack(pk), jnp.stack(pv)
    p_c, p_n, p_m = jnp.stack(pcc), jnp.stack(pn), jnp.stack(pm)
    s_conv, s_k, s_v = jnp.stack(sc), jnp.stack(sk), jnp.stack(sv)
    s_c, s_n, s_m = jnp.stack(scc), jnp.stack(sn), jnp.stack(sm)
    return (yp, ys, p_conv, p_k, p_v, p_c, p_n, p_m, s_conv, s_k, s_v, s_c, s_n, s_m)
```

```python
import contextlib
import numpy as np
import concourse.bass as bass
import concourse.mybir as mybir
from concourse.bass_utils import run_bass_kernel_spmd

F32 = mybir.dt.float32
BF16 = mybir.dt.bfloat16
ALU = mybir.AluOpType
AF = mybir.ActivationFunctionType

D = 2048
NCH = 16
HD = 128
NH = 8
MH = 4
DFF = 8192
IN_DIM = 6664
C_XA, C_GB, C_GC, C_Q, C_K, C_V, C_MQ, C_MK, C_MV, C_MO, C_IG, C_FG = (
    0, 512, 1024, 1536, 2560, 3584, 4608, 5120, 5632, 6144, 6656, 6660)
EPS = 1e-6
SCALE = 128 ** -0.5
NPAR = 66
UNITS_PER_LAYER = 104


class Buf:
    __slots__ = ("w", "r", "name")

    def __init__(self, name=""):
        self.w = {}
        self.r = {}
        self.name = name


class Prog:
    ENGS = ("pe", "act", "dve", "pool", "sp")

    def __init__(self):
        self.ops = []
        self.dma_count = {}
        self.dma_last = {}

    def key(self, i):
        o = self.ops[i]
        return ("d", o[3]) if o[3] is not None else o[0]

    def op(self, eng, fn, rd=(), wr=(), dma=None, wa=()):
        i = len(self.ops)
        raw = set()
        deps = {}

        def need(j):
            k = self.key(j)
            if deps.get(k, -1) < j:
                deps[k] = j

        for b in rd:
            for j in b.w.values():
                raw.add(j)
                need(j)
        for b in wr:
            for j in b.w.values():
                need(j)
            for j in b.r.values():
                need(j)
        for b in wa:
            for j in b.r.values():
                need(j)
        if dma is None:
            if eng in deps and deps[eng] not in raw:
                cand = [j for j in raw if self.key(j) == eng]
                if cand:
                    deps[eng] = max(cand)
                else:
                    del deps[eng]
            mykey = eng
            ordn = None
        else:
            mykey = ("d", dma)
            if dma in self.dma_last:
                need(self.dma_last[dma])
            self.dma_last[dma] = i
            self.dma_count[dma] = self.dma_count.get(dma, 0) + 1
            ordn = self.dma_count[dma]
        self.ops.append([eng, fn, sorted(deps.values()), dma, ordn, False, 0])
        for b in rd:
            b.r[mykey] = i
        for b in wr:
            b.w = {mykey: i}
            b.r = {}
        for b in wa:
            b.w[mykey] = i
        return i

    def emit(self, nc, stack):
        ops = self.ops
        sems = {e: stack.enter_context(nc.semaphore("s_" + e)) for e in self.ENGS}
        dsems = {d: stack.enter_context(nc.semaphore("d_%d" % d)) for d in sorted(self.dma_count)}
        waited = {e: {} for e in self.ENGS}
        waits = []
        for i, o in enumerate(ops):
            e = o[0]
            wl = []
            for j in o[2]:
                k = self.key(j)
                if waited[e].get(k, -1) >= j:
                    continue
                waited[e][k] = j
                wl.append(j)
                if ops[j][3] is None:
                    ops[j][5] = True
            waits.append(wl)
        cnt = {e: 0 for e in self.ENGS}
        for o in ops:
            if o[3] is None and o[5]:
                cnt[o[0]] += 1
                o[6] = cnt[o[0]]
        per = {e: [] for e in self.ENGS}
        for i, o in enumerate(ops):
            per[o[0]].append(i)

        def run(engname, eng):
            for i in per[engname]:
                o = ops[i]
                for j in waits[i]:
                    oj = ops[j]
                    if oj[3] is not None:
                        eng.wait_ge(dsems[oj[3]], 16 * oj[4])
                    else:
                        eng.wait_ge(sems[oj[0]], oj[6])
                ins = o[1](eng)
                if o[3] is not None:
                    ins.then_inc(dsems[o[3]], 16)
                elif o[5]:
                    ins.then_inc(sems[engname], 1)

        with nc.Block() as block:
            @block.tensor
            def _(e):
                run("pe", e)

            @block.scalar
            def _(e):
                run("act", e)

            @block.vector
            def _(e):
                run("dve", e)

            @block.gpsimd
            def _(e):
                run("pool", e)

            @block.sync
            def _(e):
                run("sp", e)


def unit_table():
    u = []
    for h in range(MH):
        u.append(("gate", h))
    for base in (C_MQ, C_MK, C_MO):
        for j in range(2):
            u.append(("fm", "win", 0, base + j * 256, base + j * 256 + 128))
    for base in (C_MK, C_MV):
        for j in range(2):
            u.append(("tm", base + j * 256))
    for c in range(4):
        u.append(("fm", "win", 0, C_GC + c * 128, C_XA + c * 128))
    for j in range(2):
        u.append(("fm", "win", 0, C_GB + j * 256, C_GB + j * 256 + 128))
    for base in (C_Q, C_K):
        for j in range(4):
            u.append(("fm", "win", 0, base + j * 256, base + j * 256 + 128))
    for j in range(4):
        u.append(("tm", C_V + j * 256))
    for j in range(8):
        u.append(("fm", "wout", 0, j * 256, j * 256 + 128))
    for q in range(4):
        for j in range(8):
            u.append(("fm", "wup", 0, q * 2048 + j * 256, q * 2048 + j * 256 + 128))
        for j in range(8):
            u.append(("fm", "wdn", q * 2048, j * 256, j * 256 + 128))
    assert len(u) == UNITS_PER_LAYER
    return u


UNITS = unit_table()


def build(NL, NT, T, TS=32, do_sample=True, n_layers_in=None):
    NTOK = NT * T
    NB = T // 128
    NQ = T // 64
    nc = bass.Bass("TRN2", target_bir_lowering=False)
    P = Prog()

    def din(name, shape, dt=F32):
        return nc.dram_tensor(name, list(shape), dt, kind="ExternalInput").ap()

    def dout(name, shape, dt=F32):
        return nc.dram_tensor(name, list(shape), dt, kind="ExternalOutput").ap()

    xp = din("xp", [NTOK, D])
    xs = din("xs", [TS, D])
    ck = din("ck", [NL, 512, 1024])
    cv = din("cv", [NL, 512, 1024])
    sconv = din("sconv", [NL, 2, 512])
    smc = din("smc", [NL, MH, 128, 128])
    smn = din("smn", [NL, MH, 128])
    smm = din("smm", [NL, MH])
    win = din("win", [NL, D, IN_DIM])
    wout = din("wout", [NL, D, D])
    wup = din("wup", [NL, D, DFF])
    wdn = din("wdn", [NL, DFF, D])
    par_d = din("par", [NL, 128, NPAR])
    nbp_d = din("nbp", [NL, 128, NH * 256])
    nbs3_d = din("nbs3", [NL, 128, NH * TS])
    nbs4_d = din("nbs4", [NL, TS, NH * TS])
    ident_d = din("ident", [128, 128])
    negmask_d = din("negmask", [128, 128])
    WM = {"win": win, "wout": wout, "wup": wup, "wdn": wdn}

    yp = dout("yp", [NTOK, D])
    ys = dout("ys", [TS, D])
    KEEP = min(512, NTOK)
    o_pconv = dout("o_pconv", [NL, 2, 512])
    o_pk = dout("o_pk", [NL, KEEP, 1024])
    o_pv = dout("o_pv", [NL, KEEP, 1024])
    o_pc = dout("o_pc", [NL, MH, 128, 128])
    o_pn = dout("o_pn", [NL, MH, 128])
    o_pm = dout("o_pm", [NL, MH])
    o_sconv = dout("o_sconv", [NL, 2, 512])
    o_sk = dout("o_sk", [NL, TS, 1024])
    o_sv = dout("o_sv", [NL, TS, 1024])
    o_sc = dout("o_sc", [NL, MH, 128, 128])
    o_sn = dout("o_sn", [NL, MH, 128])
    o_sm = dout("o_sm", [NL, MH])

    wscr_l = [nc.dram_tensor("wscr%d" % l, [UNITS_PER_LAYER, 128, 4096], BF16).ap() for l in range(NL)]
    nbscr = nc.dram_tensor("nbscr", [NL, 128, NH * 256], BF16).ap()
    nbs3scr = nc.dram_tensor("nbs3scr", [NL, 128, NH * TS], BF16).ap()
    nbs4scr = nc.dram_tensor("nbs4scr", [NL, TS, NH * TS], BF16).ap()

    stack = contextlib.ExitStack()
    with stack:
        def sb(name, shape, dt=F32):
            return stack.enter_context(nc.sbuf_tensor("sb_" + name, list(shape), dt))

        banks = [stack.enter_context(nc.psum_tensor("bank%d" % i, [128, 512], F32)) for i in range(8)]
        bank_bufs = [Buf("bank%d" % i) for i in range(8)]
        pool_ctr = {"A": 0, "B": 0}

        def psA():
            i = pool_ctr["A"] % 4
            pool_ctr["A"] += 1
            return banks[i], bank_bufs[i]

        def psB():
            i = 4 + pool_ctr["B"] % 4
            pool_ctr["B"] += 1
            return banks[i], bank_bufs[i]

        xT = sb("xT", [128, NCH, T])
        hT = sb("hT", [128, NCH, T], BF16)
        big = sb("big", [128, NCH, T], BF16)
        wb = [sb("wb%d" % i, [128, 4096], BF16) for i in range(3)]
        wb_buf = [Buf("wb%d" % i) for i in range(3)]
        stage = sb("stage", [128, 2048])
        stA, stB = Buf("stA"), Buf("stB")
        kst = [sb("kst%d" % l, [128, NH, 512], BF16) for l in range(NL)]
        vst = [sb("vst%d" % l, [128, 4, 1024], BF16) for l in range(NL)]
        kst_b = [Buf() for _ in range(NL)]
        vst_b = [Buf() for _ in range(NL)]
        kcur = sb("kcur", [128, NH, T], BF16)
        vcur = sb("vcur", [128, NB, 1024], BF16)
        qT = sb("qT", [128, NH, T], BF16)
        kcur_b = [Buf() for _ in range(NH)]
        qT_b = [Buf() for _ in range(NH)]
        vcur_b = Buf()
        nbt = sb("nbt", [128, NH, 256], BF16)
        nbt3 = nbt[:, :, 0:TS]
        nbt4 = nbt[:, :, TS:2 * TS]
        nbt_b = Buf()
        g1 = sb("g1", [128, MH, T])
        g2 = sb("g2", [128, MH, T])
        g3 = sb("g3", [128, MH, T])
        g1_b = [Buf() for _ in range(MH)]
        g2_b = [Buf() for _ in range(MH)]
        g3_b = [Buf() for _ in range(MH)]
        mqT = sb("mqT", [128, MH, T], BF16)
        mkT = sb("mkT", [128, MH, T], BF16)
        sigo = sb("sigo", [128, MH, T], BF16)
        mqT_b = [Buf() for _ in range(MH)]
        mkT_b = [Buf() for _ in range(MH)]
        sigo_b = [Buf() for _ in range(MH)]
        mktok = sb("mktok", [128, NB, 512], BF16)
        mvtok = sb("mvtok", [128, NB, 512], BF16)
        mktok_b, mvtok_b = Buf(), Buf()
        uext = [sb("uext%d" % i, [128, T + 2]) for i in range(2)]
        uext_b = [Buf() for _ in range(2)]
        utmp = [sb("utmp%d" % i, [128, T]) for i in range(2)]
        utmp_b = [Buf() for _ in range(2)]
        rstd = [sb("rstd%d" % i, [128, T]) for i in range(2)]
        rstd_b = [Buf() for _ in range(2)]
        sqt = [sb("sqt%d" % i, [128, T], BF16) for i in range(2)]
        sqt_b = [Buf() for _ in range(2)]
        Eb = [sb("E%d" % i, [128, T], BF16) for i in range(2)]
        Eb_b = [Buf() for _ in range(2)]
        par = sb("par", [128, NL, NPAR])
        negbf = sb("negbf", [128, NL, MH])
        negch = sb("negch", [128, NL, NH])
        ident = sb("ident", [128, 128])
        identb = sb("identb", [128, 128], BF16)
        negmask = sb("negmask", [128, 128])
        ones_bf = sb("ones_bf", [128, 128], BF16)
        ones_f = sb("ones_f", [128, T])
        Cst = [sb("Cst%d" % l, [128, MH, 128]) for l in range(NL)]
        Cbf1 = sb("Cbf1", [128, MH, 128], BF16)
        nbf1 = sb("nbf1", [128, MH, 128], BF16)
        Cbf = [Cbf1 for l in range(NL)]
        cbf_b = [Buf() for _ in range(MH)]
        nst = [sb("nst%d" % l, [128, MH, 128]) for l in range(NL)]
        nbf = [nbf1 for l in range(NL)]
        carry = [sb("carry%d" % l, [128, 2 * MH]) for l in range(NL)]
        Cst_b = [[Buf() for _ in range(MH)] for _ in range(NL)]
        carry_b = [[Buf() for _ in range(MH)] for _ in range(NL)]
        cst = [sb("cst%d" % l, [128, 4, 2]) for l in range(NL)]
        cst_b = [[Buf() for _ in range(4)] for _ in range(NL)]
        xT_b = [Buf("xT%d" % c) for c in range(NCH)]
        hT_b = Buf("hT")
        big_b = [Buf("big%d" % c) for c in range(NCH)]
        NR = 2
        acol = [sb("acol%d" % i, [128, 2]) for i in range(NR)]
        Wt = [sb("Wt%d" % i, [128, 128]) for i in range(NR)]
        PT = [sb("PT%d" % i, [128, 128], BF16) for i in range(NR)]
        eint = [sb("eint%d" % i, [128, 128]) for i in range(NR)]
        qtil = [sb("qtil%d" % i, [128, 128], BF16) for i in range(NR)]
        wcol = [sb("wcol%d" % i, [128, 2]) for i in range(NR)]
        kw = [sb("kw%d" % i, [128, 128], BF16) for i in range(NR)]
        tmp_b = [[Buf() for _ in range(8)] for _ in range(NR)]
        ddt = [sb("ddt%d" % i, [128, MH, 128]) for i in range(1)]
        ddt_b = [Buf() for _ in range(1)]
        wgf = sb("wgf", [128, NCH, 8])
        wgf_b = Buf()
        mout = sb("mout", [128, 2 * MH])
        mout_b = Buf()

        dma_ids = {"n": 0}

        def new_dsem():
            dma_ids["n"] += 1
            return dma_ids["n"] - 1

        wsem = [new_dsem() for _ in range(3)]
        psem = [new_dsem() for _ in range(8)]
        msem = [new_dsem() for _ in range(6)]
        ctr = {"p": 0, "m": 0, "w": 0, "rot": 0, "sq": 0, "rs": 0, "E": 0, "u": 0, "t": 0, "dd": 0}

        def pdma():
            ctr["p"] += 1
            return psem[ctr["p"] % 8]

        def mdma():
            ctr["m"] += 1
            return msem[ctr["m"] % 6]

        def MM(out, lhsT, rhs, start, stop, rd, wr):
            P.op("pe", lambda e: e.matmul(out, lhsT, rhs, start=start, stop=stop), rd, wr)

        def TR(out, in_, idn, rd, wr):
            P.op("pe", lambda e: e.transpose(out, in_, idn), rd, wr)

        def ACT(out, in_, func, rd, wr, bias=None, scale=1.0):
            if bias is None:
                P.op("act", lambda e: e.activation(out, in_, func, scale=scale), rd, wr)
            else:
                P.op("act", lambda e: e.activation(out, in_, func, bias=bias, scale=scale), rd, wr)

        def TT(out, a, b, op, rd, wr, eng="dve"):
            P.op(eng, lambda e: e.tensor_tensor(out, a, b, op), rd, wr)

        def STT(out, in0, scalar, in1, op0, op1, rd, wr):
            P.op("dve", lambda e: e.scalar_tensor_tensor(out, in0, scalar, in1, op0, op1), rd, wr)

        def TS_(out, in0, s1, s2, op0, op1, rd, wr, eng="dve"):
            P.op(eng, lambda e: e.tensor_scalar(out, in0, s1, s2, op0, op1), rd, wr)

        def CP(out, in_, rd, wr, eng="dve"):
            P.op(eng, lambda e: e.tensor_copy(out, in_), rd, wr)

        def RCP(out, in_, rd, wr):
            P.op("dve", lambda e: e.reciprocal(out, in_), rd, wr)

        def SCAN(out, d0, d1, init, op0, op1, rd, wr):
            P.op("dve", lambda e: e.tensor_tensor_scan(out, d0, d1, init, op0, op1), rd, wr)

        def MSET(ap, val, wr, eng="pool"):
            P.op(eng, lambda e: e.memset(ap, val), (), wr)

        def DMA(eng, out, in_, rd, wr, sem, nonc=False):
            if nonc:
                def f(e):
                    with nc.allow_non_contiguous_dma(reason="small strided transfer"):
                        return e.dma_start(out=out, in_=in_)
            else:
                def f(e):
                    return e.dma_start(out=out, in_=in_)
            P.op(eng, f, rd, wr, dma=sem)

        marks = {}
        ALLB = Buf("init")
        DMA("pool", ident[:], ident_d, (), [ALLB], mdma())
        DMA("pool", negmask[:], negmask_d, (), [ALLB], mdma())
        DMA("pool", par[:], par_d.rearrange("l p n -> p l n"), (), [ALLB], mdma(), nonc=True)
        MSET(ones_bf[:], 1.0, [ALLB])
        MSET(ones_f[:], 1.0, [ALLB])
        CP(identb[:], ident[:], [ALLB], [ALLB], eng="pool")
        for l in range(NL):
            TS_(negbf[:, l, :], par[:, l, 54:58], -1.0, None, ALU.mult, ALU.bypass, [ALLB], [ALLB], eng="pool")
            TS_(negch[:, l, :], par[:, l, 58:66], -1.0, None, ALU.mult, ALU.bypass, [ALLB], [ALLB], eng="pool")

        def pr(l, a, b=None):
            return par[:, l, a:(a + 1 if b is None else b)]

        nbscr_b = [Buf() for _ in range(NL)]
        for l in range(NL):
            DMA("pool", stage[:, 0:NH * 256], nbp_d[l], [ALLB], [stA, stB], mdma())
            for h in range(NH):
                ACT(nbt[:, h, :], stage[:, h * 256:(h + 1) * 256], AF.Exp, [stA, stB, ALLB], [nbt_b],
                    bias=negch[:, l, h:h + 1])
            for h in range(NH):
                MSET(nbt[64:128, h, 0:64], 0.0, [nbt_b])
            DMA("pool", nbscr[l].rearrange("p (h n) -> p h n", h=NH), nbt[:], [nbt_b], [nbscr_b[l]], mdma())
            DMA("pool", stage[:, 0:NH * TS], nbs3_d[l], [nbt_b], [stA, stB], mdma())
            for h in range(NH):
                ACT(nbt3[:, h, :], stage[:, h * TS:(h + 1) * TS], AF.Exp, [stA, stB], [nbt_b],
                    bias=negch[:, l, h:h + 1])
            DMA("pool", nbs3scr[l].rearrange("p (h n) -> p h n", h=NH), nbt3, [nbt_b], [nbscr_b[l]], mdma())
            DMA("pool", stage[0:TS, 0:NH * TS], nbs4_d[l], [nbt_b], [stA, stB], mdma())
            for h in range(NH):
                ACT(nbt4[0:TS, h, :], stage[0:TS, h * TS:(h + 1) * TS], AF.Exp, [stA, stB], [nbt_b],
                    bias=negch[0:TS, l, h:h + 1])
            DMA("pool", nbs4scr[l].rearrange("p (h n) -> p h n", h=NH), nbt4[0:TS], [nbt_b], [nbscr_b[l]], mdma())

        marks['init'] = len(P.ops)
        wscr_b = [[Buf() for _ in range(UNITS_PER_LAYER)] for _ in range(NL)]

        def DMAw(out, in_, b, sem):
            P.op("pool", lambda e: e.dma_start(out=out, in_=in_), (), (), dma=sem, wa=[b])

        for l in range(NL):
            DMA("pool", wgf[:], win[l][:, C_IG:C_IG + 8].rearrange("(kc p) g -> p kc g", p=128),
                [wgf_b], [wgf_b], mdma(), nonc=True)
            for ui, u in enumerate(UNITS):
                dst = wscr_l[l][ui]
                if u[0] == "gate":
                    h = u[1]
                    rb = ui % 2
                    wv = wb[rb][:, :].rearrange("p (j kc m) -> p j kc m", j=2, kc=NCH)
                    for j, col in enumerate((h, 4 + h)):
                        CP(wv[:, j, :, :], wgf[:, :, col:col + 1].to_broadcast([128, NCH, 128]),
                           [wgf_b], [wb_buf[rb]], eng="pool")
                    DMA("pool", dst, wb[rb][:, :], [wb_buf[rb]], [wscr_b[l][ui]], pdma())
                elif u[0] == "fm":
                    W = WM[u[1]][l]
                    r0 = u[2]
                    for j, col in enumerate(u[3:5]):
                        src = W[r0:r0 + 2048, col:col + 128].rearrange("(kc p) m -> p kc m", p=128)
                        DMAw(dst[:, j * 2048:(j + 1) * 2048].rearrange("p (kc m) -> p kc m", m=128), src,
                             wscr_b[l][ui], pdma())
                else:
                    col = u[1]
                    src = win[l][:, col:col + 256].rearrange("(kc p) n -> p kc n", p=128)
                    DMAw(dst.rearrange("p (kc n) -> p kc n", n=256), src, wscr_b[l][ui], pdma())

        marks['prepass'] = len(P.ops)
        def wload(l, ui):
            s = ctr["w"] % 3
            ctr["w"] += 1
            DMA("sp", wb[s][:], wscr_l[l][ui], [wscr_b[l][ui]], [wb_buf[s]], wsem[s])
            return s

        def fm_unit(l, ui, rhs, rbufs, N, evac):
            s = wload(l, ui)
            for j in range(2):
                ps, pb = psA()
                for kc in range(NCH):
                    MM(ps[:, :N], wb[s][:, (j * NCH + kc) * 128:(j * NCH + kc + 1) * 128], rhs(kc),
                       kc == 0, kc == NCH - 1, [wb_buf[s]] + rbufs(kc), [pb])
                evac(j, ps, pb)

        def tm_unit(l, ui, N, evac):
            s = wload(l, ui)
            bs = min(128, N)
            for blk in range(N // bs):
                ps, pb = psA()
                for kc in range(NCH):
                    MM(ps[:bs, :256], hT[:, kc, blk * bs:(blk + 1) * bs], wb[s][:, kc * 256:(kc + 1) * 256],
                       kc == 0, kc == NCH - 1, [wb_buf[s], hT_b], [pb])
                evac(blk, ps, pb)

        def rot(name, n):
            i = ctr[name] % n
            ctr[name] += 1
            return i

        def rmsnorm_hT(l, goff, N):
            ps, pb = psA()
            for c in range(NCH):
                i = rot("sq", 2)
                ACT(sqt[i][:, :N], xT[:, c, :N], AF.Square, [xT_b[c]], [sqt_b[i]])
                MM(ps[:, :N], ones_bf[:], sqt[i][:, :N], c == 0, c == NCH - 1, [sqt_b[i]], [pb])
            r = rot("rs", 2)
            ACT(rstd[r][:, :N], ps[:, :N], AF.Sqrt, [pb], [rstd_b[r]], bias=epsc[:, 0:1], scale=1.0 / D)
            RCP(rstd[r][:, :N], rstd[r][:, :N], [rstd_b[r]], [rstd_b[r]])
            for c in range(NCH):
                STT(hT[:, c, :N], xT[:, c, :N], pr(l, goff + c), rstd[r][:, :N], ALU.mult, ALU.mult,
                    [xT_b[c], rstd_b[r]], [hT_b])

        epsc = sb("epsc", [128, 2])
        MSET(epsc[:, 0:1], EPS, [ALLB])
        MSET(epsc[:, 1:2], 1.0, [ALLB])
        m0s = sb("m0s", [128, MH])
        m0s_b = [Buf() for _ in range(MH)]
        ncol = sb("ncol", [128, MH])
        ncol_b = Buf()

        def headnorm(ps, pb, N, gcol, out_ap, out_rd, out_wr, also_f32=None):
            i = rot("sq", 2)
            ACT(sqt[i][:, :N], ps[:, :N], AF.Square, [pb], [sqt_b[i]])
            ps2, pb2 = psA()
            MM(ps2[:, :N], ones_bf[:], sqt[i][:, :N], True, True, [sqt_b[i]], [pb2])
            r = rot("rs", 2)
            ACT(rstd[r][:, :N], ps2[:, :N], AF.Sqrt, [pb2], [rstd_b[r]], bias=epsc[:, 0:1], scale=1.0 / HD)
            RCP(rstd[r][:, :N], rstd[r][:, :N], [rstd_b[r]], [rstd_b[r]])
            STT(out_ap, ps[:, :N], gcol, rstd[r][:, :N], ALU.mult, ALU.mult, [pb, rstd_b[r]] + out_rd, out_wr)
            if also_f32 is not None:
                STT(also_f32[0], ps[:, :N], gcol, rstd[r][:, :N], ALU.mult, ALU.mult, [pb, rstd_b[r]], also_f32[1])

        def layer(l, N, tt, is_sample, emit_kv, kv_row0, okk, ovv):
            bs = min(128, N)
            nblk = N // bs
            if is_sample:
                DMA("pool", nbt3, nbs3scr[l].rearrange("p (h n) -> p h n", h=NH), [nbscr_b[l]], [nbt_b], mdma())
                DMA("pool", nbt4[0:TS], nbs4scr[l].rearrange("p (h n) -> p h n", h=NH), [nbscr_b[l]], [nbt_b], mdma())
            else:
                DMA("pool", nbt[:], nbscr[l].rearrange("p (h n) -> p h n", h=NH), [nbscr_b[l]], [nbt_b], mdma())
            rmsnorm_hT(l, 0, N)
            rh = lambda kc: hT[:, kc, :N]
            rhb = lambda kc: [hT_b]
            ui = [0]

            def nxt():
                ui[0] += 1
                return ui[0] - 1

            for h in range(MH):
                def ev(j, ps, pb, h=h):
                    if j == 0:
                        ACT(g3[:, h, :N], ps[:, :N], AF.Identity, [pb], [g3_b[h]], bias=pr(l, 50 + h))
                    else:
                        ACT(g1[:, h, :N], ps[:, :N], AF.Exp, [pb], [g1_b[h]], bias=negbf[:, l, h:h + 1], scale=-1.0)
                        ACT(g1[:, h, :N], g1[:, h, :N], AF.Ln, [g1_b[h]], [g1_b[h]], bias=epsc[:, 1:2])
                        SCAN(g2[:, h, :N], ones_f[:, :N], g1[:, h, :N], carry[l][:, h:h + 1], ALU.mult, ALU.add,
                             [g1_b[h], carry_b[l][h]], [g2_b[h]])
                        CP(carry[l][:, h:h + 1], g2[:, h, N - 1:N], [g2_b[h]], [carry_b[l][h]])
                        TT(g3[:, h, :N], g3[:, h, :N], g2[:, h, :N], ALU.add, [g3_b[h], g2_b[h]], [g3_b[h]])
                        CP(m0s[:, h:h + 1], carry[l][:, 4 + h:5 + h], [carry_b[l][h]], [m0s_b[h]])
                        SCAN(g1[:, h, :N], ones_f[:, :N], g3[:, h, :N], m0s[:, h:h + 1], ALU.mult, ALU.max,
                             [g3_b[h], m0s_b[h], g1_b[h]], [g1_b[h]])
                        CP(carry[l][:, 4 + h:5 + h], g1[:, h, N - 1:N], [g1_b[h]], [carry_b[l][h]])
                        TT(g2[:, h, :N], g2[:, h, :N], g1[:, h, :N], ALU.subtract, [g2_b[h], g1_b[h]], [g2_b[h]])
                        ACT(g2[:, h, :N], g2[:, h, :N], AF.Exp, [g2_b[h]], [g2_b[h]])
                fm_unit(l, nxt(), rh, rhb, N, ev)
            for dst, dstb, kind in ((mqT, mqT_b, 0), (mkT, mkT_b, 1), (sigo, sigo_b, 2)):
                for u2 in range(2):
                    def ev(j, ps, pb, u2=u2, dst=dst, dstb=dstb, kind=kind):
                        hh = 2 * u2 + j
                        if kind == 2:
                            ACT(dst[:, hh, :N], ps[:, :N], AF.Sigmoid, [pb], [dstb[hh]])
                        elif kind == 0:
                            ACT(dst[:, hh, :N], ps[:, :N], AF.Copy, [pb], [dstb[hh]])
                        else:
                            CP(dst[:, hh, :N], ps[:, :N], [pb], [dstb[hh]])
                    fm_unit(l, nxt(), rh, rhb, N, ev)
            for dst, dstb in ((mktok, mktok_b), (mvtok, mvtok_b)):
                for u2 in range(2):
                    def ev(blk, ps, pb, u2=u2, dst=dst, dstb=dstb):
                        o = dst[:bs, blk, u2 * 256:(u2 + 1) * 256]
                        i_ = ps[:bs, :256]
                        P.op("dve", lambda e, o=o, i_=i_: e.tensor_copy(o, i_), [pb], (), wa=[dstb])
                    tm_unit(l, nxt(), N, ev)
            cvs = {}
            for c in range(4):
                def ev(j, ps, pb, c=c):
                    if j == 0:
                        i = rot("u", 2)
                        cvs["i"] = i
                        ACT(utmp[i][:, :N], ps[:, :N], AF.Copy, [pb], [utmp_b[i]])
                    else:
                        i = cvs["i"]
                        CP(uext[i][:, 0:2], cst[l][:, c, :], [cst_b[l][c]], [uext_b[i]], eng="pool")
                        TT(uext[i][:, 2:2 + N], utmp[i][:, :N], ps[:, :N], ALU.mult, [utmp_b[i], pb, uext_b[i]],
                           [uext_b[i]])
                        CP(cst[l][:, c, :], uext[i][:, N:N + 2], [uext_b[i]], [cst_b[l][c]], eng="pool")
                        a = stage[:, c * T:c * T + N]
                        ACT(a, uext[i][:, 0:N], AF.Copy, [uext_b[i]], [stA], scale=pr(l, 32 + 3 * c))
                        STT(a, uext[i][:, 1:N + 1], pr(l, 33 + 3 * c), a, ALU.mult, ALU.add, [uext_b[i], stA], [stA])
                        STT(a, uext[i][:, 2:N + 2], pr(l, 34 + 3 * c), a, ALU.mult, ALU.add, [uext_b[i], stA], [stA])
                fm_unit(l, nxt(), rh, rhb, N, ev)
            for u2 in range(2):
                def ev(j, ps, pb, u2=u2):
                    c = 2 * u2 + j
                    TT(big[:, c, :N], ps[:, :N], stage[:, c * T:c * T + N], ALU.mult, [pb, stA], [big_b[c]])
                fm_unit(l, nxt(), rh, rhb, N, ev)
            for u2 in range(4):
                def ev(j, ps, pb, u2=u2):
                    h = 2 * u2 + j
                    headnorm(ps, pb, N, pr(l, 44), qT[:, h, :N], [], [qT_b[h]])
                fm_unit(l, nxt(), rh, rhb, N, ev)
            for u2 in range(4):
                def ev(j, ps, pb, u2=u2):
                    h = 2 * u2 + j
                    if emit_kv:
                        i = rot("u", 2)
                        headnorm(ps, pb, N, pr(l, 45), kcur[:, h, :N], [], [kcur_b[h]],
                                 also_f32=(utmp[i][:, :N], [utmp_b[i]]))
                        for blk in range(nblk):
                            pt, ptb = psA()
                            TR(pt[:bs, 0:128], utmp[i][:, blk * bs:(blk + 1) * bs], ident[:], [utmp_b[i]], [ptb])
                            o = stage[:bs, blk * 1024 + h * 128:blk * 1024 + (h + 1) * 128]
                            i_ = pt[:bs, 0:128]
                            fw = (h == 0 and blk == 0)
                            P.op("act", lambda e, o=o, i_=i_: e.activation(o, i_, AF.Copy), [ptb],
                                 [stA, stB] if fw else (), wa=() if fw else [stA, stB])
                    else:
                        headnorm(ps, pb, N, pr(l, 45), kcur[:, h, :N], [], [kcur_b[h]])
                fm_unit(l, nxt(), rh, rhb, N, ev)
            if emit_kv:
                for blk in range(nblk):
                    DMA("pool", okk[l, kv_row0 + blk * bs:kv_row0 + (blk + 1) * bs, :],
                        stage[:bs, blk * 1024:(blk + 1) * 1024], [stA, stB], [], mdma())
            for u2 in range(4):
                def ev(blk, ps, pb, u2=u2):
                    o = vcur[:bs, blk, u2 * 256:(u2 + 1) * 256]
                    i_ = ps[:bs, :256]
                    P.op("dve", lambda e, o=o, i_=i_: e.tensor_copy(o, i_), [pb], (), wa=[vcur_b])
                    if emit_kv:
                        o2 = stage[:bs, blk * 1024 + u2 * 256:blk * 1024 + (u2 + 1) * 256]
                        fw = (u2 == 0 and blk == 0)
                        P.op("dve", lambda e, o2=o2, i_=i_: e.tensor_copy(o2, i_), [pb],
                             [stA, stB] if fw else (), wa=() if fw else [stA, stB])
                tm_unit(l, nxt(), N, ev)
            if emit_kv:
                for blk in range(nblk):
                    DMA("pool", ovv[l, kv_row0 + blk * bs:kv_row0 + (blk + 1) * bs, :],
                        stage[:bs, blk * 1024:(blk + 1) * 1024], [stA, stB], [], mdma())
            assert ui[0] == 32

            if is_sample:
                ktiles = [("p", jj, 128, 0, N, None) for jj in range(4)] + [("c", 0, N, 0, N, None)]
            else:
                ktiles = []
                chunk0 = tt * NQ
                for j in range(4 + NB):
                    kc0 = 2 * j - 8
                    if chunk0 + kc0 < 0:
                        continue
                    i0, i1 = max(0, kc0), min(NQ - 1, kc0 + 9)
                    if i0 > i1:
                        continue
                    ktiles.append(("p" if j < 4 else "c", j if j < 4 else j - 4, 128, i0 * 64, (i1 + 1) * 64, kc0))
            for h in range(NH):
                pv, pvb = psB()
                dn, dnb = psB()
                for ti, (src, jj, ksz, q0, q1, kc0) in enumerate(ktiles):
                    nq = q1 - q0
                    if src == "p":
                        kap, kb = kst[l][:, h, jj * 128:(jj + 1) * 128], kst_b[l]
                        vap, vb = vst[l][:, jj, h * 128:(h + 1) * 128], vst_b[l]
                    else:
                        kap, kb = kcur[:, h, jj * 128:jj * 128 + ksz], kcur_b[h]
                        vap, vb = vcur[:ksz, jj, h * 128:(h + 1) * 128], vcur_b
                    ps, pb = psA()
                    MM(ps[:ksz, :nq], kap, qT[:, h, q0:q1], True, True, [kb, qT_b[h]], [pb])
                    e = rot("E", 2)
                    ACT(Eb[e][:ksz, :nq], ps[:ksz, :nq], AF.Exp, [pb], [Eb_b[e]], bias=pr(l, 58 + h)[:ksz],
                        scale=SCALE)
                    if is_sample:
                        if src == "p" and jj == 3:
                            TT(Eb[e][:, :nq], Eb[e][:, :nq], nbt3[:, h, :], ALU.mult, [Eb_b[e], nbt_b], [Eb_b[e]])
                        elif src == "c":
                            TT(Eb[e][:ksz, :nq], Eb[e][:ksz, :nq], nbt4[:ksz, h, :], ALU.mult, [Eb_b[e], nbt_b],
                               [Eb_b[e]])
                    else:
                        ia, ib = q0 // 64, min(q1 // 64 - 1, kc0 + 3)
                        if ia <= ib:
                            TT(Eb[e][:, ia * 64 - q0:(ib + 1) * 64 - q0], Eb[e][:, ia * 64 - q0:(ib + 1) * 64 - q0],
                               nbt[:, h, (ia - kc0) * 64:(ib - kc0 + 1) * 64], ALU.mult, [Eb_b[e], nbt_b], [Eb_b[e]])
                        if (kc0 + 9) * 64 < q1:
                            c9 = (kc0 + 9) * 64 - q0
                            MSET(Eb[e][0:64, c9:c9 + 64], 0.0, [Eb_b[e]])
                    first, last = ti == 0, ti == len(ktiles) - 1
                    MM(pv[:, q0:q1], vap, Eb[e][:ksz, :nq], first, last, [vb, Eb_b[e]], [pvb])
                    MM(dn[:, q0:q1], ones_bf[:ksz, :], Eb[e][:ksz, :nq], first, last, [Eb_b[e]], [dnb])
                r = rot("rs", 2)
                RCP(rstd[r][:, :N], dn[:, :N], [dnb], [rstd_b[r]])
                TT(big[:, 4 + h, :N], pv[:, :N], rstd[r][:, :N], ALU.mult, [pvb, rstd_b[r]], [big_b[4 + h]])
            if not is_sample:
                if N < 512:
                    for h in range(NH):
                        CP(kst[l][:, h, 0:512 - N], kst[l][:, h, N:512], [kst_b[l]], [kst_b[l]], eng="pool")
                    for jb in range(4 - NB):
                        CP(vst[l][:, jb, :], vst[l][:, jb + NB, :], [vst_b[l]], [vst_b[l]], eng="pool")
                for h in range(NH):
                    CP(kst[l][:, h, 512 - N:512], kcur[:, h, :N], [kcur_b[h], kst_b[l]], [kst_b[l]], eng="pool")
                for jb in range(NB):
                    CP(vst[l][:, 4 - NB + jb, :], vcur[:, jb, :], [vcur_b, vst_b[l]], [vst_b[l]], eng="pool")

            for h in range(MH):
                ACT(Cbf[l][:, h, :], Cst[l][:, h, :], AF.Copy, [Cst_b[l][h]], [cbf_b[h]])
                ACT(nbf[l][:, h, :], nst[l][:, h, :], AF.Copy, [Cst_b[l][h]], [cbf_b[h]])
            hf = stage[:, 1024:1024 + MH * T].rearrange("p (h t) -> p h t", h=MH)
            for r_ in range(nblk):
                c0, c1 = r_ * bs, (r_ + 1) * bs
                nump, numb = psB()
                denp, denb = psB()
                for h in range(MH):
                    ix = rot("t", NR)
                    tb = tmp_b[ix]
                    M0 = m0s[:, h:h + 1] if r_ == 0 else g1[:, h, c0 - 1:c0]
                    M0b = m0s_b[h] if r_ == 0 else g1_b[h]
                    M1 = g1[:, h, c1 - 1:c1]
                    sb_ = Cst_b[l][h]
                    pt, ptb = psA()
                    TR(pt[:bs, 0:128], g3[:, h, c0:c1], ident[:], [g3_b[h]], [ptb])
                    CP(acol[ix][:bs, 0:1], pt[:bs, 0:1], [ptb], [tb[0]])
                    TT(Wt[ix][:bs, :bs], negmask[:bs, :bs], g1[:bs, h, c0:c1], ALU.subtract, [g1_b[h]], [tb[1]])
                    ACT(Wt[ix][:bs, :bs], Wt[ix][:bs, :bs], AF.Exp, [tb[1], tb[0]], [tb[1]], bias=acol[ix][:bs, 0:1])
                    pss, pssb = psA()
                    MM(pss[:bs, :bs], mkT[:, h, c0:c1], mqT[:, h, c0:c1], True, True, [mkT_b[h], mqT_b[h]], [pssb])
                    STT(PT[ix][:bs, :bs], pss[:bs, :bs], SCALE, Wt[ix][:bs, :bs], ALU.mult, ALU.mult,
                        [pssb, tb[1]], [tb[2]])
                    ACT(eint[ix][:, :bs], g1[:, h, c0:c1], AF.Exp, [g1_b[h], M0b], [tb[3]], bias=M0, scale=-1.0)
                    TT(qtil[ix][:, :bs], mqT[:, h, c0:c1], eint[ix][:, :bs], ALU.mult, [mqT_b[h], tb[3]], [tb[4]])
                    no = nump[:, h * 128:h * 128 + bs]
                    do = denp[:, h * 128:h * 128 + bs]
                    MM(no, Cbf[l][:, h, :], qtil[ix][:, :bs], True, False, [cbf_b[h], tb[4]], [numb])
                    MM(do, nbf[l][:, h, :], qtil[ix][:, :bs], True, False, [cbf_b[h], tb[4]], [denb])
                    MM(no, mvtok[:bs, r_, h * 128:(h + 1) * 128], PT[ix][:bs, :bs], False, True, [mvtok_b, tb[2]],
                       [numb])
                    MM(do, ones_bf[:bs, :], PT[ix][:bs, :bs], False, True, [tb[2]], [denb])
                    ACT(wcol[ix][:bs, 0:1], M1[:bs], AF.Exp, [g1_b[h], tb[0]], [tb[5]], bias=acol[ix][:bs, 0:1],
                        scale=-1.0)
                    ACT(wcol[ix][:, 1:2], M1, AF.Exp, [g1_b[h], M0b], [tb[6]], bias=M0, scale=-1.0)
                    TS_(kw[ix][:bs, :], mktok[:bs, r_, h * 128:(h + 1) * 128], wcol[ix][:bs, 0:1], SCALE,
                        ALU.mult, ALU.mult, [mktok_b, tb[5]], [tb[7]])
                    pu, pub = psA()
                    MM(pu[:, 0:128], kw[ix][:bs, :], mvtok[:bs, r_, h * 128:(h + 1) * 128], True, True,
                       [tb[7], mvtok_b], [pub])
                    pn, pnb = psA()
                    MM(pn[:, 0:128], kw[ix][:bs, :], ones_bf[:bs, :], True, True, [tb[7]], [pnb])
                    STT(Cst[l][:, h, :], Cst[l][:, h, :], wcol[ix][:, 1:2], pu[:, 0:128], ALU.mult, ALU.add,
                        [pub, tb[6], sb_], [sb_])
                    STT(nst[l][:, h, :], nst[l][:, h, :], wcol[ix][:, 1:2], pn[:, 0:128], ALU.mult, ALU.add,
                        [pnb, tb[6], sb_], [sb_])
                    ACT(Cbf[l][:, h, :], Cst[l][:, h, :], AF.Copy, [sb_], [cbf_b[h]])
                    ACT(nbf[l][:, h, :], nst[l][:, h, :], AF.Copy, [sb_], [cbf_b[h]])
                di = rot("dd", 1)
                nv = nump[:, :].rearrange("p (h n) -> p h n", h=MH)[:, :, :bs]
                dv = denp[:, :].rearrange("p (h n) -> p h n", h=MH)[:, :, :bs]
                ACT(ddt[di][:, :, :bs], dv, AF.Abs, [denb], [ddt_b[di]])
                TT(ddt[di][:, :, :bs], ddt[di][:, :, :bs], g2[:, :, c0:c1], ALU.max, [ddt_b[di]] + g2_b, [ddt_b[di]])
                RCP(ddt[di][:, :, :bs], ddt[di][:, :, :bs], [ddt_b[di]], [ddt_b[di]])
                TT(hf[:, :, c0:c1], nv, ddt[di][:, :, :bs], ALU.mult, [numb, ddt_b[di]], [stB])
            for h in range(MH):
                i = rot("sq", 2)
                ACT(sqt[i][:, :N], hf[:, h, :N], AF.Square, [stB], [sqt_b[i]])
                ps2, pb2 = psA()
                MM(ps2[:, :N], ones_bf[:], sqt[i][:, :N], True, True, [sqt_b[i]], [pb2])
                r = rot("rs", 2)
                ACT(rstd[r][:, :N], ps2[:, :N], AF.Sqrt, [pb2], [rstd_b[r]], bias=epsc[:, 0:1], scale=1.0 / HD)
                RCP(rstd[r][:, :N], rstd[r][:, :N], [rstd_b[r]], [rstd_b[r]])
                iu = rot("u", 2)
                STT(utmp[iu][:, :N], hf[:, h, :N], pr(l, 46 + h), rstd[r][:, :N], ALU.mult, ALU.mult,
                    [stB, rstd_b[r]], [utmp_b[iu]])
                TT(big[:, 12 + h, :N], utmp[iu][:, :N], sigo[:, h, :N], ALU.mult, [utmp_b[iu], sigo_b[h]],
                   [big_b[12 + h]])

            rb_ = lambda kc: big[:, kc, :N]
            rbb = lambda kc: [big_b[kc]]
            for u2 in range(8):
                def ev(j, ps, pb, u2=u2):
                    c = 2 * u2 + j
                    TT(xT[:, c, :N], xT[:, c, :N], ps[:, :N], ALU.add, [xT_b[c], pb], [xT_b[c]])
                fm_unit(l, nxt(), rb_, rbb, N, ev)
            rmsnorm_hT(l, 16, N)
            for q in range(4):
                for u2 in range(8):
                    def ev(j, ps, pb, u2=u2):
                        fc = 2 * u2 + j
                        i = rot("u", 2)
                        ACT(utmp[i][:, :N], ps[:, :N], AF.Relu, [pb], [utmp_b[i]])
                        TT(big[:, fc, :N], utmp[i][:, :N], utmp[i][:, :N], ALU.mult, [utmp_b[i]], [big_b[fc]], eng="pool")
                    fm_unit(l, nxt(), rh, rhb, N, ev)
                for u2 in range(8):
                    def ev(j, ps, pb, u2=u2):
                        c = 2 * u2 + j
                        TT(xT[:, c, :N], xT[:, c, :N], ps[:, :N], ALU.add, [xT_b[c], pb], [xT_b[c]])
                    fm_unit(l, nxt(), rb_, rbb, N, ev)
            assert ui[0] == UNITS_PER_LAYER

        def state_out(l, oc, on, om, ocv):
            for h in range(MH):
                DMA("pool", oc[l, h], Cst[l][:, h, :], [Cst_b[l][h]], [], mdma())
                DMA("pool", on[l, h].rearrange("(p o) -> p o", o=1), nst[l][:, h, 0:1], [Cst_b[l][h]], [], mdma(),
                    nonc=True)
            TT(mout[:, 0:MH], carry[l][:, MH:2 * MH], carry[l][:, 0:MH], ALU.subtract, carry_b[l], [mout_b])
            DMA("pool", om[l:l + 1, :], mout[0:1, 0:MH], [mout_b], [], mdma())
            for c in range(4):
                DMA("pool", ocv[l][:, c * 128:(c + 1) * 128].rearrange("j p -> p j"), cst[l][:, c, :], [cst_b[l][c]], [],
                    mdma(), nonc=True)

        if do_sample:
            DMA("pool", stage[0:TS, :], xs, [stA, stB], [stA, stB], mdma())
            for g in range(4):
                ps, pb = psA()
                for j in range(4):
                    c = 4 * g + j
                    TR(ps[:, j * TS:(j + 1) * TS], stage[0:TS, c * 128:(c + 1) * 128], ident[0:TS, 0:TS],
                       [stA, stB], [pb])
                CP(xT[:, 4 * g:4 * g + 4, 0:TS], ps[:, 0:4 * TS].rearrange("p (j t) -> p j t", j=4), [pb],
                   xT_b[4 * g:4 * g + 4])
            for l in range(NL):
                DMA("pool", Cst[l][:], smc[l].rearrange("h p v -> p h v"), Cst_b[l], Cst_b[l], mdma())
                DMA("pool", ncol[:], smn[l].rearrange("h p -> p h"), [ncol_b], [ncol_b], mdma(), nonc=True)
                for h in range(MH):
                    TS_(nst[l][:, h, :], ones_f[:, 0:128], ncol[:, h:h + 1], None, ALU.mult, ALU.bypass,
                        [ncol_b, Cst_b[l][h]], [Cst_b[l][h]])
                MSET(carry[l][:, 0:MH], 0.0, carry_b[l])
                DMA("pool", carry[l][:, MH:2 * MH], smm[l].partition_broadcast(128), carry_b[l], carry_b[l],
                    mdma(), nonc=True)
                for c in range(4):
                    DMA("pool", cst[l][:, c, :], sconv[l][:, c * 128:(c + 1) * 128].rearrange("j p -> p j"),
                        [cst_b[l][c]], [cst_b[l][c]], mdma(), nonc=True)
                DMA("pool", vst[l][:], cv[l].rearrange("(b p) f -> p b f", p=128), [vst_b[l]], [vst_b[l]], mdma())
                bigf = big[:, :, :].rearrange("p c t -> p (c t)")
                for blk in range(4):
                    DMA("pool", bigf[:, blk * 1024:(blk + 1) * 1024],
                        ck[l, blk * 128:(blk + 1) * 128, :], big_b, big_b, mdma())
                for h in range(NH):
                    ps, pb = psA()
                    psv = ps[:, :].bitcast(BF16)
                    for blk in range(4):
                        TR(psv[:, blk * 128:(blk + 1) * 128], bigf[:, blk * 1024 + h * 128:blk * 1024 + (h + 1) * 128],
                           identb[:], big_b, [pb])
                    CP(kst[l][:, h, :], psv[:, 0:512], [pb], [kst_b[l]])
                marks['sload%d' % l] = len(P.ops)
                layer(l, TS, 0, True, True, 0, o_sk, o_sv)
                marks['slayer%d' % l] = len(P.ops)
                state_out(l, o_sc, o_sn, o_sm, o_sconv)
                for h in range(MH):
                    MSET(Cst[l][:, h, :], 0.0, [Cst_b[l][h]])
                    MSET(nst[l][:, h, :], 0.0, [Cst_b[l][h]])
                MSET(carry[l][:], 0.0, carry_b[l])
                MSET(cst[l][:], 0.0, cst_b[l])
            for g in range(4):
                ps, pb = psA()
                for j in range(4):
                    c = 4 * g + j
                    TR(ps[0:TS, j * 128:(j + 1) * 128], xT[:, c, 0:TS], ident[:], [xT_b[c]], [pb])
                o_ = stage[0:TS, g * 512:(g + 1) * 512]
                i_ = ps[0:TS, 0:512]
                P.op("dve", lambda e, o_=o_, i_=i_: e.tensor_copy(o_, i_), [pb],
                     [stA, stB] if g == 0 else (), wa=() if g == 0 else [stA, stB])
            DMA("pool", ys, stage[0:TS, :], [stA, stB], [], mdma())
        else:
            for l in range(NL):
                for h in range(MH):
                    MSET(Cst[l][:, h, :], 0.0, [Cst_b[l][h]])
                    MSET(nst[l][:, h, :], 0.0, [Cst_b[l][h]])
                MSET(carry[l][:], 0.0, carry_b[l])
                MSET(cst[l][:], 0.0, cst_b[l])

        marks['sample'] = len(P.ops)
        for tt in range(NT):
            for blk in range(NB):
                DMA("pool", stage[:, :], xp[tt * T + blk * 128:tt * T + (blk + 1) * 128, :], [stA, stB], [stA, stB],
                    mdma())
                for g in range(4):
                    ps, pb = psA()
                    for j in range(4):
                        c = 4 * g + j
                        TR(ps[:, j * 128:(j + 1) * 128], stage[:, c * 128:(c + 1) * 128], ident[:], [stA, stB], [pb])
                    CP(xT[:, 4 * g:4 * g + 4, blk * 128:(blk + 1) * 128],
                       ps[:, 0:512].rearrange("p (j t) -> p j t", j=4), [pb], xT_b[4 * g:4 * g + 4])
            row0 = tt * T - (NTOK - KEEP)
            for l in range(NL):
                layer(l, T, tt, False, row0 >= 0, max(row0, 0), o_pk, o_pv)
            for blk in range(NB):
                for g in range(4):
                    ps, pb = psA()
                    for j in range(4):
                        c = 4 * g + j
                        TR(ps[:, j * 128:(j + 1) * 128], xT[:, c, blk * 128:(blk + 1) * 128], ident[:], [xT_b[c]],
                           [pb])
                    o_ = stage[:, g * 512:(g + 1) * 512]
                    i_ = ps[:, 0:512]
                    P.op("dve", lambda e, o_=o_, i_=i_: e.tensor_copy(o_, i_), [pb],
                         [stA, stB] if g == 0 else (), wa=() if g == 0 else [stA, stB])
                DMA("pool", yp[tt * T + blk * 128:tt * T + (blk + 1) * 128, :], stage[:, :], [stA, stB], [], mdma())
        for l in range(NL):
            state_out(l, o_pc, o_pn, o_pm, o_pconv)
        fin = Buf("fin")
        for d in list(P.dma_last):
            pass
        import os
        cut = os.environ.get("KPREFIX")
        if cut:
            ncut = marks[cut] if cut in marks else int(cut)
            del P.ops[ncut:]
        lastd = {}
        for i_, o_ in enumerate(P.ops):
            if o_[3] is not None:
                lastd[o_[3]] = i_
        P.ops.append(["pool", lambda e: e.memset(mout[:, 0:1], 0.0), sorted(lastd.values()), None, None, False, 0])
        print("nops", len(P.ops), {k: v for k, v in marks.items()}, flush=True)
        P.emit(nc, stack)
    return nc


def host_params(norm_mix_g, conv_w, q_norm_g, k_norm_g, rel_bias, b_igate, b_fgate, mlstm_norm_g, norm_mlp_g, TS=32):
    NL = norm_mix_g.shape[0]
    par = np.zeros((NL, 128, NPAR), np.float32)
    par[:, :, 0:16] = norm_mix_g.reshape(NL, 16, 128).transpose(0, 2, 1)
    par[:, :, 16:32] = norm_mlp_g.reshape(NL, 16, 128).transpose(0, 2, 1)
    cw = conv_w.reshape(NL, 3, 4, 128)
    par[:, :, 32:44] = cw.transpose(0, 3, 2, 1).reshape(NL, 128, 12)
    par[:, :, 44] = q_norm_g
    par[:, :, 45] = k_norm_g
    par[:, :, 46:50] = mlstm_norm_g.reshape(NL, 4, 128).transpose(0, 2, 1)
    par[:, :, 50:54] = b_igate[:, None, :]
    par[:, :, 54:58] = b_fgate[:, None, :]
    par[:, :, 58:66] = rel_bias[:, None, :, 256]
    k = np.arange(128)[:, None]
    c = np.arange(256)[None, :]
    idx = np.clip(c - k, -128, 128) + 128
    nbp = rel_bias[:, :, idx].transpose(0, 2, 1, 3).reshape(NL, 128, NH * 256)
    c2 = np.arange(TS)[None, :]
    idx3 = np.clip(c2 + 128 - k, -128, 128) + 128
    nbs3 = rel_bias[:, :, idx3].transpose(0, 2, 1, 3).reshape(NL, 128, NH * TS)
    k4 = np.arange(TS)[:, None]
    idx4 = np.clip(c2 - k4, -128, 128) + 128
    nbs4 = rel_bias[:, :, idx4].transpose(0, 2, 1, 3).reshape(NL, TS, NH * TS)
    return (np.ascontiguousarray(par), np.ascontiguousarray(nbp, np.float32), np.ascontiguousarray(nbs3, np.float32),
            np.ascontiguousarray(nbs4, np.float32))


def const_inputs():
    ident = np.eye(128, dtype=np.float32)
    s = np.arange(128)[:, None]
    t = np.arange(128)[None, :]
    negmask = np.where(s <= t, 0.0, -1e30).astype(np.float32)
    return ident, negmask


_T = 256


def kernel(x_prompt, x_sample, cache_att_k, cache_att_v, state_conv, state_mlstm_c, state_mlstm_n, state_mlstm_m,
           norm_mix_g, w_in, conv_w, q_norm_g, k_norm_g, rel_bias, b_igate, b_fgate, mlstm_norm_g, w_out,
           norm_mlp_g, w_up, w_down):
    f = lambda a: np.ascontiguousarray(np.asarray(a), dtype=np.float32)
    x_prompt, x_sample = f(x_prompt), f(x_sample)
    NLAY = w_in.shape[0]
    B, L, _ = x_prompt.shape
    SB, TS, _ = x_sample.shape
    n = 8
    NT = L // _T
    nc = build(NLAY, NT, _T, TS)
    par, nbp, nbs3, nbs4 = host_params(f(norm_mix_g), f(conv_w), f(q_norm_g), f(k_norm_g), f(rel_bias), f(b_igate),
                                       f(b_fgate), f(mlstm_norm_g), f(norm_mlp_g), TS)
    ident, negmask = const_inputs()
    ck, cv = f(cache_att_k), f(cache_att_v)
    sc, smc, smn, smm = f(state_conv), f(state_mlstm_c), f(state_mlstm_n), f(state_mlstm_m)
    w_in, w_out, w_up, w_down = f(w_in), f(w_out), f(w_up), f(w_down)
    in_maps = []
    for c in range(n):
        s = (c // 2) % B
        sbi = c % SB
        in_maps.append({
            "xp": x_prompt[s], "xs": x_sample[sbi],
            "ck": np.ascontiguousarray(ck[:, sbi].reshape(NLAY, 512, 1024)),
            "cv": np.ascontiguousarray(cv[:, sbi].reshape(NLAY, 512, 1024)),
            "sconv": np.ascontiguousarray(sc[:, sbi]), "smc": np.ascontiguousarray(smc[:, sbi]),
            "smn": np.ascontiguousarray(smn[:, sbi]), "smm": np.ascontiguousarray(smm[:, sbi]),
            "win": w_in, "wout": w_out, "wup": w_up, "wdn": w_down,
            "par": par, "nbp": nbp, "nbs3": nbs3, "nbs4": nbs4, "ident": ident, "negmask": negmask,
        })
    res = run_bass_kernel_spmd(nc, in_maps, core_ids=list(range(n)))
    R = res.results
    pc = [2 * s for s in range(B)]
    yp = np.stack([R[c]["yp"] for c in pc])
    ys = np.stack([R[c]["ys"] for c in range(SB)])
    KEEP = min(512, L)

    def pst(name, shp):
        return np.stack([R[c][name] for c in pc], axis=1).reshape(shp)

    def sst(name, shp):
        return np.stack([R[c][name] for c in range(SB)], axis=1).reshape(shp)

    p_conv = pst("o_pconv", (NLAY, B, 2, 512))
    p_k = pst("o_pk", (NLAY, B, KEEP, NH, HD))
    p_v = pst("o_pv", (NLAY, B, KEEP, NH, HD))
    p_c = pst("o_pc", (NLAY, B, MH, 128, 128))
    p_n = pst("o_pn", (NLAY, B, MH, 128))
    p_m = pst("o_pm", (NLAY, B, MH))
    s_conv = sst("o_sconv", (NLAY, SB, 2, 512))
    s_k = sst("o_sk", (NLAY, SB, TS, NH, HD))
    s_v = sst("o_sv", (NLAY, SB, TS, NH, HD))
    s_c = sst("o_sc", (NLAY, SB, MH, 128, 128))
    s_n = sst("o_sn", (NLAY, SB, MH, 128))
    s_m = sst("o_sm", (NLAY, SB, MH))
    outs = (yp, ys, p_conv, p_k, p_v, p_c, p_n, p_m, s_conv, s_k, s_v, s_c, s_n, s_m)
    return tuple(np.ascontiguousarray(o, dtype=np.float32) for o in outs)
```

```python
import contextlib
import numpy as np
import concourse.bass as bass
import concourse.mybir as mybir
from concourse.bass_utils import run_bass_kernel_spmd

F32 = mybir.dt.float32
BF16 = mybir.dt.bfloat16
ALU = mybir.AluOpType
AF = mybir.ActivationFunctionType

D = 2048
NCH = 16
HD = 128
NH = 8
MH = 4
DFF = 8192
IN_DIM = 6664
C_XA, C_GB, C_GC, C_Q, C_K, C_V, C_MQ, C_MK, C_MV, C_MO, C_IG, C_FG = (
    0, 512, 1024, 1536, 2560, 3584, 4608, 5120, 5632, 6144, 6656, 6660)
EPS = 1e-6
SCALE = 128 ** -0.5
NPAR = 66
UNITS_PER_LAYER = 104


class Buf:
    __slots__ = ("w", "r", "name")

    def __init__(self, name=""):
        self.w = {}
        self.r = {}
        self.name = name


class Prog:
    ENGS = ("pe", "act", "dve", "pool", "sp")

    def __init__(self):
        self.ops = []
        self.dma_count = {}
        self.dma_last = {}
        self.dma_inc = {}

    def key(self, i):
        o = self.ops[i]
        return ("d", o[3]) if o[3] is not None else o[0]

    def op(self, eng, fn, rd=(), wr=(), dma=None, wa=(), dma_inc=16):
        i = len(self.ops)
        raw = set()
        deps = {}

        def need(j):
            k = self.key(j)
            if deps.get(k, -1) < j:
                deps[k] = j

        for b in rd:
            for j in b.w.values():
                raw.add(j)
                need(j)
        for b in wr:
            for j in b.w.values():
                need(j)
            for j in b.r.values():
                need(j)
        for b in wa:
            for j in b.r.values():
                need(j)
        if dma is None:
            if eng in deps and deps[eng] not in raw:
                cand = [j for j in raw if self.key(j) == eng]
                if cand:
                    deps[eng] = max(cand)
                else:
                    del deps[eng]
            mykey = eng
            ordn = None
        else:
            mykey = ("d", dma)
            if dma in self.dma_last:
                need(self.dma_last[dma])
            self.dma_last[dma] = i
            self.dma_inc[dma] = dma_inc
            self.dma_count[dma] = self.dma_count.get(dma, 0) + 1
            ordn = self.dma_count[dma]
        self.ops.append([eng, fn, sorted(deps.values()), dma, ordn, False, 0])
        for b in rd:
            b.r[mykey] = i
        for b in wr:
            b.w = {mykey: i}
            b.r = {}
        for b in wa:
            b.w[mykey] = i
        return i

    def emit(self, nc, stack):
        ops = self.ops
        sems = {e: stack.enter_context(nc.semaphore("s_" + e)) for e in self.ENGS}
        dsems = {d: stack.enter_context(nc.semaphore("d_%d" % d)) for d in sorted(self.dma_count)}
        waited = {e: {} for e in self.ENGS}
        waits = []
        for i, o in enumerate(ops):
            e = o[0]
            wl = []
            for j in o[2]:
                k = self.key(j)
                if waited[e].get(k, -1) >= j:
                    continue
                waited[e][k] = j
                wl.append(j)
                if ops[j][3] is None:
                    ops[j][5] = True
            waits.append(wl)
        cnt = {e: 0 for e in self.ENGS}
        for o in ops:
            if o[3] is None and o[5]:
                cnt[o[0]] += 1
                o[6] = cnt[o[0]]
        per = {e: [] for e in self.ENGS}
        for i, o in enumerate(ops):
            per[o[0]].append(i)

        def run(engname, eng):
            for i in per[engname]:
                o = ops[i]
                for j in waits[i]:
                    oj = ops[j]
                    if oj[3] is not None:
                        eng.wait_ge(dsems[oj[3]], self.dma_inc[oj[3]] * oj[4])
                    else:
                        eng.wait_ge(sems[oj[0]], oj[6])
                ins = o[1](eng)
                if o[3] is not None:
                    if self.dma_inc[o[3]] == 1:
                        ins.then_inc(dsems[o[3]])
                    else:
                        ins.then_inc(dsems[o[3]], 16)
                elif o[5]:
                    ins.then_inc(sems[engname], 1)

        with nc.Block() as block:
            @block.tensor
            def _(e):
                run("pe", e)

            @block.scalar
            def _(e):
                run("act", e)

            @block.vector
            def _(e):
                run("dve", e)

            @block.gpsimd
            def _(e):
                run("pool", e)

            @block.sync
            def _(e):
                run("sp", e)


def unit_table():
    u = []
    for h in range(MH):
        u.append(("gate", h))
    for base in (C_MQ, C_MK, C_MO):
        for j in range(2):
            u.append(("fm", "win", 0, base + j * 256, base + j * 256 + 128))
    for base in (C_MK, C_MV):
        for j in range(2):
            u.append(("tm", base + j * 256))
    for c in range(4):
        u.append(("fm", "win", 0, C_GC + c * 128, C_XA + c * 128))
    for j in range(2):
        u.append(("fm", "win", 0, C_GB + j * 256, C_GB + j * 256 + 128))
    for base in (C_Q, C_K):
        for j in range(4):
            u.append(("fm", "win", 0, base + j * 256, base + j * 256 + 128))
    for j in range(4):
        u.append(("tm", C_V + j * 256))
    for j in range(8):
        u.append(("fm", "wout", 0, j * 256, j * 256 + 128))
    for q in range(4):
        for j in range(8):
            u.append(("fm", "wup", 0, q * 2048 + j * 256, q * 2048 + j * 256 + 128))
        for j in range(8):
            u.append(("fm", "wdn", q * 2048, j * 256, j * 256 + 128))
    assert len(u) == UNITS_PER_LAYER
    return u


UNITS = unit_table()


def build(NL, NT, T, TS=32, NSS=3):
    NTOK = NT * T
    NB = T // 128
    NQ = T // 64
    nc = bass.Bass("TRN2", target_bir_lowering=False)
    P = Prog()

    def din(name, shape, dt=F32):
        return nc.dram_tensor(name, list(shape), dt, kind="ExternalInput").ap()

    def dout(name, shape, dt=F32):
        return nc.dram_tensor(name, list(shape), dt, kind="ExternalOutput").ap()

    xp = din("xp", [NTOK, D])
    xs = din("xs", [NSS, TS, D])
    ck = din("ck", [NSS, NL, 512, 1024])
    cv = din("cv", [NSS, NL, 512, 1024])
    sconv = din("sconv", [NSS, NL, 2, 512])
    smc = din("smc", [NSS, NL, MH, 128, 128])
    smn = din("smn", [NSS, NL, MH, 128])
    smm = din("smm", [NSS, NL, MH])
    role_d = din("role", [128, 2])
    win = din("win", [NL, D, IN_DIM])
    wout = din("wout", [NL, D, D])
    wup = din("wup", [NL, D, DFF])
    wdn = din("wdn", [NL, DFF, D])
    par_d = din("par", [NL, 128, NPAR])
    nbp_d = din("nbp", [NL, 128, NH * 256])
    nbs3_d = din("nbs3", [NL, 128, NH * TS])
    nbs4_d = din("nbs4", [NL, TS, NH * TS])
    ident_d = din("ident", [128, 128])
    negmask_d = din("negmask", [128, 128])
    WM = {"win": win, "wout": wout, "wup": wup, "wdn": wdn}

    yp = dout("yp", [(NT + 1) * T, D])
    ys = dout("ys", [NSS * TS, D])
    o_pconv = dout("o_pconv", [2, NL, 2, 512])
    o_pk = dout("o_pk", [3, NL, T, 1024])
    o_pv = dout("o_pv", [3, NL, T, 1024])
    o_pc = dout("o_pc", [2, NL, MH, 128, 128])
    o_pn = dout("o_pn", [2, NL, MH, 128])
    o_pm = dout("o_pm", [2, NL, MH])
    o_sconv = dout("o_sconv", [NSS, NL, 2, 512])
    o_sk = dout("o_sk", [NSS, NL, TS, 1024])
    o_sv = dout("o_sv", [NSS, NL, TS, 1024])
    o_sc = dout("o_sc", [NSS, NL, MH, 128, 128])
    o_sn = dout("o_sn", [NSS, NL, MH, 128])
    o_sm = dout("o_sm", [NSS, NL, MH])

    bounce_p = nc.dram_tensor("bounce_p", [T, D], F32).ap()
    gath_p = nc.dram_tensor("gath_p", [2 * T, D], F32).ap()
    bounce_s = nc.dram_tensor("bounce_s", [TS, D], F32).ap()
    gath_s = nc.dram_tensor("gath_s", [2 * TS, D], F32).ap()
    GROUPS = [[0, 1], [2, 3], [4, 5], [6, 7]]
    wscr_l = [nc.dram_tensor("wscr%d" % l, [UNITS_PER_LAYER, 128, 4096], BF16).ap() for l in range(NL)]
    nbscr = nc.dram_tensor("nbscr", [NL, 128, NH * 256], BF16).ap()
    nbs3scr = nc.dram_tensor("nbs3scr", [NL, 128, NH * TS], BF16).ap()
    nbs4scr = nc.dram_tensor("nbs4scr", [NL, TS, NH * TS], BF16).ap()

    stack = contextlib.ExitStack()
    with stack:
        def sb(name, shape, dt=F32):
            return stack.enter_context(nc.sbuf_tensor("sb_" + name, list(shape), dt))

        banks = [stack.enter_context(nc.psum_tensor("bank%d" % i, [128, 512], F32)) for i in range(8)]
        bank_bufs = [Buf("bank%d" % i) for i in range(8)]
        pool_ctr = {"A": 0, "B": 0}

        def psA():
            i = pool_ctr["A"] % 4
            pool_ctr["A"] += 1
            return banks[i], bank_bufs[i]

        def psB():
            i = 4 + pool_ctr["B"] % 4
            pool_ctr["B"] += 1
            return banks[i], bank_bufs[i]

        xT = sb("xT", [128, NCH, T])
        hT = sb("hT", [128, NCH, T], BF16)
        big = sb("big", [128, NCH, T], BF16)
        wb = [sb("wb%d" % i, [128, 4096], BF16) for i in range(3)]
        wb_buf = [Buf("wb%d" % i) for i in range(3)]
        stage = sb("stage", [128, 2048])
        stA, stB = Buf("stA"), Buf("stB")
        stage2 = sb("stage2", [128, 2048])
        st2 = Buf("st2")
        role = sb("role", [128, 2])
        kst = [sb("kst%d" % l, [128, NH, 512], BF16) for l in range(NL)]
        vst = [sb("vst%d" % l, [128, 4, 1024], BF16) for l in range(NL)]
        kst_b = [Buf() for _ in range(NL)]
        vst_b = [Buf() for _ in range(NL)]
        kcur = sb("kcur", [128, NH, T], BF16)
        vcur = sb("vcur", [128, NB, 1024], BF16)
        qT = sb("qT", [128, NH, T], BF16)
        kcur_b = [Buf() for _ in range(NH)]
        qT_b = [Buf() for _ in range(NH)]
        vcur_b = Buf()
        nbt = sb("nbt", [128, NH, 256], BF16)
        nbt3 = nbt[:, :, 0:TS]
        nbt4 = nbt[:, :, TS:2 * TS]
        nbt_b = Buf()
        g1 = sb("g1", [128, MH, T])
        g2 = sb("g2", [128, MH, T])
        g3 = sb("g3", [128, MH, T])
        g1_b = [Buf() for _ in range(MH)]
        g2_b = [Buf() for _ in range(MH)]
        g3_b = [Buf() for _ in range(MH)]
        mqT = sb("mqT", [128, MH, T], BF16)
        mkT = sb("mkT", [128, MH, T], BF16)
        sigo = sb("sigo", [128, MH, T], BF16)
        mqT_b = [Buf() for _ in range(MH)]
        mkT_b = [Buf() for _ in range(MH)]
        sigo_b = [Buf() for _ in range(MH)]
        mktok = sb("mktok", [128, NB, 512], BF16)
        mvtok = sb("mvtok", [128, NB, 512], BF16)
        mktok_b, mvtok_b = Buf(), Buf()
        uext = [sb("uext%d" % i, [128, T + 2]) for i in range(2)]
        uext_b = [Buf() for _ in range(2)]
        utmp = [sb("utmp%d" % i, [128, T]) for i in range(2)]
        utmp_b = [Buf() for _ in range(2)]
        rstd = [sb("rstd%d" % i, [128, T]) for i in range(2)]
        rstd_b = [Buf() for _ in range(2)]
        sqt = [sb("sqt%d" % i, [128, T], BF16) for i in range(2)]
        sqt_b = [Buf() for _ in range(2)]
        Eb = [sb("E%d" % i, [128, T], BF16) for i in range(2)]
        Eb_b = [Buf() for _ in range(2)]
        par = sb("par", [128, NL, NPAR])
        negbf = sb("negbf", [128, NL, MH])
        negch = sb("negch", [128, NL, NH])
        ident = sb("ident", [128, 128])
        identb = sb("identb", [128, 128], BF16)
        negmask = sb("negmask", [128, 128])
        ones_bf = sb("ones_bf", [128, 128], BF16)
        ones_f = sb("ones_f", [128, T])
        Cst = [sb("Cst%d" % l, [128, MH, 128]) for l in range(NL)]
        Cbf1 = sb("Cbf1", [128, MH, 128], BF16)
        nbf1 = sb("nbf1", [128, MH, 128], BF16)
        Cbf = [Cbf1 for l in range(NL)]
        cbf_b = [Buf() for _ in range(MH)]
        nst = [sb("nst%d" % l, [128, MH, 128]) for l in range(NL)]
        nbf = [nbf1 for l in range(NL)]
        carry = [sb("carry%d" % l, [128, 2 * MH]) for l in range(NL)]
        Cst_b = [[Buf() for _ in range(MH)] for _ in range(NL)]
        carry_b = [[Buf() for _ in range(MH)] for _ in range(NL)]
        cst = [sb("cst%d" % l, [128, 4, 2]) for l in range(NL)]
        cst_b = [[Buf() for _ in range(4)] for _ in range(NL)]
        xT_b = [Buf("xT%d" % c) for c in range(NCH)]
        hT_b = Buf("hT")
        big_b = [Buf("big%d" % c) for c in range(NCH)]
        NR = 2
        acol = [sb("acol%d" % i, [128, 2]) for i in range(NR)]
        Wt = [sb("Wt%d" % i, [128, 128]) for i in range(NR)]
        PT = [sb("PT%d" % i, [128, 128], BF16) for i in range(NR)]
        eint = [sb("eint%d" % i, [128, 128]) for i in range(NR)]
        qtil = [sb("qtil%d" % i, [128, 128], BF16) for i in range(NR)]
        wcol = [sb("wcol%d" % i, [128, 2]) for i in range(NR)]
        kw = [sb("kw%d" % i, [128, 128], BF16) for i in range(NR)]
        tmp_b = [[Buf() for _ in range(8)] for _ in range(NR)]
        ddt = [sb("ddt%d" % i, [128, MH, 128]) for i in range(1)]
        ddt_b = [Buf() for _ in range(1)]
        wgf = sb("wgf", [128, NCH, 8])
        wgf_b = Buf()
        mout = sb("mout", [128, 2 * MH])
        mout_b = Buf()

        dma_ids = {"n": 0}

        def new_dsem():
            dma_ids["n"] += 1
            return dma_ids["n"] - 1

        wsem = [new_dsem() for _ in range(3)]
        psem = [new_dsem() for _ in range(8)]
        msem = [new_dsem() for _ in range(6)]
        ctr = {"p": 0, "m": 0, "w": 0, "rot": 0, "sq": 0, "rs": 0, "E": 0, "u": 0, "t": 0, "dd": 0}

        def pdma():
            ctr["p"] += 1
            return psem[ctr["p"] % 8]

        def mdma():
            ctr["m"] += 1
            return msem[ctr["m"] % 6]

        def MM(out, lhsT, rhs, start, stop, rd, wr):
            P.op("pe", lambda e: e.matmul(out, lhsT, rhs, start=start, stop=stop), rd, wr)

        def TR(out, in_, idn, rd, wr):
            P.op("pe", lambda e: e.transpose(out, in_, idn), rd, wr)

        def ACT(out, in_, func, rd, wr, bias=None, scale=1.0):
            if bias is None:
                P.op("act", lambda e: e.activation(out, in_, func, scale=scale), rd, wr)
            else:
                P.op("act", lambda e: e.activation(out, in_, func, bias=bias, scale=scale), rd, wr)

        def TT(out, a, b, op, rd, wr, eng="dve"):
            P.op(eng, lambda e: e.tensor_tensor(out, a, b, op), rd, wr)

        def STT(out, in0, scalar, in1, op0, op1, rd, wr):
            P.op("dve", lambda e: e.scalar_tensor_tensor(out, in0, scalar, in1, op0, op1), rd, wr)

        def TS_(out, in0, s1, s2, op0, op1, rd, wr, eng="dve"):
            P.op(eng, lambda e: e.tensor_scalar(out, in0, s1, s2, op0, op1), rd, wr)

        def CP(out, in_, rd, wr, eng="dve"):
            P.op(eng, lambda e: e.tensor_copy(out, in_), rd, wr)

        def RCP(out, in_, rd, wr):
            P.op("dve", lambda e: e.reciprocal(out, in_), rd, wr)

        def SCAN(out, d0, d1, init, op0, op1, rd, wr):
            P.op("dve", lambda e: e.tensor_tensor_scan(out, d0, d1, init, op0, op1), rd, wr)

        def MSET(ap, val, wr, eng="pool"):
            P.op(eng, lambda e: e.memset(ap, val), (), wr)

        def DMA(eng, out, in_, rd, wr, sem, nonc=False):
            if nonc:
                def f(e):
                    with nc.allow_non_contiguous_dma(reason="small strided transfer"):
                        return e.dma_start(out=out, in_=in_)
            else:
                def f(e):
                    return e.dma_start(out=out, in_=in_)
            P.op(eng, f, rd, wr, dma=sem)

        marks = {}
        ALLB = Buf("init")
        DMA("pool", ident[:], ident_d, (), [ALLB], mdma())
        DMA("pool", negmask[:], negmask_d, (), [ALLB], mdma())
        DMA("pool", role[:], role_d, (), [ALLB], mdma())
        DMA("pool", par[:], par_d.rearrange("l p n -> p l n"), (), [ALLB], mdma(), nonc=True)
        MSET(ones_bf[:], 1.0, [ALLB])
        MSET(ones_f[:], 1.0, [ALLB])
        CP(identb[:], ident[:], [ALLB], [ALLB], eng="pool")
        for l in range(NL):
            TS_(negbf[:, l, :], par[:, l, 54:58], -1.0, None, ALU.mult, ALU.bypass, [ALLB], [ALLB], eng="pool")
            TS_(negch[:, l, :], par[:, l, 58:66], -1.0, None, ALU.mult, ALU.bypass, [ALLB], [ALLB], eng="pool")

        def pr(l, a, b=None):
            return par[:, l, a:(a + 1 if b is None else b)]

        nbscr_b = [Buf() for _ in range(NL)]
        for l in range(NL):
            DMA("pool", stage[:, 0:NH * 256], nbp_d[l], [ALLB], [stA, stB], mdma())
            for h in range(NH):
                ACT(nbt[:, h, :], stage[:, h * 256:(h + 1) * 256], AF.Exp, [stA, stB, ALLB], [nbt_b],
                    bias=negch[:, l, h:h + 1])
            for h in range(NH):
                MSET(nbt[64:128, h, 0:64], 0.0, [nbt_b])
            DMA("pool", nbscr[l].rearrange("p (h n) -> p h n", h=NH), nbt[:], [nbt_b], [nbscr_b[l]], mdma())
            DMA("pool", stage[:, 0:NH * TS], nbs3_d[l], [nbt_b], [stA, stB], mdma())
            for h in range(NH):
                ACT(nbt3[:, h, :], stage[:, h * TS:(h + 1) * TS], AF.Exp, [stA, stB], [nbt_b],
                    bias=negch[:, l, h:h + 1])
            DMA("pool", nbs3scr[l].rearrange("p (h n) -> p h n", h=NH), nbt3, [nbt_b], [nbscr_b[l]], mdma())
            DMA("pool", stage[0:TS, 0:NH * TS], nbs4_d[l], [nbt_b], [stA, stB], mdma())
            for h in range(NH):
                ACT(nbt4[0:TS, h, :], stage[0:TS, h * TS:(h + 1) * TS], AF.Exp, [stA, stB], [nbt_b],
                    bias=negch[0:TS, l, h:h + 1])
            DMA("pool", nbs4scr[l].rearrange("p (h n) -> p h n", h=NH), nbt4[0:TS], [nbt_b], [nbscr_b[l]], mdma())

        marks['init'] = len(P.ops)
        wscr_b = [[Buf() for _ in range(UNITS_PER_LAYER)] for _ in range(NL)]

        def DMAw(out, in_, b, sem):
            P.op("pool", lambda e: e.dma_start(out=out, in_=in_), (), (), dma=sem, wa=[b])

        for l in range(NL):
            DMA("pool", wgf[:], win[l][:, C_IG:C_IG + 8].rearrange("(kc p) g -> p kc g", p=128),
                [wgf_b], [wgf_b], mdma(), nonc=True)
            for ui, u in enumerate(UNITS):
                dst = wscr_l[l][ui]
                if u[0] == "gate":
                    h = u[1]
                    rb = ui % 2
                    wv = wb[rb][:, :].rearrange("p (j kc m) -> p j kc m", j=2, kc=NCH)
                    for j, col in enumerate((h, 4 + h)):
                        CP(wv[:, j, :, :], wgf[:, :, col:col + 1].to_broadcast([128, NCH, 128]),
                           [wgf_b], [wb_buf[rb]], eng="pool")
                    DMA("pool", dst, wb[rb][:, :], [wb_buf[rb]], [wscr_b[l][ui]], pdma())
                elif u[0] == "fm":
                    W = WM[u[1]][l]
                    r0 = u[2]
                    for j, col in enumerate(u[3:5]):
                        src = W[r0:r0 + 2048, col:col + 128].rearrange("(kc p) m -> p kc m", p=128)
                        DMAw(dst[:, j * 2048:(j + 1) * 2048].rearrange("p (kc m) -> p kc m", m=128), src,
                             wscr_b[l][ui], pdma())
                else:
                    col = u[1]
                    src = win[l][:, col:col + 256].rearrange("(kc p) n -> p kc n", p=128)
                    DMAw(dst.rearrange("p (kc n) -> p kc n", n=256), src, wscr_b[l][ui], pdma())

        marks['prepass'] = len(P.ops)
        def wload(l, ui):
            s = ctr["w"] % 3
            ctr["w"] += 1
            DMA("sp", wb[s][:], wscr_l[l][ui], [wscr_b[l][ui]], [wb_buf[s]], wsem[s])
            return s

        def fm_unit(l, ui, rhs, rbufs, N, evac):
            s = wload(l, ui)
            for j in range(2):
                ps, pb = psA()
                for kc in range(NCH):
                    MM(ps[:, :N], wb[s][:, (j * NCH + kc) * 128:(j * NCH + kc + 1) * 128], rhs(kc),
                       kc == 0, kc == NCH - 1, [wb_buf[s]] + rbufs(kc), [pb])
                evac(j, ps, pb)

        def tm_unit(l, ui, N, evac):
            s = wload(l, ui)
            bs = min(128, N)
            for blk in range(N // bs):
                ps, pb = psA()
                for kc in range(NCH):
                    MM(ps[:bs, :256], hT[:, kc, blk * bs:(blk + 1) * bs], wb[s][:, kc * 256:(kc + 1) * 256],
                       kc == 0, kc == NCH - 1, [wb_buf[s], hT_b], [pb])
                evac(blk, ps, pb)

        def rot(name, n):
            i = ctr[name] % n
            ctr[name] += 1
            return i

        def rmsnorm_hT(l, goff, N):
            ps, pb = psA()
            for c in range(NCH):
                i = rot("sq", 2)
                ACT(sqt[i][:, :N], xT[:, c, :N], AF.Square, [xT_b[c]], [sqt_b[i]])
                MM(ps[:, :N], ones_bf[:], sqt[i][:, :N], c == 0, c == NCH - 1, [sqt_b[i]], [pb])
            r = rot("rs", 2)
            ACT(rstd[r][:, :N], ps[:, :N], AF.Sqrt, [pb], [rstd_b[r]], bias=epsc[:, 0:1], scale=1.0 / D)
            RCP(rstd[r][:, :N], rstd[r][:, :N], [rstd_b[r]], [rstd_b[r]])
            for c in range(NCH):
                STT(hT[:, c, :N], xT[:, c, :N], pr(l, goff + c), rstd[r][:, :N], ALU.mult, ALU.mult,
                    [xT_b[c], rstd_b[r]], [hT_b])

        epsc = sb("epsc", [128, 2])
        MSET(epsc[:, 0:1], EPS, [ALLB])
        MSET(epsc[:, 1:2], 1.0, [ALLB])
        m0s = sb("m0s", [128, MH])
        m0s_b = [Buf() for _ in range(MH)]
        ncol = sb("ncol", [128, MH])
        ncol_b = Buf()

        def headnorm(ps, pb, N, gcol, out_ap, out_rd, out_wr, also_f32=None):
            i = rot("sq", 2)
            ACT(sqt[i][:, :N], ps[:, :N], AF.Square, [pb], [sqt_b[i]])
            ps2, pb2 = psA()
            MM(ps2[:, :N], ones_bf[:], sqt[i][:, :N], True, True, [sqt_b[i]], [pb2])
            r = rot("rs", 2)
            ACT(rstd[r][:, :N], ps2[:, :N], AF.Sqrt, [pb2], [rstd_b[r]], bias=epsc[:, 0:1], scale=1.0 / HD)
            RCP(rstd[r][:, :N], rstd[r][:, :N], [rstd_b[r]], [rstd_b[r]])
            STT(out_ap, ps[:, :N], gcol, rstd[r][:, :N], ALU.mult, ALU.mult, [pb, rstd_b[r]] + out_rd, out_wr)
            if also_f32 is not None:
                STT(also_f32[0], ps[:, :N], gcol, rstd[r][:, :N], ALU.mult, ALU.mult, [pb, rstd_b[r]], also_f32[1])

        def layer(l, N, tt, is_sample, emit_kv, kv_row0, okk, ovv, tt_b=None):
            bs = min(128, N)
            nblk = N // bs
            if is_sample:
                DMA("pool", nbt3, nbs3scr[l].rearrange("p (h n) -> p h n", h=NH), [nbscr_b[l]], [nbt_b], mdma())
                DMA("pool", nbt4[0:TS], nbs4scr[l].rearrange("p (h n) -> p h n", h=NH), [nbscr_b[l]], [nbt_b], mdma())
            else:
                DMA("pool", nbt[:], nbscr[l].rearrange("p (h n) -> p h n", h=NH), [nbscr_b[l]], [nbt_b], mdma())
            rmsnorm_hT(l, 0, N)
            rh = lambda kc: hT[:, kc, :N]
            rhb = lambda kc: [hT_b]
            ui = [0]

            def nxt():
                ui[0] += 1
                return ui[0] - 1

            for h in range(MH):
                def ev(j, ps, pb, h=h):
                    if j == 0:
                        ACT(g3[:, h, :N], ps[:, :N], AF.Identity, [pb], [g3_b[h]], bias=pr(l, 50 + h))
                    else:
                        ACT(g1[:, h, :N], ps[:, :N], AF.Exp, [pb], [g1_b[h]], bias=negbf[:, l, h:h + 1], scale=-1.0)
                        ACT(g1[:, h, :N], g1[:, h, :N], AF.Ln, [g1_b[h]], [g1_b[h]], bias=epsc[:, 1:2])
                        SCAN(g2[:, h, :N], ones_f[:, :N], g1[:, h, :N], carry[l][:, h:h + 1], ALU.mult, ALU.add,
                             [g1_b[h], carry_b[l][h]], [g2_b[h]])
                        CP(carry[l][:, h:h + 1], g2[:, h, N - 1:N], [g2_b[h]], [carry_b[l][h]])
                        TT(g3[:, h, :N], g3[:, h, :N], g2[:, h, :N], ALU.add, [g3_b[h], g2_b[h]], [g3_b[h]])
                        CP(m0s[:, h:h + 1], carry[l][:, 4 + h:5 + h], [carry_b[l][h]], [m0s_b[h]])
                        SCAN(g1[:, h, :N], ones_f[:, :N], g3[:, h, :N], m0s[:, h:h + 1], ALU.mult, ALU.max,
                             [g3_b[h], m0s_b[h], g1_b[h]], [g1_b[h]])
                        CP(carry[l][:, 4 + h:5 + h], g1[:, h, N - 1:N], [g1_b[h]], [carry_b[l][h]])
                        TT(g2[:, h, :N], g2[:, h, :N], g1[:, h, :N], ALU.subtract, [g2_b[h], g1_b[h]], [g2_b[h]])
                        ACT(g2[:, h, :N], g2[:, h, :N], AF.Exp, [g2_b[h]], [g2_b[h]])
                fm_unit(l, nxt(), rh, rhb, N, ev)
            for dst, dstb, kind in ((mqT, mqT_b, 0), (mkT, mkT_b, 1), (sigo, sigo_b, 2)):
                for u2 in range(2):
                    def ev(j, ps, pb, u2=u2, dst=dst, dstb=dstb, kind=kind):
                        hh = 2 * u2 + j
                        if kind == 2:
                            ACT(dst[:, hh, :N], ps[:, :N], AF.Sigmoid, [pb], [dstb[hh]])
                        elif kind == 0:
                            ACT(dst[:, hh, :N], ps[:, :N], AF.Copy, [pb], [dstb[hh]])
                        else:
                            CP(dst[:, hh, :N], ps[:, :N], [pb], [dstb[hh]])
                    fm_unit(l, nxt(), rh, rhb, N, ev)
            for dst, dstb in ((mktok, mktok_b), (mvtok, mvtok_b)):
                for u2 in range(2):
                    def ev(blk, ps, pb, u2=u2, dst=dst, dstb=dstb):
                        o = dst[:bs, blk, u2 * 256:(u2 + 1) * 256]
                        i_ = ps[:bs, :256]
                        P.op("dve", lambda e, o=o, i_=i_: e.tensor_copy(o, i_), [pb], (), wa=[dstb])
                    tm_unit(l, nxt(), N, ev)
            cvs = {}
            for c in range(4):
                def ev(j, ps, pb, c=c):
                    if j == 0:
                        i = rot("u", 2)
                        cvs["i"] = i
                        ACT(utmp[i][:, :N], ps[:, :N], AF.Copy, [pb], [utmp_b[i]])
                    else:
                        i = cvs["i"]
                        CP(uext[i][:, 0:2], cst[l][:, c, :], [cst_b[l][c]], [uext_b[i]], eng="pool")
                        TT(uext[i][:, 2:2 + N], utmp[i][:, :N], ps[:, :N], ALU.mult, [utmp_b[i], pb, uext_b[i]],
                           [uext_b[i]])
                        CP(cst[l][:, c, :], uext[i][:, N:N + 2], [uext_b[i]], [cst_b[l][c]], eng="pool")
                        a = stage[:, c * T:c * T + N]
                        ACT(a, uext[i][:, 0:N], AF.Copy, [uext_b[i]], [stA], scale=pr(l, 32 + 3 * c))
                        STT(a, uext[i][:, 1:N + 1], pr(l, 33 + 3 * c), a, ALU.mult, ALU.add, [uext_b[i], stA], [stA])
                        STT(a, uext[i][:, 2:N + 2], pr(l, 34 + 3 * c), a, ALU.mult, ALU.add, [uext_b[i], stA], [stA])
                fm_unit(l, nxt(), rh, rhb, N, ev)
            for u2 in range(2):
                def ev(j, ps, pb, u2=u2):
                    c = 2 * u2 + j
                    TT(big[:, c, :N], ps[:, :N], stage[:, c * T:c * T + N], ALU.mult, [pb, stA], [big_b[c]])
                fm_unit(l, nxt(), rh, rhb, N, ev)
            for u2 in range(4):
                def ev(j, ps, pb, u2=u2):
                    h = 2 * u2 + j
                    headnorm(ps, pb, N, pr(l, 44), qT[:, h, :N], [], [qT_b[h]])
                fm_unit(l, nxt(), rh, rhb, N, ev)
            for u2 in range(4):
                def ev(j, ps, pb, u2=u2):
                    h = 2 * u2 + j
                    if emit_kv:
                        i = rot("u", 2)
                        headnorm(ps, pb, N, pr(l, 45), kcur[:, h, :N], [], [kcur_b[h]],
                                 also_f32=(utmp[i][:, :N], [utmp_b[i]]))
                        for blk in range(nblk):
                            pt, ptb = psA()
                            TR(pt[:bs, 0:128], utmp[i][:, blk * bs:(blk + 1) * bs], ident[:], [utmp_b[i]], [ptb])
                            o = stage[:bs, blk * 1024 + h * 128:blk * 1024 + (h + 1) * 128]
                            i_ = pt[:bs, 0:128]
                            fw = (h == 0 and blk == 0)
                            P.op("act", lambda e, o=o, i_=i_: e.activation(o, i_, AF.Copy), [ptb],
                                 [stA, stB] if fw else (), wa=() if fw else [stA, stB])
                    else:
                        headnorm(ps, pb, N, pr(l, 45), kcur[:, h, :N], [], [kcur_b[h]])
                fm_unit(l, nxt(), rh, rhb, N, ev)
            if emit_kv:
                for blk in range(nblk):
                    DMA("pool", okk[l, kv_row0 + blk * bs:kv_row0 + (blk + 1) * bs, :],
                        stage[:bs, blk * 1024:(blk + 1) * 1024], [stA, stB], [], mdma())
            for u2 in range(4):
                def ev(blk, ps, pb, u2=u2):
                    o = vcur[:bs, blk, u2 * 256:(u2 + 1) * 256]
                    i_ = ps[:bs, :256]
                    P.op("dve", lambda e, o=o, i_=i_: e.tensor_copy(o, i_), [pb], (), wa=[vcur_b])
                    if emit_kv:
                        o2 = stage[:bs, blk * 1024 + u2 * 256:blk * 1024 + (u2 + 1) * 256]
                        fw = (u2 == 0 and blk == 0)
                        P.op("dve", lambda e, o2=o2, i_=i_: e.tensor_copy(o2, i_), [pb],
                             [stA, stB] if fw else (), wa=() if fw else [stA, stB])
                tm_unit(l, nxt(), N, ev)
            if emit_kv:
                for blk in range(nblk):
                    DMA("pool", ovv[l, kv_row0 + blk * bs:kv_row0 + (blk + 1) * bs, :],
                        stage[:bs, blk * 1024:(blk + 1) * 1024], [stA, stB], [], mdma())
            assert ui[0] == 32

            if is_sample:
                ktiles = [("p", jj, 128, 0, N, None, False) for jj in range(4)] + [("c", 0, N, 0, N, None, False)]
            else:
                ktiles = []
                chunk0 = tt * NQ
                for j in range(4 + NB):
                    kc0 = 2 * j - 8
                    if chunk0 + kc0 < 0:
                        continue
                    i0, i1 = max(0, kc0), min(NQ - 1, kc0 + 9)
                    if i0 > i1:
                        continue
                    mk_ = (tt_b is not None) and j < 4 and (tt_b * NQ + kc0 < 0)
                    ktiles.append(("p" if j < 4 else "c", j if j < 4 else j - 4, 128, i0 * 64, (i1 + 1) * 64, kc0, mk_))
            for h in range(NH):
                pv, pvb = psB()
                dn, dnb = psB()
                for ti, (src, jj, ksz, q0, q1, kc0, mk_) in enumerate(ktiles):
                    nq = q1 - q0
                    if src == "p":
                        kap, kb = kst[l][:, h, jj * 128:(jj + 1) * 128], kst_b[l]
                        vap, vb = vst[l][:, jj, h * 128:(h + 1) * 128], vst_b[l]
                    else:
                        kap, kb = kcur[:, h, jj * 128:jj * 128 + ksz], kcur_b[h]
                        vap, vb = vcur[:ksz, jj, h * 128:(h + 1) * 128], vcur_b
                    ps, pb = psA()
                    MM(ps[:ksz, :nq], kap, qT[:, h, q0:q1], True, True, [kb, qT_b[h]], [pb])
                    e = rot("E", 2)
                    ACT(Eb[e][:ksz, :nq], ps[:ksz, :nq], AF.Exp, [pb], [Eb_b[e]], bias=pr(l, 58 + h)[:ksz],
                        scale=SCALE)
                    if is_sample:
                        if src == "p" and jj == 3:
                            TT(Eb[e][:, :nq], Eb[e][:, :nq], nbt3[:, h, :], ALU.mult, [Eb_b[e], nbt_b], [Eb_b[e]])
                        elif src == "c":
                            TT(Eb[e][:ksz, :nq], Eb[e][:ksz, :nq], nbt4[:ksz, h, :], ALU.mult, [Eb_b[e], nbt_b],
                               [Eb_b[e]])
                    else:
                        ia, ib = q0 // 64, min(q1 // 64 - 1, kc0 + 3)
                        if ia <= ib:
                            TT(Eb[e][:, ia * 64 - q0:(ib + 1) * 64 - q0], Eb[e][:, ia * 64 - q0:(ib + 1) * 64 - q0],
                               nbt[:, h, (ia - kc0) * 64:(ib - kc0 + 1) * 64], ALU.mult, [Eb_b[e], nbt_b], [Eb_b[e]])
                        if (kc0 + 9) * 64 < q1:
                            c9 = (kc0 + 9) * 64 - q0
                            MSET(Eb[e][0:64, c9:c9 + 64], 0.0, [Eb_b[e]])
                    if mk_:
                        TS_(Eb[e][:ksz, :nq], Eb[e][:ksz, :nq], role[:ksz, 0:1], None, ALU.mult, ALU.bypass,
                            [Eb_b[e]], [Eb_b[e]])
                    first, last = ti == 0, ti == len(ktiles) - 1
                    MM(pv[:, q0:q1], vap, Eb[e][:ksz, :nq], first, last, [vb, Eb_b[e]], [pvb])
                    MM(dn[:, q0:q1], ones_bf[:ksz, :], Eb[e][:ksz, :nq], first, last, [Eb_b[e]], [dnb])
                r = rot("rs", 2)
                RCP(rstd[r][:, :N], dn[:, :N], [dnb], [rstd_b[r]])
                TT(big[:, 4 + h, :N], pv[:, :N], rstd[r][:, :N], ALU.mult, [pvb, rstd_b[r]], [big_b[4 + h]])
            if not is_sample:
                if N < 512:
                    for h in range(NH):
                        CP(kst[l][:, h, 0:512 - N], kst[l][:, h, N:512], [kst_b[l]], [kst_b[l]], eng="pool")
                    for jb in range(4 - NB):
                        CP(vst[l][:, jb, :], vst[l][:, jb + NB, :], [vst_b[l]], [vst_b[l]], eng="pool")
                for h in range(NH):
                    CP(kst[l][:, h, 512 - N:512], kcur[:, h, :N], [kcur_b[h], kst_b[l]], [kst_b[l]], eng="pool")
                for jb in range(NB):
                    CP(vst[l][:, 4 - NB + jb, :], vcur[:, jb, :], [vcur_b, vst_b[l]], [vst_b[l]], eng="pool")

            for h in range(MH):
                ACT(Cbf[l][:, h, :], Cst[l][:, h, :], AF.Copy, [Cst_b[l][h]], [cbf_b[h]])
                ACT(nbf[l][:, h, :], nst[l][:, h, :], AF.Copy, [Cst_b[l][h]], [cbf_b[h]])
            hf = stage[:, 1024:1024 + MH * T].rearrange("p (h t) -> p h t", h=MH)
            for r_ in range(nblk):
                c0, c1 = r_ * bs, (r_ + 1) * bs
                nump, numb = psB()
                denp, denb = psB()
                for h in range(MH):
                    ix = rot("t", NR)
                    tb = tmp_b[ix]
                    M0 = m0s[:, h:h + 1] if r_ == 0 else g1[:, h, c0 - 1:c0]
                    M0b = m0s_b[h] if r_ == 0 else g1_b[h]
                    M1 = g1[:, h, c1 - 1:c1]
                    sb_ = Cst_b[l][h]
                    pt, ptb = psA()
                    TR(pt[:bs, 0:128], g3[:, h, c0:c1], ident[:], [g3_b[h]], [ptb])
                    CP(acol[ix][:bs, 0:1], pt[:bs, 0:1], [ptb], [tb[0]])
                    TT(Wt[ix][:bs, :bs], negmask[:bs, :bs], g1[:bs, h, c0:c1], ALU.subtract, [g1_b[h]], [tb[1]])
                    ACT(Wt[ix][:bs, :bs], Wt[ix][:bs, :bs], AF.Exp, [tb[1], tb[0]], [tb[1]], bias=acol[ix][:bs, 0:1])
                    pss, pssb = psA()
                    MM(pss[:bs, :bs], mkT[:, h, c0:c1], mqT[:, h, c0:c1], True, True, [mkT_b[h], mqT_b[h]], [pssb])
                    STT(PT[ix][:bs, :bs], pss[:bs, :bs], SCALE, Wt[ix][:bs, :bs], ALU.mult, ALU.mult,
                        [pssb, tb[1]], [tb[2]])
                    ACT(eint[ix][:, :bs], g1[:, h, c0:c1], AF.Exp, [g1_b[h], M0b], [tb[3]], bias=M0, scale=-1.0)
                    TT(qtil[ix][:, :bs], mqT[:, h, c0:c1], eint[ix][:, :bs], ALU.mult, [mqT_b[h], tb[3]], [tb[4]])
                    no = nump[:, h * 128:h * 128 + bs]
                    do = denp[:, h * 128:h * 128 + bs]
                    MM(no, Cbf[l][:, h, :], qtil[ix][:, :bs], True, False, [cbf_b[h], tb[4]], [numb])
                    MM(do, nbf[l][:, h, :], qtil[ix][:, :bs], True, False, [cbf_b[h], tb[4]], [denb])
                    MM(no, mvtok[:bs, r_, h * 128:(h + 1) * 128], PT[ix][:bs, :bs], False, True, [mvtok_b, tb[2]],
                       [numb])
                    MM(do, ones_bf[:bs, :], PT[ix][:bs, :bs], False, True, [tb[2]], [denb])
                    ACT(wcol[ix][:bs, 0:1], M1[:bs], AF.Exp, [g1_b[h], tb[0]], [tb[5]], bias=acol[ix][:bs, 0:1],
                        scale=-1.0)
                    ACT(wcol[ix][:, 1:2], M1, AF.Exp, [g1_b[h], M0b], [tb[6]], bias=M0, scale=-1.0)
                    TS_(kw[ix][:bs, :], mktok[:bs, r_, h * 128:(h + 1) * 128], wcol[ix][:bs, 0:1], SCALE,
                        ALU.mult, ALU.mult, [mktok_b, tb[5]], [tb[7]])
                    pu, pub = psA()
                    MM(pu[:, 0:128], kw[ix][:bs, :], mvtok[:bs, r_, h * 128:(h + 1) * 128], True, True,
                       [tb[7], mvtok_b], [pub])
                    pn, pnb = psA()
                    MM(pn[:, 0:128], kw[ix][:bs, :], ones_bf[:bs, :], True, True, [tb[7]], [pnb])
                    STT(Cst[l][:, h, :], Cst[l][:, h, :], wcol[ix][:, 1:2], pu[:, 0:128], ALU.mult, ALU.add,
                        [pub, tb[6], sb_], [sb_])
                    STT(nst[l][:, h, :], nst[l][:, h, :], wcol[ix][:, 1:2], pn[:, 0:128], ALU.mult, ALU.add,
                        [pnb, tb[6], sb_], [sb_])
                    ACT(Cbf[l][:, h, :], Cst[l][:, h, :], AF.Copy, [sb_], [cbf_b[h]])
                    ACT(nbf[l][:, h, :], nst[l][:, h, :], AF.Copy, [sb_], [cbf_b[h]])
                di = rot("dd", 1)
                nv = nump[:, :].rearrange("p (h n) -> p h n", h=MH)[:, :, :bs]
                dv = denp[:, :].rearrange("p (h n) -> p h n", h=MH)[:, :, :bs]
                ACT(ddt[di][:, :, :bs], dv, AF.Abs, [denb], [ddt_b[di]])
                TT(ddt[di][:, :, :bs], ddt[di][:, :, :bs], g2[:, :, c0:c1], ALU.max, [ddt_b[di]] + g2_b, [ddt_b[di]])
                RCP(ddt[di][:, :, :bs], ddt[di][:, :, :bs], [ddt_b[di]], [ddt_b[di]])
                TT(hf[:, :, c0:c1], nv, ddt[di][:, :, :bs], ALU.mult, [numb, ddt_b[di]], [stB])
            for h in range(MH):
                i = rot("sq", 2)
                ACT(sqt[i][:, :N], hf[:, h, :N], AF.Square, [stB], [sqt_b[i]])
                ps2, pb2 = psA()
                MM(ps2[:, :N], ones_bf[:], sqt[i][:, :N], True, True, [sqt_b[i]], [pb2])
                r = rot("rs", 2)
                ACT(rstd[r][:, :N], ps2[:, :N], AF.Sqrt, [pb2], [rstd_b[r]], bias=epsc[:, 0:1], scale=1.0 / HD)
                RCP(rstd[r][:, :N], rstd[r][:, :N], [rstd_b[r]], [rstd_b[r]])
                iu = rot("u", 2)
                STT(utmp[iu][:, :N], hf[:, h, :N], pr(l, 46 + h), rstd[r][:, :N], ALU.mult, ALU.mult,
                    [stB, rstd_b[r]], [utmp_b[iu]])
                TT(big[:, 12 + h, :N], utmp[iu][:, :N], sigo[:, h, :N], ALU.mult, [utmp_b[iu], sigo_b[h]],
                   [big_b[12 + h]])

            rb_ = lambda kc: big[:, kc, :N]
            rbb = lambda kc: [big_b[kc]]
            for u2 in range(8):
                def ev(j, ps, pb, u2=u2):
                    c = 2 * u2 + j
                    TT(xT[:, c, :N], xT[:, c, :N], ps[:, :N], ALU.add, [xT_b[c], pb], [xT_b[c]])
                fm_unit(l, nxt(), rb_, rbb, N, ev)
            rmsnorm_hT(l, 16, N)
            for q in range(4):
                for u2 in range(8):
                    def ev(j, ps, pb, u2=u2):
                        fc = 2 * u2 + j
                        i = rot("u", 2)
                        ACT(utmp[i][:, :N], ps[:, :N], AF.Relu, [pb], [utmp_b[i]])
                        TT(big[:, fc, :N], utmp[i][:, :N], utmp[i][:, :N], ALU.mult, [utmp_b[i]], [big_b[fc]], eng="pool")
                    fm_unit(l, nxt(), rh, rhb, N, ev)
                for u2 in range(8):
                    def ev(j, ps, pb, u2=u2):
                        c = 2 * u2 + j
                        TT(xT[:, c, :N], xT[:, c, :N], ps[:, :N], ALU.add, [xT_b[c], pb], [xT_b[c]])
                    fm_unit(l, nxt(), rb_, rbb, N, ev)
            assert ui[0] == UNITS_PER_LAYER

        def state_out(l, oc, on, om, ocv):
            for h in range(MH):
                DMA("pool", oc[l, h], Cst[l][:, h, :], [Cst_b[l][h]], [], mdma())
                DMA("pool", on[l, h].rearrange("(p o) -> p o", o=1), nst[l][:, h, 0:1], [Cst_b[l][h]], [], mdma(),
                    nonc=True)
            TT(mout[:, 0:MH], carry[l][:, MH:2 * MH], carry[l][:, 0:MH], ALU.subtract, carry_b[l], [mout_b])
            DMA("pool", om[l:l + 1, :], mout[0:1, 0:MH], [mout_b], [], mdma())
            for c in range(4):
                DMA("pool", ocv[l][:, c * 128:(c + 1) * 128].rearrange("j p -> p j"), cst[l][:, c, :], [cst_b[l][c]], [],
                    mdma(), nonc=True)

        z0 = Buf("zinit")
        MSET(stage2[:, :], 0.0, [st2])
        for blk in range(NB):
            DMA("pool", gath_p[blk * 128:(blk + 1) * 128, :], stage2[:, :], [st2], [], mdma())
        DMA("pool", gath_s[0:TS, :], stage2[0:TS, :], [st2], [], mdma())
        gp_b, gs_b, bp_b, bs_b = Buf("gp"), Buf("gs"), Buf("bp"), Buf("bs")
        for i_ in range(len(P.ops) - NB - 1, len(P.ops) - 1):
            gp_b.w[P.key(i_)] = i_
        gs_b.w[P.key(len(P.ops) - 1)] = len(P.ops) - 1
        csem = new_dsem()

        def exchange(bounce, gath, bb, gb):
            i_ap, o_ap = bounce.opt(), gath.opt()
            P.op("pool", lambda e: e.collective_compute("AllGather", ALU.bypass, replica_groups=GROUPS,
                                                        ins=[i_ap], outs=[o_ap]),
                 [bb], [gb], dma=csem, dma_inc=1)

        def reset_states(l):
            for h in range(MH):
                MSET(Cst[l][:, h, :], 0.0, [Cst_b[l][h]])
                MSET(nst[l][:, h, :], 0.0, [Cst_b[l][h]])
            MSET(carry[l][:], 0.0, carry_b[l])
            MSET(cst[l][:], 0.0, cst_b[l])

        def select_in(npart):
            TS_(stage[:npart, :], stage[:npart, :], role[:npart, 0:1], None, ALU.mult, ALU.bypass, [stA, stB],
                [stA, stB])
            STT(stage[:npart, :], stage2[:npart, :], role[:npart, 1:2], stage[:npart, :], ALU.mult, ALU.add,
                [st2, stA, stB], [stA, stB])

        for k in range(NSS):
            DMA("pool", stage[0:TS, :], xs[k], [stA, stB], [stA, stB], mdma())
            DMA("pool", stage2[0:TS, :], gath_s[0:TS, :], [gs_b, st2], [st2], mdma())
            select_in(TS)
            for g in range(4):
                ps, pb = psA()
                for j in range(4):
                    c = 4 * g + j
                    TR(ps[:, j * TS:(j + 1) * TS], stage[0:TS, c * 128:(c + 1) * 128], ident[0:TS, 0:TS],
                       [stA, stB], [pb])
                CP(xT[:, 4 * g:4 * g + 4, 0:TS], ps[:, 0:4 * TS].rearrange("p (j t) -> p j t", j=4), [pb],
                   xT_b[4 * g:4 * g + 4])
            for l in range(NL):
                DMA("pool", Cst[l][:], smc[k, l].rearrange("h p v -> p h v"), Cst_b[l], Cst_b[l], mdma())
                DMA("pool", ncol[:], smn[k, l].rearrange("h p -> p h"), [ncol_b], [ncol_b], mdma(), nonc=True)
                for h in range(MH):
                    TS_(nst[l][:, h, :], ones_f[:, 0:128], ncol[:, h:h + 1], None, ALU.mult, ALU.bypass,
                        [ncol_b, Cst_b[l][h]], [Cst_b[l][h]])
                MSET(carry[l][:, 0:MH], 0.0, carry_b[l])
                DMA("pool", carry[l][:, MH:2 * MH], smm[k, l].partition_broadcast(128), carry_b[l], carry_b[l],
                    mdma(), nonc=True)
                for c in range(4):
                    DMA("pool", cst[l][:, c, :], sconv[k, l][:, c * 128:(c + 1) * 128].rearrange("j p -> p j"),
                        [cst_b[l][c]], [cst_b[l][c]], mdma(), nonc=True)
                DMA("pool", vst[l][:], cv[k, l].rearrange("(b p) f -> p b f", p=128), [vst_b[l]], [vst_b[l]], mdma())
                bigf = big[:, :, :].rearrange("p c t -> p (c t)")
                for blk in range(4):
                    DMA("pool", bigf[:, blk * 1024:(blk + 1) * 1024],
                        ck[k, l, blk * 128:(blk + 1) * 128, :], big_b, big_b, mdma())
                for h in range(NH):
                    ps, pb = psA()
                    psv = ps[:, :].bitcast(BF16)
                    for blk in range(4):
                        TR(psv[:, blk * 128:(blk + 1) * 128], bigf[:, blk * 1024 + h * 128:blk * 1024 + (h + 1) * 128],
                           identb[:], big_b, [pb])
                    CP(kst[l][:, h, :], psv[:, 0:512], [pb], [kst_b[l]])
                marks['sload%d_%d' % (k, l)] = len(P.ops)
                layer(l, TS, 0, True, True, 0, o_sk[k], o_sv[k])
                marks['slayer%d_%d' % (k, l)] = len(P.ops)
                state_out(l, o_sc[k], o_sn[k], o_sm[k], o_sconv[k])
                reset_states(l)
            for g in range(4):
                ps, pb = psA()
                for j in range(4):
                    c = 4 * g + j
                    TR(ps[0:TS, j * 128:(j + 1) * 128], xT[:, c, 0:TS], ident[:], [xT_b[c]], [pb])
                o_ = stage[0:TS, g * 512:(g + 1) * 512]
                i_ = ps[0:TS, 0:512]
                P.op("dve", lambda e, o_=o_, i_=i_: e.tensor_copy(o_, i_), [pb],
                     [stA, stB] if g == 0 else (), wa=() if g == 0 else [stA, stB])
            DMA("pool", ys[k * TS:(k + 1) * TS, :], stage[0:TS, :], [stA, stB], [], mdma())
            DMA("pool", bounce_s, stage[0:TS, :], [stA, stB], [bs_b], mdma())
            exchange(bounce_s, gath_s, bs_b, gs_b)

        marks['sample'] = len(P.ops)
        for k in range(NT + 1):
            ta = min(k, NT - 1)
            for blk in range(NB):
                DMA("pool", stage[:, :], xp[ta * T + blk * 128:ta * T + (blk + 1) * 128, :], [stA, stB], [stA, stB],
                    mdma())
                DMA("pool", stage2[:, :], gath_p[blk * 128:(blk + 1) * 128, :], [gp_b, st2], [st2], mdma())
                select_in(128)
                for g in range(4):
                    ps, pb = psA()
                    for j in range(4):
                        c = 4 * g + j
                        TR(ps[:, j * 128:(j + 1) * 128], stage[:, c * 128:(c + 1) * 128], ident[:], [stA, stB], [pb])
                    CP(xT[:, 4 * g:4 * g + 4, blk * 128:(blk + 1) * 128],
                       ps[:, 0:512].rearrange("p (j t) -> p j t", j=4), [pb], xT_b[4 * g:4 * g + 4])
            slot = k - (NT - 2)
            for l in range(NL):
                layer(l, T, k, False, slot >= 0, 0, o_pk[max(slot, 0)], o_pv[max(slot, 0)], tt_b=k - 1)
            if k == 0:
                for l in range(NL):
                    for h in range(MH):
                        TS_(Cst[l][:, h, :], Cst[l][:, h, :], role[:, 0:1], None, ALU.mult, ALU.bypass,
                            [Cst_b[l][h]], [Cst_b[l][h]])
                        TS_(nst[l][:, h, :], nst[l][:, h, :], role[:, 0:1], None, ALU.mult, ALU.bypass,
                            [Cst_b[l][h]], [Cst_b[l][h]])
                    TS_(carry[l][:], carry[l][:], role[:, 0:1], None, ALU.mult, ALU.bypass, carry_b[l], carry_b[l])
                    TS_(cst[l][:], cst[l][:], role[:, 0:1], None, ALU.mult, ALU.bypass, cst_b[l], cst_b[l])
            for blk in range(NB):
                for g in range(4):
                    ps, pb = psA()
                    for j in range(4):
                        c = 4 * g + j
                        TR(ps[:, j * 128:(j + 1) * 128], xT[:, c, blk * 128:(blk + 1) * 128], ident[:], [xT_b[c]],
                           [pb])
                    o_ = stage[:, g * 512:(g + 1) * 512]
                    i_ = ps[:, 0:512]
                    P.op("dve", lambda e, o_=o_, i_=i_: e.tensor_copy(o_, i_), [pb],
                         [stA, stB] if g == 0 else (), wa=() if g == 0 else [stA, stB])
                DMA("pool", yp[k * T + blk * 128:k * T + (blk + 1) * 128, :], stage[:, :], [stA, stB], [], mdma())
                P.op("pool", (lambda blk: lambda e: e.dma_start(out=bounce_p[blk * 128:(blk + 1) * 128, :],
                                                                in_=stage[:, :]))(blk),
                     [stA, stB], [bp_b] if blk == 0 else (), dma=mdma(), wa=() if blk == 0 else [bp_b])
            if k < NT:
                exchange(bounce_p, gath_p, bp_b, gp_b)
            if k >= NT - 1:
                v_ = k - (NT - 1)
                for l in range(NL):
                    state_out(l, o_pc[v_], o_pn[v_], o_pm[v_], o_pconv[v_])
        fin = Buf("fin")
        for d in list(P.dma_last):
            pass
        import os
        cut = os.environ.get("KPREFIX")
        if cut:
            ncut = marks[cut] if cut in marks else int(cut)
            del P.ops[ncut:]
        lastd = {}
        for i_, o_ in enumerate(P.ops):
            if o_[3] is not None:
                lastd[o_[3]] = i_
        P.ops.append(["pool", lambda e: e.memset(mout[:, 0:1], 0.0), sorted(lastd.values()), None, None, False, 0])
        print("nops", len(P.ops), {k: v for k, v in marks.items()}, flush=True)
        P.emit(nc, stack)
    return nc


def host_params(norm_mix_g, conv_w, q_norm_g, k_norm_g, rel_bias, b_igate, b_fgate, mlstm_norm_g, norm_mlp_g, TS=32):
    NL = norm_mix_g.shape[0]
    par = np.zeros((NL, 128, NPAR), np.float32)
    par[:, :, 0:16] = norm_mix_g.reshape(NL, 16, 128).transpose(0, 2, 1)
    par[:, :, 16:32] = norm_mlp_g.reshape(NL, 16, 128).transpose(0, 2, 1)
    cw = conv_w.reshape(NL, 3, 4, 128)
    par[:, :, 32:44] = cw.transpose(0, 3, 2, 1).reshape(NL, 128, 12)
    par[:, :, 44] = q_norm_g
    par[:, :, 45] = k_norm_g
    par[:, :, 46:50] = mlstm_norm_g.reshape(NL, 4, 128).transpose(0, 2, 1)
    par[:, :, 50:54] = b_igate[:, None, :]
    par[:, :, 54:58] = b_fgate[:, None, :]
    par[:, :, 58:66] = rel_bias[:, None, :, 256]
    k = np.arange(128)[:, None]
    c = np.arange(256)[None, :]
    idx = np.clip(c - k, -128, 128) + 128
    nbp = rel_bias[:, :, idx].transpose(0, 2, 1, 3).reshape(NL, 128, NH * 256)
    c2 = np.arange(TS)[None, :]
    idx3 = np.clip(c2 + 128 - k, -128, 128) + 128
    nbs3 = rel_bias[:, :, idx3].transpose(0, 2, 1, 3).reshape(NL, 128, NH * TS)
    k4 = np.arange(TS)[:, None]
    idx4 = np.clip(c2 - k4, -128, 128) + 128
    nbs4 = rel_bias[:, :, idx4].transpose(0, 2, 1, 3).reshape(NL, TS, NH * TS)
    return (np.ascontiguousarray(par), np.ascontiguousarray(nbp, np.float32), np.ascontiguousarray(nbs3, np.float32),
            np.ascontiguousarray(nbs4, np.float32))


def const_inputs():
    ident = np.eye(128, dtype=np.float32)
    s = np.arange(128)[:, None]
    t = np.arange(128)[None, :]
    negmask = np.where(s <= t, 0.0, -1e30).astype(np.float32)
    return ident, negmask


_T = 256
_NSS = 3


def make_in_maps(inp, NLC, T, TS, n_pairs, NT):
    f = lambda a: np.ascontiguousarray(np.asarray(a), dtype=np.float32)
    g = {k: f(v) for k, v in inp.items()}
    L = NT * T
    SB = g["x_sample"].shape[0]
    ident, negmask = const_inputs()
    maps = []
    for c in range(2 * n_pairs):
        s, st = c // 2, c % 2
        ls = slice(st * NLC, (st + 1) * NLC)
        par, nbp, nbs3, nbs4 = host_params(g["norm_mix_g"][ls], g["conv_w"][ls], g["q_norm_g"][ls], g["k_norm_g"][ls],
                                           g["rel_bias"][ls], g["b_igate"][ls], g["b_fgate"][ls],
                                           g["mlstm_norm_g"][ls], g["norm_mlp_g"][ls], TS)
        sidx = [min(max(2 * s + k - st, 2 * s), 2 * s + 1) % SB for k in range(_NSS)]
        role = np.zeros((128, 2), np.float32)
        role[:, st] = 1.0
        maps.append({
            "xp": g["x_prompt"][s % g["x_prompt"].shape[0], :L],
            "xs": np.ascontiguousarray(g["x_sample"][sidx]),
            "ck": np.ascontiguousarray(g["cache_att_k"][ls][:, sidx].transpose(1, 0, 2, 3, 4).reshape(_NSS, NLC, 512, 1024)),
            "cv": np.ascontiguousarray(g["cache_att_v"][ls][:, sidx].transpose(1, 0, 2, 3, 4).reshape(_NSS, NLC, 512, 1024)),
            "sconv": np.ascontiguousarray(g["state_conv"][ls][:, sidx].transpose(1, 0, 2, 3)),
            "smc": np.ascontiguousarray(g["state_mlstm_c"][ls][:, sidx].transpose(1, 0, 2, 3, 4)),
            "smn": np.ascontiguousarray(g["state_mlstm_n"][ls][:, sidx].transpose(1, 0, 2, 3)),
            "smm": np.ascontiguousarray(g["state_mlstm_m"][ls][:, sidx].transpose(1, 0, 2)),
            "role": role,
            "win": np.ascontiguousarray(g["w_in"][ls]), "wout": np.ascontiguousarray(g["w_out"][ls]),
            "wup": np.ascontiguousarray(g["w_up"][ls]), "wdn": np.ascontiguousarray(g["w_down"][ls]),
            "par": par, "nbp": nbp, "nbs3": nbs3, "nbs4": nbs4, "ident": ident, "negmask": negmask,
        })
    return maps


def assemble(R, NLC, T, TS, n_pairs, NT, SB):
    yp, ys = [], [None] * SB
    P = {k: [[None] * n_pairs for _ in range(2 * NLC)] for k in ("conv", "k", "v", "c", "n", "m")}
    S = {k: [[None] * SB for _ in range(2 * NLC)] for k in ("conv", "k", "v", "c", "n", "m")}
    for s in range(n_pairs):
        A, B = R[2 * s], R[2 * s + 1]
        yp.append(B["yp"][T:(NT + 1) * T])
        for j in range(2):
            if 2 * s + j < SB:
                ys[2 * s + j] = B["ys"][(j + 1) * TS:(j + 2) * TS]
        for st, C in ((0, A), (1, B)):
            for l in range(NLC):
                gl = st * NLC + l
                P["conv"][gl][s] = C["o_pconv"][st, l]
                P["c"][gl][s] = C["o_pc"][st, l]
                P["n"][gl][s] = C["o_pn"][st, l]
                P["m"][gl][s] = C["o_pm"][st, l]
                nsl = min(2, NT)
                P["k"][gl][s] = np.concatenate([C["o_pk"][st + i + (2 - nsl), l] for i in range(nsl)], axis=0)
                P["v"][gl][s] = np.concatenate([C["o_pv"][st + i + (2 - nsl), l] for i in range(nsl)], axis=0)
                for j in range(2):
                    if 2 * s + j < SB:
                        S["conv"][gl][2 * s + j] = C["o_sconv"][j + st, l]
                        S["k"][gl][2 * s + j] = C["o_sk"][j + st, l]
                        S["v"][gl][2 * s + j] = C["o_sv"][j + st, l]
                        S["c"][gl][2 * s + j] = C["o_sc"][j + st, l]
                        S["n"][gl][2 * s + j] = C["o_sn"][j + st, l]
                        S["m"][gl][2 * s + j] = C["o_sm"][j + st, l]
    st_ = lambda d: np.stack([np.stack(x) for x in d])
    KEEP = min(512, NT * T)
    NLT = 2 * NLC
    outs = (np.stack(yp), np.stack(ys), st_(P["conv"]),
            st_(P["k"]).reshape(NLT, n_pairs, KEEP, NH, HD), st_(P["v"]).reshape(NLT, n_pairs, KEEP, NH, HD),
            st_(P["c"]), st_(P["n"]), st_(P["m"]), st_(S["conv"]),
            st_(S["k"]).reshape(NLT, SB, TS, NH, HD), st_(S["v"]).reshape(NLT, SB, TS, NH, HD),
            st_(S["c"]), st_(S["n"]), st_(S["m"]))
    return tuple(np.ascontiguousarray(o, dtype=np.float32) for o in outs)


def kernel(x_prompt, x_sample, cache_att_k, cache_att_v, state_conv, state_mlstm_c, state_mlstm_n, state_mlstm_m,
           norm_mix_g, w_in, conv_w, q_norm_g, k_norm_g, rel_bias, b_igate, b_fgate, mlstm_norm_g, w_out,
           norm_mlp_g, w_up, w_down):
    inp = dict(x_prompt=x_prompt, x_sample=x_sample, cache_att_k=cache_att_k, cache_att_v=cache_att_v,
               state_conv=state_conv, state_mlstm_c=state_mlstm_c, state_mlstm_n=state_mlstm_n,
               state_mlstm_m=state_mlstm_m, norm_mix_g=norm_mix_g, w_in=w_in, conv_w=conv_w, q_norm_g=q_norm_g,
               k_norm_g=k_norm_g, rel_bias=rel_bias, b_igate=b_igate, b_fgate=b_fgate, mlstm_norm_g=mlstm_norm_g,
               w_out=w_out, norm_mlp_g=norm_mlp_g, w_up=w_up, w_down=w_down)
    NLAY = np.asarray(w_in).shape[0]
    B, L, _ = np.asarray(x_prompt).shape
    SB, TS, _ = np.asarray(x_sample).shape
    NLC = NLAY // 2
    NT = L // _T
    nc = build(NLC, NT, _T, TS, _NSS)
    in_maps = make_in_maps(inp, NLC, _T, TS, B, NT)
    res = run_bass_kernel_spmd(nc, in_maps, core_ids=list(range(2 * B)))
    return assemble(res.results, NLC, _T, TS, B, NT, SB)
```

```python
import contextlib
import numpy as np
import concourse.bass as bass
import concourse.mybir as mybir
from concourse.bass_utils import run_bass_kernel_spmd

F32 = mybir.dt.float32
BF16 = mybir.dt.bfloat16
ALU = mybir.AluOpType
AF = mybir.ActivationFunctionType

D = 2048
NCH = 16
HD = 128
NH = 8
MH = 4
DFF = 8192
IN_DIM = 6664
C_XA, C_GB, C_GC, C_Q, C_K, C_V, C_MQ, C_MK, C_MV, C_MO, C_IG, C_FG = (
    0, 512, 1024, 1536, 2560, 3584, 4608, 5120, 5632, 6144, 6656, 6660)
EPS = 1e-6
SCALE = 128 ** -0.5
NPAR = 66
UNITS_PER_LAYER = 104


class Buf:
    __slots__ = ("w", "r", "name")

    def __init__(self, name=""):
        self.w = {}
        self.r = {}
        self.name = name


class Prog:
    ENGS = ("pe", "act", "dve", "pool", "sp")

    def __init__(self):
        self.ops = []
        self.dma_count = {}
        self.dma_last = {}
        self.dma_inc = {}

    def key(self, i):
        o = self.ops[i]
        return ("d", o[3]) if o[3] is not None else o[0]

    def op(self, eng, fn, rd=(), wr=(), dma=None, wa=(), dma_inc=16):
        i = len(self.ops)
        raw = set()
        deps = {}

        def need(j):
            k = self.key(j)
            if deps.get(k, -1) < j:
                deps[k] = j

        for b in rd:
            for j in b.w.values():
                raw.add(j)
                need(j)
        for b in wr:
            for j in b.w.values():
                need(j)
            for j in b.r.values():
                need(j)
        for b in wa:
            for j in b.r.values():
                need(j)
        if dma is None:
            if eng in deps and deps[eng] not in raw:
                cand = [j for j in raw if self.key(j) == eng]
                if cand:
                    deps[eng] = max(cand)
                else:
                    del deps[eng]
            mykey = eng
            ordn = None
        else:
            mykey = ("d", dma)
            if dma in self.dma_last:
                need(self.dma_last[dma])
            self.dma_last[dma] = i
            self.dma_inc[dma] = dma_inc
            self.dma_count[dma] = self.dma_count.get(dma, 0) + 1
            ordn = self.dma_count[dma]
        self.ops.append([eng, fn, sorted(deps.values()), dma, ordn, False, 0])
        for b in rd:
            b.r[mykey] = i
        for b in wr:
            b.w = {mykey: i}
            b.r = {}
        for b in wa:
            b.w[mykey] = i
        return i

    def emit(self, nc, stack):
        ops = self.ops
        sems = {e: stack.enter_context(nc.semaphore("s_" + e)) for e in self.ENGS}
        dsems = {d: stack.enter_context(nc.semaphore("d_%d" % d)) for d in sorted(self.dma_count)}
        waited = {e: {} for e in self.ENGS}
        waits = []
        for i, o in enumerate(ops):
            e = o[0]
            wl = []
            for j in o[2]:
                k = self.key(j)
                if waited[e].get(k, -1) >= j:
                    continue
                waited[e][k] = j
                wl.append(j)
                if ops[j][3] is None:
                    ops[j][5] = True
            waits.append(wl)
        cnt = {e: 0 for e in self.ENGS}
        for o in ops:
            if o[3] is None and o[5]:
                cnt[o[0]] += 1
                o[6] = cnt[o[0]]
        per = {e: [] for e in self.ENGS}
        for i, o in enumerate(ops):
            per[o[0]].append(i)

        def run(engname, eng):
            for i in per[engname]:
                o = ops[i]
                for j in waits[i]:
                    oj = ops[j]
                    if oj[3] is not None:
                        eng.wait_ge(dsems[oj[3]], self.dma_inc[oj[3]] * oj[4])
                    else:
                        eng.wait_ge(sems[oj[0]], oj[6])
                ins = o[1](eng)
                if o[3] is not None:
                    if self.dma_inc[o[3]] == 1:
                        ins.then_inc(dsems[o[3]])
                    else:
                        ins.then_inc(dsems[o[3]], 16)
                elif o[5]:
                    ins.then_inc(sems[engname], 1)

        with nc.Block() as block:
            @block.tensor
            def _(e):
                run("pe", e)

            @block.scalar
            def _(e):
                run("act", e)

            @block.vector
            def _(e):
                run("dve", e)

            @block.gpsimd
            def _(e):
                run("pool", e)

            @block.sync
            def _(e):
                run("sp", e)


def unit_table():
    u = []
    for h in range(MH):
        u.append(("gate", h))
    for base in (C_MQ, C_MK, C_MO):
        for j in range(2):
            u.append(("fm", "win", 0, base + j * 256, base + j * 256 + 128))
    for base in (C_MK, C_MV):
        for j in range(2):
            u.append(("tm", base + j * 256))
    for c in range(4):
        u.append(("fm", "win", 0, C_GC + c * 128, C_XA + c * 128))
    for j in range(2):
        u.append(("fm", "win", 0, C_GB + j * 256, C_GB + j * 256 + 128))
    for base in (C_Q, C_K):
        for j in range(4):
            u.append(("fm", "win", 0, base + j * 256, base + j * 256 + 128))
    for j in range(4):
        u.append(("tm", C_V + j * 256))
    for j in range(8):
        u.append(("fm", "wout", 0, j * 256, j * 256 + 128))
    for q in range(4):
        for j in range(8):
            u.append(("fm", "wup", 0, q * 2048 + j * 256, q * 2048 + j * 256 + 128))
        for j in range(8):
            u.append(("fm", "wdn", q * 2048, j * 256, j * 256 + 128))
    assert len(u) == UNITS_PER_LAYER
    return u


UNITS = unit_table()


def build(NL, NT, T, TS=32, NSS=3):
    NTOK = NT * T
    NB = T // 128
    NQ = T // 64
    nc = bass.Bass("TRN2", target_bir_lowering=False)
    P = Prog()

    def din(name, shape, dt=F32):
        return nc.dram_tensor(name, list(shape), dt, kind="ExternalInput").ap()

    def dout(name, shape, dt=F32):
        return nc.dram_tensor(name, list(shape), dt, kind="ExternalOutput").ap()

    xp = din("xp", [NTOK, D])
    xs = din("xs", [NSS, TS, D])
    ck = din("ck", [NSS, NL, 512, 1024])
    cv = din("cv", [NSS, NL, 512, 1024])
    sconv = din("sconv", [NSS, NL, 2, 512])
    smc = din("smc", [NSS, NL, MH, 128, 128])
    smn = din("smn", [NSS, NL, MH, 128])
    smm = din("smm", [NSS, NL, MH])
    role_d = din("role", [128, 2])
    win = din("win", [NL, D, IN_DIM])
    wout = din("wout", [NL, D, D])
    wup = din("wup", [NL, D, DFF])
    wdn = din("wdn", [NL, DFF, D])
    par_d = din("par", [NL, 128, NPAR])
    nbp_d = din("nbp", [NL, 128, NH * 256])
    nbs3_d = din("nbs3", [NL, 128, NH * TS])
    nbs4_d = din("nbs4", [NL, TS, NH * TS])
    ident_d = din("ident", [128, 128])
    negmask_d = din("negmask", [128, 128])
    WM = {"win": win, "wout": wout, "wup": wup, "wdn": wdn}

    yp = dout("yp", [(NT + 1) * T, D])
    ys = dout("ys", [NSS * TS, D])
    o_pconv = dout("o_pconv", [2, NL, 2, 512])
    o_pk = dout("o_pk", [3, NL, T, 1024])
    o_pv = dout("o_pv", [3, NL, T, 1024])
    o_pc = dout("o_pc", [2, NL, MH, 128, 128])
    o_pn = dout("o_pn", [2, NL, MH, 128])
    o_pm = dout("o_pm", [2, NL, MH])
    o_sconv = dout("o_sconv", [NSS, NL, 2, 512])
    o_sk = dout("o_sk", [NSS, NL, TS, 1024])
    o_sv = dout("o_sv", [NSS, NL, TS, 1024])
    o_sc = dout("o_sc", [NSS, NL, MH, 128, 128])
    o_sn = dout("o_sn", [NSS, NL, MH, 128])
    o_sm = dout("o_sm", [NSS, NL, MH])

    bounce_p = nc.dram_tensor("bounce_p", [T, D], F32).ap()
    gath_p = nc.dram_tensor("gath_p", [2 * T, D], F32).ap()
    bounce_s = nc.dram_tensor("bounce_s", [TS, D], F32).ap()
    gath_s = nc.dram_tensor("gath_s", [2 * TS, D], F32).ap()
    GROUPS = [[0, 1], [2, 3], [4, 5], [6, 7]]
    wscr_l = [nc.dram_tensor("wscr%d" % l, [UNITS_PER_LAYER, 128, 4096], BF16).ap() for l in range(NL)]
    nbscr = nc.dram_tensor("nbscr", [NL, 128, NH * 256], BF16).ap()
    nbs3scr = nc.dram_tensor("nbs3scr", [NL, 128, NH * TS], BF16).ap()
    nbs4scr = nc.dram_tensor("nbs4scr", [NL, TS, NH * TS], BF16).ap()

    stack = contextlib.ExitStack()
    with stack:
        def sb(name, shape, dt=F32):
            return stack.enter_context(nc.sbuf_tensor("sb_" + name, list(shape), dt))

        banks = [stack.enter_context(nc.psum_tensor("bank%d" % i, [128, 512], F32)) for i in range(8)]
        bank_bufs = [Buf("bank%d" % i) for i in range(8)]
        pool_ctr = {"A": 0, "B": 0}

        def psA():
            i = pool_ctr["A"] % 4
            pool_ctr["A"] += 1
            return banks[i], bank_bufs[i]

        def psB():
            i = 4 + pool_ctr["B"] % 4
            pool_ctr["B"] += 1
            return banks[i], bank_bufs[i]

        xT = sb("xT", [128, NCH, T])
        hT = sb("hT", [128, NCH, T], BF16)
        big = sb("big", [128, NCH, T], BF16)
        NWS = 6
        wb = [sb("wb%d" % i, [128, 4096], BF16) for i in range(NWS)]
        wb_buf = [Buf("wb%d" % i) for i in range(NWS)]
        stage = sb("stage", [128, 2048])
        stA, stB = Buf("stA"), Buf("stB")
        stage2 = sb("stage2", [128, 2048])
        st2 = Buf("st2")
        role = sb("role", [128, 2])
        kst = [sb("kst%d" % l, [128, NH, 512], BF16) for l in range(NL)]
        vst = [sb("vst%d" % l, [128, 4, 1024], BF16) for l in range(NL)]
        kst_b = [Buf() for _ in range(NL)]
        vst_b = [Buf() for _ in range(NL)]
        kcur = sb("kcur", [128, NH, T], BF16)
        vcur = sb("vcur", [128, NB, 1024], BF16)
        qT = sb("qT", [128, NH, T], BF16)
        kcur_b = [Buf() for _ in range(NH)]
        qT_b = [Buf() for _ in range(NH)]
        vcur_b = Buf()
        nbt = sb("nbt", [128, NH, 256], BF16)
        nbt3 = nbt[:, :, 0:TS]
        nbt4 = nbt[:, :, TS:2 * TS]
        nbt_b = Buf()
        g1 = sb("g1", [128, MH, T])
        g2 = sb("g2", [128, MH, T])
        g3 = sb("g3", [128, MH, T])
        g1_b = [Buf() for _ in range(MH)]
        g2_b = [Buf() for _ in range(MH)]
        g3_b = [Buf() for _ in range(MH)]
        mqT = sb("mqT", [128, MH, T], BF16)
        mkT = sb("mkT", [128, MH, T], BF16)
        sigo = sb("sigo", [128, MH, T], BF16)
        mqT_b = [Buf() for _ in range(MH)]
        mkT_b = [Buf() for _ in range(MH)]
        sigo_b = [Buf() for _ in range(MH)]
        mktok = sb("mktok", [128, NB, 512], BF16)
        mvtok = sb("mvtok", [128, NB, 512], BF16)
        mktok_b, mvtok_b = Buf(), Buf()
        uext = [sb("uext%d" % i, [128, T + 2]) for i in range(2)]
        uext_b = [Buf() for _ in range(2)]
        utmp = [sb("utmp%d" % i, [128, T]) for i in range(2)]
        utmp_b = [Buf() for _ in range(2)]
        rstd = [sb("rstd%d" % i, [128, T]) for i in range(2)]
        rstd_b = [Buf() for _ in range(2)]
        sqt = [sb("sqt%d" % i, [128, T], BF16) for i in range(3)]
        sqt_b = [Buf() for _ in range(3)]
        NE = 5
        Eb = [sb("E%d" % i, [128, T], BF16) for i in range(NE)]
        Eb_b = [Buf() for _ in range(NE)]
        par = sb("par", [128, NL, NPAR])
        negbf = sb("negbf", [128, NL, MH])
        negch = sb("negch", [128, NL, NH])
        ident = sb("ident", [128, 128])
        identb = sb("identb", [128, 128], BF16)
        negmask = sb("negmask", [128, 128])
        ones_bf = sb("ones_bf", [128, 128], BF16)
        ones_f = sb("ones_f", [128, T])
        Cst = [sb("Cst%d" % l, [128, MH, 128]) for l in range(NL)]
        Cbf1 = sb("Cbf1", [128, MH, 128], BF16)
        nbf1 = sb("nbf1", [128, MH, 128], BF16)
        Cbf = [Cbf1 for l in range(NL)]
        cbf_b = [Buf() for _ in range(MH)]
        nst = [sb("nst%d" % l, [128, MH, 128]) for l in range(NL)]
        nbf = [nbf1 for l in range(NL)]
        carry = [sb("carry%d" % l, [128, 2 * MH]) for l in range(NL)]
        Cst_b = [[Buf() for _ in range(MH)] for _ in range(NL)]
        carry_b = [[Buf() for _ in range(MH)] for _ in range(NL)]
        cst = [sb("cst%d" % l, [128, 4, 2]) for l in range(NL)]
        cst_b = [[Buf() for _ in range(4)] for _ in range(NL)]
        xT_b = [Buf("xT%d" % c) for c in range(NCH)]
        hT_b = Buf("hT")
        big_b = [Buf("big%d" % c) for c in range(NCH)]
        NR = 4
        acol = [sb("acol%d" % i, [128, 2]) for i in range(NR)]
        Wt = [sb("Wt%d" % i, [128, 128]) for i in range(NR)]
        PT = [sb("PT%d" % i, [128, 128], BF16) for i in range(NR)]
        eint = [sb("eint%d" % i, [128, 128]) for i in range(NR)]
        qtil = [sb("qtil%d" % i, [128, 128], BF16) for i in range(NR)]
        wcol = [sb("wcol%d" % i, [128, 2]) for i in range(NR)]
        kw = [sb("kw%d" % i, [128, 128], BF16) for i in range(NR)]
        tmp_b = [[Buf() for _ in range(8)] for _ in range(NR)]
        ddt = [sb("ddt%d" % i, [128, MH, 128]) for i in range(1)]
        ddt_b = [Buf() for _ in range(1)]
        wgf = sb("wgf", [128, NCH, 8])
        wgf_b = Buf()
        mout = sb("mout", [128, 2 * MH])
        mout_b = Buf()

        dma_ids = {"n": 0}

        def new_dsem():
            dma_ids["n"] += 1
            return dma_ids["n"] - 1

        wsem = [new_dsem() for _ in range(NWS)]
        psem = [new_dsem() for _ in range(8)]
        msem = [new_dsem() for _ in range(6)]
        ctr = {"p": 0, "m": 0, "w": 0, "rot": 0, "sq": 0, "rs": 0, "E": 0, "u": 0, "t": 0, "dd": 0}

        def pdma():
            ctr["p"] += 1
            return psem[ctr["p"] % 8]

        def mdma():
            ctr["m"] += 1
            return msem[ctr["m"] % 6]

        def MM(out, lhsT, rhs, start, stop, rd, wr):
            P.op("pe", lambda e: e.matmul(out, lhsT, rhs, start=start, stop=stop), rd, wr)

        def TR(out, in_, idn, rd, wr):
            P.op("pe", lambda e: e.transpose(out, in_, idn), rd, wr)

        def ACT(out, in_, func, rd, wr, bias=None, scale=1.0):
            if bias is None:
                P.op("act", lambda e: e.activation(out, in_, func, scale=scale), rd, wr)
            else:
                P.op("act", lambda e: e.activation(out, in_, func, bias=bias, scale=scale), rd, wr)

        def TT(out, a, b, op, rd, wr, eng="dve"):
            P.op(eng, lambda e: e.tensor_tensor(out, a, b, op), rd, wr)

        def STT(out, in0, scalar, in1, op0, op1, rd, wr):
            P.op("dve", lambda e: e.scalar_tensor_tensor(out, in0, scalar, in1, op0, op1), rd, wr)

        def TS_(out, in0, s1, s2, op0, op1, rd, wr, eng="dve"):
            P.op(eng, lambda e: e.tensor_scalar(out, in0, s1, s2, op0, op1), rd, wr)

        def CP(out, in_, rd, wr, eng="dve"):
            P.op(eng, lambda e: e.tensor_copy(out, in_), rd, wr)

        def RCP(out, in_, rd, wr):
            P.op("dve", lambda e: e.reciprocal(out, in_), rd, wr)

        def SCAN(out, d0, d1, init, op0, op1, rd, wr):
            P.op("dve", lambda e: e.tensor_tensor_scan(out, d0, d1, init, op0, op1), rd, wr)

        def MSET(ap, val, wr, eng="pool"):
            P.op(eng, lambda e: e.memset(ap, val), (), wr)

        def DMA(eng, out, in_, rd, wr, sem, nonc=False):
            if nonc:
                def f(e):
                    with nc.allow_non_contiguous_dma(reason="small strided transfer"):
                        return e.dma_start(out=out, in_=in_)
            else:
                def f(e):
                    return e.dma_start(out=out, in_=in_)
            P.op(eng, f, rd, wr, dma=sem)

        marks = {}
        ALLB = Buf("init")
        DMA("pool", ident[:], ident_d, (), [ALLB], mdma())
        DMA("pool", negmask[:], negmask_d, (), [ALLB], mdma())
        DMA("pool", role[:], role_d, (), [ALLB], mdma())
        DMA("pool", par[:], par_d.rearrange("l p n -> p l n"), (), [ALLB], mdma(), nonc=True)
        MSET(ones_bf[:], 1.0, [ALLB])
        MSET(ones_f[:], 1.0, [ALLB])
        CP(identb[:], ident[:], [ALLB], [ALLB], eng="pool")
        for l in range(NL):
            TS_(negbf[:, l, :], par[:, l, 54:58], -1.0, None, ALU.mult, ALU.bypass, [ALLB], [ALLB], eng="pool")
            TS_(negch[:, l, :], par[:, l, 58:66], -1.0, None, ALU.mult, ALU.bypass, [ALLB], [ALLB], eng="pool")

        def pr(l, a, b=None):
            return par[:, l, a:(a + 1 if b is None else b)]

        nbscr_b = [Buf() for _ in range(NL)]
        for l in range(NL):
            DMA("pool", stage[:, 0:NH * 256], nbp_d[l], [ALLB], [stA, stB], mdma())
            for h in range(NH):
                ACT(nbt[:, h, :], stage[:, h * 256:(h + 1) * 256], AF.Exp, [stA, stB, ALLB], [nbt_b],
                    bias=negch[:, l, h:h + 1])
            for h in range(NH):
                MSET(nbt[64:128, h, 0:64], 0.0, [nbt_b])
            DMA("pool", nbscr[l].rearrange("p (h n) -> p h n", h=NH), nbt[:], [nbt_b], [nbscr_b[l]], mdma())
            DMA("pool", stage[:, 0:NH * TS], nbs3_d[l], [nbt_b], [stA, stB], mdma())
            for h in range(NH):
                ACT(nbt3[:, h, :], stage[:, h * TS:(h + 1) * TS], AF.Exp, [stA, stB], [nbt_b],
                    bias=negch[:, l, h:h + 1])
            DMA("pool", nbs3scr[l].rearrange("p (h n) -> p h n", h=NH), nbt3, [nbt_b], [nbscr_b[l]], mdma())
            DMA("pool", stage[0:TS, 0:NH * TS], nbs4_d[l], [nbt_b], [stA, stB], mdma())
            for h in range(NH):
                ACT(nbt4[0:TS, h, :], stage[0:TS, h * TS:(h + 1) * TS], AF.Exp, [stA, stB], [nbt_b],
                    bias=negch[0:TS, l, h:h + 1])
            DMA("pool", nbs4scr[l].rearrange("p (h n) -> p h n", h=NH), nbt4[0:TS], [nbt_b], [nbscr_b[l]], mdma())

        marks['init'] = len(P.ops)
        wscr_b = [[Buf() for _ in range(UNITS_PER_LAYER)] for _ in range(NL)]

        def DMAw(out, in_, b, sem):
            P.op("pool", lambda e: e.dma_start(out=out, in_=in_), (), (), dma=sem, wa=[b])

        for l in range(NL):
            DMA("pool", wgf[:], win[l][:, C_IG:C_IG + 8].rearrange("(kc p) g -> p kc g", p=128),
                [wgf_b], [wgf_b], mdma(), nonc=True)
            for ui, u in enumerate(UNITS):
                dst = wscr_l[l][ui]
                if u[0] == "gate":
                    h = u[1]
                    rb = ui % 2
                    wv = wb[rb][:, :].rearrange("p (j kc m) -> p j kc m", j=2, kc=NCH)
                    for j, col in enumerate((h, 4 + h)):
                        CP(wv[:, j, :, :], wgf[:, :, col:col + 1].to_broadcast([128, NCH, 128]),
                           [wgf_b], [wb_buf[rb]], eng="pool")
                    DMA("pool", dst, wb[rb][:, :], [wb_buf[rb]], [wscr_b[l][ui]], pdma())
                elif u[0] == "fm":
                    W = WM[u[1]][l]
                    r0 = u[2]
                    for j, col in enumerate(u[3:5]):
                        src = W[r0:r0 + 2048, col:col + 128].rearrange("(kc p) m -> p kc m", p=128)
                        DMAw(dst[:, j * 2048:(j + 1) * 2048].rearrange("p (kc m) -> p kc m", m=128), src,
                             wscr_b[l][ui], pdma())
                else:
                    col = u[1]
                    src = win[l][:, col:col + 256].rearrange("(kc p) n -> p kc n", p=128)
                    DMAw(dst.rearrange("p (kc n) -> p kc n", n=256), src, wscr_b[l][ui], pdma())

        marks['prepass'] = len(P.ops)
        def wload(l, ui):
            s = ctr["w"] % NWS
            ctr["w"] += 1
            DMA("sp", wb[s][:], wscr_l[l][ui], [wscr_b[l][ui]], [wb_buf[s]], wsem[s])
            return s

        def fm_unit(l, ui, rhs, rbufs, N, evac):
            s = wload(l, ui)
            for j in range(2):
                ps, pb = psA()
                for kc in range(NCH):
                    MM(ps[:, :N], wb[s][:, (j * NCH + kc) * 128:(j * NCH + kc + 1) * 128], rhs(kc),
                       kc == 0, kc == NCH - 1, [wb_buf[s]] + rbufs(kc), [pb])
                evac(j, ps, pb)

        def tm_unit(l, ui, N, evac):
            s = wload(l, ui)
            bs = min(128, N)
            for blk in range(N // bs):
                ps, pb = psA()
                for kc in range(NCH):
                    MM(ps[:bs, :256], hT[:, kc, blk * bs:(blk + 1) * bs], wb[s][:, kc * 256:(kc + 1) * 256],
                       kc == 0, kc == NCH - 1, [wb_buf[s], hT_b], [pb])
                evac(blk, ps, pb)

        def rot(name, n):
            i = ctr[name] % n
            ctr[name] += 1
            return i

        def rmsnorm_hT(l, goff, N, pre=None):
            if pre is None:
                ps, pb = psA()
                for c in range(NCH):
                    i = rot("sq", 3)
                    ACT(sqt[i][:, :N], xT[:, c, :N], AF.Square, [xT_b[c]], [sqt_b[i]])
                    MM(ps[:, :N], ones_bf[:], sqt[i][:, :N], c == 0, c == NCH - 1, [sqt_b[i]], [pb])
            else:
                ps, pb = pre
            r = rot("rs", 2)
            ACT(rstd[r][:, :N], ps[:, :N], AF.Sqrt, [pb], [rstd_b[r]], bias=epsc[:, 0:1], scale=1.0 / D)
            RCP(rstd[r][:, :N], rstd[r][:, :N], [rstd_b[r]], [rstd_b[r]])
            for c in range(NCH):
                STT(hT[:, c, :N], xT[:, c, :N], pr(l, goff + c), rstd[r][:, :N], ALU.mult, ALU.mult,
                    [xT_b[c], rstd_b[r]], [hT_b])

        epsc = sb("epsc", [128, 2])
        MSET(epsc[:, 0:1], EPS, [ALLB])
        MSET(epsc[:, 1:2], 1.0, [ALLB])
        m0s = sb("m0s", [128, MH])
        m0s_b = [Buf() for _ in range(MH)]
        ncol = sb("ncol", [128, MH])
        ncol_b = Buf()

        def headnorm(ps, pb, N, gcol, out_ap, out_rd, out_wr, also_f32=None):
            i = rot("sq", 3)
            ACT(sqt[i][:, :N], ps[:, :N], AF.Square, [pb], [sqt_b[i]])
            ps2, pb2 = psA()
            MM(ps2[:, :N], ones_bf[:], sqt[i][:, :N], True, True, [sqt_b[i]], [pb2])
            r = rot("rs", 2)
            ACT(rstd[r][:, :N], ps2[:, :N], AF.Sqrt, [pb2], [rstd_b[r]], bias=epsc[:, 0:1], scale=1.0 / HD)
            RCP(rstd[r][:, :N], rstd[r][:, :N], [rstd_b[r]], [rstd_b[r]])
            STT(out_ap, ps[:, :N], gcol, rstd[r][:, :N], ALU.mult, ALU.mult, [pb, rstd_b[r]] + out_rd, out_wr)
            if also_f32 is not None:
                STT(also_f32[0], ps[:, :N], gcol, rstd[r][:, :N], ALU.mult, ALU.mult, [pb, rstd_b[r]], also_f32[1])

        nxt_pre = [None]

        def layer(l, N, tt, is_sample, emit_kv, kv_row0, okk, ovv, tt_b=None, next_norm=False):
            bs = min(128, N)
            nblk = N // bs
            if is_sample:
                DMA("pool", nbt3, nbs3scr[l].rearrange("p (h n) -> p h n", h=NH), [nbscr_b[l]], [nbt_b], mdma())
                DMA("pool", nbt4[0:TS], nbs4scr[l].rearrange("p (h n) -> p h n", h=NH), [nbscr_b[l]], [nbt_b], mdma())
            else:
                DMA("pool", nbt[:], nbscr[l].rearrange("p (h n) -> p h n", h=NH), [nbscr_b[l]], [nbt_b], mdma())
            rmsnorm_hT(l, 0, N, pre=nxt_pre[0])
            nxt_pre[0] = None
            rh = lambda kc: hT[:, kc, :N]
            rhb = lambda kc: [hT_b]
            ui = [0]

            def nxt():
                ui[0] += 1
                return ui[0] - 1

            for h in range(MH):
                def ev(j, ps, pb, h=h):
                    if j == 0:
                        ACT(g3[:, h, :N], ps[:, :N], AF.Identity, [pb], [g3_b[h]], bias=pr(l, 50 + h))
                    else:
                        ACT(g1[:, h, :N], ps[:, :N], AF.Exp, [pb], [g1_b[h]], bias=negbf[:, l, h:h + 1], scale=-1.0)
                        ACT(g1[:, h, :N], g1[:, h, :N], AF.Ln, [g1_b[h]], [g1_b[h]], bias=epsc[:, 1:2])
                        SCAN(g2[:, h, :N], ones_f[:, :N], g1[:, h, :N], carry[l][:, h:h + 1], ALU.mult, ALU.add,
                             [g1_b[h], carry_b[l][h]], [g2_b[h]])
                        CP(carry[l][:, h:h + 1], g2[:, h, N - 1:N], [g2_b[h]], [carry_b[l][h]])
                        TT(g3[:, h, :N], g3[:, h, :N], g2[:, h, :N], ALU.add, [g3_b[h], g2_b[h]], [g3_b[h]])
                        CP(m0s[:, h:h + 1], carry[l][:, 4 + h:5 + h], [carry_b[l][h]], [m0s_b[h]])
                        SCAN(g1[:, h, :N], ones_f[:, :N], g3[:, h, :N], m0s[:, h:h + 1], ALU.mult, ALU.max,
                             [g3_b[h], m0s_b[h], g1_b[h]], [g1_b[h]])
                        CP(carry[l][:, 4 + h:5 + h], g1[:, h, N - 1:N], [g1_b[h]], [carry_b[l][h]])
                        TT(g2[:, h, :N], g2[:, h, :N], g1[:, h, :N], ALU.subtract, [g2_b[h], g1_b[h]], [g2_b[h]])
                        ACT(g2[:, h, :N], g2[:, h, :N], AF.Exp, [g2_b[h]], [g2_b[h]])
                fm_unit(l, nxt(), rh, rhb, N, ev)
            for dst, dstb, kind in ((mqT, mqT_b, 0), (mkT, mkT_b, 1), (sigo, sigo_b, 2)):
                for u2 in range(2):
                    def ev(j, ps, pb, u2=u2, dst=dst, dstb=dstb, kind=kind):
                        hh = 2 * u2 + j
                        if kind == 2:
                            ACT(dst[:, hh, :N], ps[:, :N], AF.Sigmoid, [pb], [dstb[hh]])
                        elif kind == 0:
                            ACT(dst[:, hh, :N], ps[:, :N], AF.Copy, [pb], [dstb[hh]])
                        else:
                            CP(dst[:, hh, :N], ps[:, :N], [pb], [dstb[hh]])
                    fm_unit(l, nxt(), rh, rhb, N, ev)
            for dst, dstb in ((mktok, mktok_b), (mvtok, mvtok_b)):
                for u2 in range(2):
                    def ev(blk, ps, pb, u2=u2, dst=dst, dstb=dstb):
                        o = dst[:bs, blk, u2 * 256:(u2 + 1) * 256]
                        i_ = ps[:bs, :256]
                        P.op("dve", lambda e, o=o, i_=i_: e.tensor_copy(o, i_), [pb], (), wa=[dstb])
                    tm_unit(l, nxt(), N, ev)
            cvs = {}
            for c in range(4):
                def ev(j, ps, pb, c=c):
                    if j == 0:
                        i = rot("u", 2)
                        cvs["i"] = i
                        ACT(utmp[i][:, :N], ps[:, :N], AF.Copy, [pb], [utmp_b[i]])
                    else:
                        i = cvs["i"]
                        CP(uext[i][:, 0:2], cst[l][:, c, :], [cst_b[l][c]], [uext_b[i]], eng="pool")
                        TT(uext[i][:, 2:2 + N], utmp[i][:, :N], ps[:, :N], ALU.mult, [utmp_b[i], pb, uext_b[i]],
                           [uext_b[i]])
                        CP(cst[l][:, c, :], uext[i][:, N:N + 2], [uext_b[i]], [cst_b[l][c]], eng="pool")
                        a = stage[:, c * T:c * T + N]
                        ACT(a, uext[i][:, 0:N], AF.Copy, [uext_b[i]], [stA], scale=pr(l, 32 + 3 * c))
                        STT(a, uext[i][:, 1:N + 1], pr(l, 33 + 3 * c), a, ALU.mult, ALU.add, [uext_b[i], stA], [stA])
                        STT(a, uext[i][:, 2:N + 2], pr(l, 34 + 3 * c), a, ALU.mult, ALU.add, [uext_b[i], stA], [stA])
                fm_unit(l, nxt(), rh, rhb, N, ev)
            for u2 in range(2):
                def ev(j, ps, pb, u2=u2):
                    c = 2 * u2 + j
                    TT(big[:, c, :N], ps[:, :N], stage[:, c * T:c * T + N], ALU.mult, [pb, stA], [big_b[c]])
                fm_unit(l, nxt(), rh, rhb, N, ev)
            for u2 in range(4):
                def ev(j, ps, pb, u2=u2):
                    h = 2 * u2 + j
                    headnorm(ps, pb, N, pr(l, 44), qT[:, h, :N], [], [qT_b[h]])
                fm_unit(l, nxt(), rh, rhb, N, ev)
            for u2 in range(4):
                def ev(j, ps, pb, u2=u2):
                    h = 2 * u2 + j
                    if emit_kv:
                        i = rot("u", 2)
                        headnorm(ps, pb, N, pr(l, 45), kcur[:, h, :N], [], [kcur_b[h]],
                                 also_f32=(utmp[i][:, :N], [utmp_b[i]]))
                        for blk in range(nblk):
                            pt, ptb = psA()
                            TR(pt[:bs, 0:128], utmp[i][:, blk * bs:(blk + 1) * bs], ident[:], [utmp_b[i]], [ptb])
                            o = stage[:bs, blk * 1024 + h * 128:blk * 1024 + (h + 1) * 128]
                            i_ = pt[:bs, 0:128]
                            fw = (h == 0 and blk == 0)
                            P.op("act", lambda e, o=o, i_=i_: e.activation(o, i_, AF.Copy), [ptb],
                                 [stA, stB] if fw else (), wa=() if fw else [stA, stB])
                    else:
                        headnorm(ps, pb, N, pr(l, 45), kcur[:, h, :N], [], [kcur_b[h]])
                fm_unit(l, nxt(), rh, rhb, N, ev)
            if emit_kv:
                for blk in range(nblk):
                    DMA("pool", okk[l, kv_row0 + blk * bs:kv_row0 + (blk + 1) * bs, :],
                        stage[:bs, blk * 1024:(blk + 1) * 1024], [stA, stB], [], mdma())
            for u2 in range(4):
                def ev(blk, ps, pb, u2=u2):
                    o = vcur[:bs, blk, u2 * 256:(u2 + 1) * 256]
                    i_ = ps[:bs, :256]
                    P.op("dve", lambda e, o=o, i_=i_: e.tensor_copy(o, i_), [pb], (), wa=[vcur_b])
                    if emit_kv:
                        o2 = stage[:bs, blk * 1024 + u2 * 256:blk * 1024 + (u2 + 1) * 256]
                        fw = (u2 == 0 and blk == 0)
                        P.op("dve", lambda e, o2=o2, i_=i_: e.tensor_copy(o2, i_), [pb],
                             [stA, stB] if fw else (), wa=() if fw else [stA, stB])
                tm_unit(l, nxt(), N, ev)
            if emit_kv:
                for blk in range(nblk):
                    DMA("pool", ovv[l, kv_row0 + blk * bs:kv_row0 + (blk + 1) * bs, :],
                        stage[:bs, blk * 1024:(blk + 1) * 1024], [stA, stB], [], mdma())
            assert ui[0] == 32

            if is_sample:
                ktiles = [("p", jj, 128, 0, N, None, False) for jj in range(4)] + [("c", 0, N, 0, N, None, False)]
            else:
                ktiles = []
                chunk0 = tt * NQ
                for j in range(4 + NB):
                    kc0 = 2 * j - 8
                    if chunk0 + kc0 < 0:
                        continue
                    i0, i1 = max(0, kc0), min(NQ - 1, kc0 + 9)
                    if i0 > i1:
                        continue
                    mk_ = (tt_b is not None) and j < 4 and (tt_b * NQ + kc0 < 0)
                    ktiles.append(("p" if j < 4 else "c", j if j < 4 else j - 4, 128, i0 * 64, (i1 + 1) * 64, kc0, mk_))
            items = [(h, ti) for h in range(NH) for ti in range(len(ktiles))]
            accs = {}
            sinfo = {}
            LA = 3

            def emit_S(idx):
                h, ti = items[idx]
                src, jj, ksz, q0, q1, kc0, mk_ = ktiles[ti]
                nq = q1 - q0
                if src == "p":
                    kap, kb = kst[l][:, h, jj * 128:(jj + 1) * 128], kst_b[l]
                    vap, vb = vst[l][:, jj, h * 128:(h + 1) * 128], vst_b[l]
                else:
                    kap, kb = kcur[:, h, jj * 128:jj * 128 + ksz], kcur_b[h]
                    vap, vb = vcur[:ksz, jj, h * 128:(h + 1) * 128], vcur_b
                ps, pb = psA()
                MM(ps[:ksz, :nq], kap, qT[:, h, q0:q1], True, True, [kb, qT_b[h]], [pb])
                e = rot("E", NE)
                ACT(Eb[e][:ksz, :nq], ps[:ksz, :nq], AF.Exp, [pb], [Eb_b[e]], bias=pr(l, 58 + h)[:ksz],
                    scale=SCALE)
                if is_sample:
                    if src == "p" and jj == 3:
                        TT(Eb[e][:, :nq], Eb[e][:, :nq], nbt3[:, h, :], ALU.mult, [Eb_b[e], nbt_b], [Eb_b[e]])
                    elif src == "c":
                        TT(Eb[e][:ksz, :nq], Eb[e][:ksz, :nq], nbt4[:ksz, h, :], ALU.mult, [Eb_b[e], nbt_b],
                           [Eb_b[e]])
                else:
                    ia, ib = q0 // 64, min(q1 // 64 - 1, kc0 + 3)
                    if ia <= ib:
                        TT(Eb[e][:, ia * 64 - q0:(ib + 1) * 64 - q0], Eb[e][:, ia * 64 - q0:(ib + 1) * 64 - q0],
                           nbt[:, h, (ia - kc0) * 64:(ib - kc0 + 1) * 64], ALU.mult, [Eb_b[e], nbt_b], [Eb_b[e]])
                    if (kc0 + 9) * 64 < q1:
                        c9 = (kc0 + 9) * 64 - q0
                        MSET(Eb[e][0:64, c9:c9 + 64], 0.0, [Eb_b[e]])
                if mk_:
                    TS_(Eb[e][:ksz, :nq], Eb[e][:ksz, :nq], role[:ksz, 0:1], None, ALU.mult, ALU.bypass,
                        [Eb_b[e]], [Eb_b[e]])
                sinfo[idx] = (e, ksz, nq, q0, q1, vap, vb)

            def emit_PV(idx):
                h, ti = items[idx]
                e, ksz, nq, q0, q1, vap, vb = sinfo.pop(idx)
                if ti == 0:
                    accs[h] = psB() + psB()
                pv, pvb, dn, dnb = accs[h]
                first, last = ti == 0, ti == len(ktiles) - 1
                MM(pv[:, q0:q1], vap, Eb[e][:ksz, :nq], first, last, [vb, Eb_b[e]], [pvb])
                MM(dn[:, q0:q1], ones_bf[:ksz, :], Eb[e][:ksz, :nq], first, last, [Eb_b[e]], [dnb])
                if last:
                    r = rot("rs", 2)
                    RCP(rstd[r][:, :N], dn[:, :N], [dnb], [rstd_b[r]])
                    TT(big[:, 4 + h, :N], pv[:, :N], rstd[r][:, :N], ALU.mult, [pvb, rstd_b[r]], [big_b[4 + h]])
                    del accs[h]

            for idx in range(len(items) + LA):
                if idx < len(items):
                    emit_S(idx)
                if idx - LA >= 0:
                    emit_PV(idx - LA)
            if not is_sample:
                if N < 512:
                    for h in range(NH):
                        CP(kst[l][:, h, 0:512 - N], kst[l][:, h, N:512], [kst_b[l]], [kst_b[l]], eng="pool")
                    for jb in range(4 - NB):
                        CP(vst[l][:, jb, :], vst[l][:, jb + NB, :], [vst_b[l]], [vst_b[l]], eng="pool")
                for h in range(NH):
                    CP(kst[l][:, h, 512 - N:512], kcur[:, h, :N], [kcur_b[h], kst_b[l]], [kst_b[l]], eng="pool")
                for jb in range(NB):
                    CP(vst[l][:, 4 - NB + jb, :], vcur[:, jb, :], [vcur_b, vst_b[l]], [vst_b[l]], eng="pool")

            for h in range(MH):
                ACT(Cbf[l][:, h, :], Cst[l][:, h, :], AF.Copy, [Cst_b[l][h]], [cbf_b[h]])
                ACT(nbf[l][:, h, :], nst[l][:, h, :], AF.Copy, [Cst_b[l][h]], [cbf_b[h]])
            hf = stage[:, 1024:1024 + MH * T].rearrange("p (h t) -> p h t", h=MH)
            for r_ in range(nblk):
                c0, c1 = r_ * bs, (r_ + 1) * bs
                nump, numb = psB()
                denp, denb = psB()
                hctx = []
                for h in range(MH):
                    ix = rot("t", NR)
                    tb = tmp_b[ix]
                    M0 = m0s[:, h:h + 1] if r_ == 0 else g1[:, h, c0 - 1:c0]
                    M0b = m0s_b[h] if r_ == 0 else g1_b[h]
                    M1 = g1[:, h, c1 - 1:c1]
                    sb_ = Cst_b[l][h]
                    pt, ptb = psA()
                    TR(pt[:bs, 0:128], g3[:, h, c0:c1], ident[:], [g3_b[h]], [ptb])
                    CP(acol[ix][:bs, 0:1], pt[:bs, 0:1], [ptb], [tb[0]])
                    TT(Wt[ix][:bs, :bs], negmask[:bs, :bs], g1[:bs, h, c0:c1], ALU.subtract, [g1_b[h]], [tb[1]])
                    ACT(Wt[ix][:bs, :bs], Wt[ix][:bs, :bs], AF.Exp, [tb[1], tb[0]], [tb[1]], bias=acol[ix][:bs, 0:1])
                    pss, pssb = psA()
                    MM(pss[:bs, :bs], mkT[:, h, c0:c1], mqT[:, h, c0:c1], True, True, [mkT_b[h], mqT_b[h]], [pssb])
                    STT(PT[ix][:bs, :bs], pss[:bs, :bs], SCALE, Wt[ix][:bs, :bs], ALU.mult, ALU.mult,
                        [pssb, tb[1]], [tb[2]])
                    ACT(eint[ix][:, :bs], g1[:, h, c0:c1], AF.Exp, [g1_b[h], M0b], [tb[3]], bias=M0, scale=-1.0)
                    TT(qtil[ix][:, :bs], mqT[:, h, c0:c1], eint[ix][:, :bs], ALU.mult, [mqT_b[h], tb[3]], [tb[4]])
                    ACT(wcol[ix][:bs, 0:1], M1[:bs], AF.Exp, [g1_b[h], tb[0]], [tb[5]], bias=acol[ix][:bs, 0:1],
                        scale=-1.0)
                    ACT(wcol[ix][:, 1:2], M1, AF.Exp, [g1_b[h], M0b], [tb[6]], bias=M0, scale=-1.0)
                    TS_(kw[ix][:bs, :], mktok[:bs, r_, h * 128:(h + 1) * 128], wcol[ix][:bs, 0:1], SCALE,
                        ALU.mult, ALU.mult, [mktok_b, tb[5]], [tb[7]])
                    hctx.append((ix, tb, sb_))
                for h in range(MH):
                    ix, tb, sb_ = hctx[h]
                    no = nump[:, h * 128:h * 128 + bs]
                    do = denp[:, h * 128:h * 128 + bs]
                    MM(no, Cbf[l][:, h, :], qtil[ix][:, :bs], True, False, [cbf_b[h], tb[4]], [numb])
                    MM(no, mvtok[:bs, r_, h * 128:(h + 1) * 128], PT[ix][:bs, :bs], False, True, [mvtok_b, tb[2]],
                       [numb])
                    MM(do, nbf[l][:, h, :], qtil[ix][:, :bs], True, False, [cbf_b[h], tb[4]], [denb])
                    MM(do, ones_bf[:bs, :], PT[ix][:bs, :bs], False, True, [tb[2]], [denb])
                for h in range(MH):
                    ix, tb, sb_ = hctx[h]
                    pu, pub = psA()
                    MM(pu[:, 0:128], kw[ix][:bs, :], mvtok[:bs, r_, h * 128:(h + 1) * 128], True, True,
                       [tb[7], mvtok_b], [pub])
                    pn, pnb = psA()
                    MM(pn[:, 0:128], kw[ix][:bs, :], ones_bf[:bs, :], True, True, [tb[7]], [pnb])
                    STT(Cst[l][:, h, :], Cst[l][:, h, :], wcol[ix][:, 1:2], pu[:, 0:128], ALU.mult, ALU.add,
                        [pub, tb[6], sb_], [sb_])
                    STT(nst[l][:, h, :], nst[l][:, h, :], wcol[ix][:, 1:2], pn[:, 0:128], ALU.mult, ALU.add,
                        [pnb, tb[6], sb_], [sb_])
                    ACT(Cbf[l][:, h, :], Cst[l][:, h, :], AF.Copy, [sb_], [cbf_b[h]])
                    ACT(nbf[l][:, h, :], nst[l][:, h, :], AF.Copy, [sb_], [cbf_b[h]])
                di = rot("dd", 1)
                nv = nump[:, :].rearrange("p (h n) -> p h n", h=MH)[:, :, :bs]
                dv = denp[:, :].rearrange("p (h n) -> p h n", h=MH)[:, :, :bs]
                ACT(ddt[di][:, :, :bs], dv, AF.Abs, [denb], [ddt_b[di]])
                TT(ddt[di][:, :, :bs], ddt[di][:, :, :bs], g2[:, :, c0:c1], ALU.max, [ddt_b[di]] + g2_b, [ddt_b[di]])
                RCP(ddt[di][:, :, :bs], ddt[di][:, :, :bs], [ddt_b[di]], [ddt_b[di]])
                TT(hf[:, :, c0:c1], nv, ddt[di][:, :, :bs], ALU.mult, [numb, ddt_b[di]], [stB])
            for h in range(MH):
                i = rot("sq", 3)
                ACT(sqt[i][:, :N], hf[:, h, :N], AF.Square, [stB], [sqt_b[i]])
                ps2, pb2 = psA()
                MM(ps2[:, :N], ones_bf[:], sqt[i][:, :N], True, True, [sqt_b[i]], [pb2])
                r = rot("rs", 2)
                ACT(rstd[r][:, :N], ps2[:, :N], AF.Sqrt, [pb2], [rstd_b[r]], bias=epsc[:, 0:1], scale=1.0 / HD)
                RCP(rstd[r][:, :N], rstd[r][:, :N], [rstd_b[r]], [rstd_b[r]])
                iu = rot("u", 2)
                STT(utmp[iu][:, :N], hf[:, h, :N], pr(l, 46 + h), rstd[r][:, :N], ALU.mult, ALU.mult,
                    [stB, rstd_b[r]], [utmp_b[iu]])
                TT(big[:, 12 + h, :N], utmp[iu][:, :N], sigo[:, h, :N], ALU.mult, [utmp_b[iu], sigo_b[h]],
                   [big_b[12 + h]])

            rb_ = lambda kc: big[:, kc, :N]
            rbb = lambda kc: [big_b[kc]]
            def resid_stats():
                st_ps, st_pb = psB()
                pend = []

                def flush():
                    while pend:
                        c_, i_ = pend.pop(0)
                        MM(st_ps[:, :N], ones_bf[:], sqt[i_][:, :N], c_ == 0, c_ == NCH - 1, [sqt_b[i_]], [st_pb])

                def ev(j, ps, pb, u2):
                    c = 2 * u2 + j
                    TT(xT[:, c, :N], xT[:, c, :N], ps[:, :N], ALU.add, [xT_b[c], pb], [xT_b[c]])
                    i = rot("sq", 3)
                    ACT(sqt[i][:, :N], xT[:, c, :N], AF.Square, [xT_b[c]], [sqt_b[i]])
                    flush()
                    pend.append((c, i))
                return ev, flush, (st_ps, st_pb)

            ev_s, flush_s, pre_s = resid_stats()
            for u2 in range(8):
                fm_unit(l, nxt(), rb_, rbb, N, lambda j, ps, pb, u2=u2: ev_s(j, ps, pb, u2))
            flush_s()
            rmsnorm_hT(l, 16, N, pre=pre_s)
            for q in range(4):
                for u2 in range(8):
                    def ev(j, ps, pb, u2=u2):
                        fc = 2 * u2 + j
                        i = rot("u", 2)
                        ACT(utmp[i][:, :N], ps[:, :N], AF.Relu, [pb], [utmp_b[i]])
                        TT(big[:, fc, :N], utmp[i][:, :N], utmp[i][:, :N], ALU.mult, [utmp_b[i]], [big_b[fc]], eng="pool")
                    fm_unit(l, nxt(), rh, rhb, N, ev)
                if q == 3 and next_norm:
                    ev_s, flush_s, pre_s = resid_stats()
                    for u2 in range(8):
                        fm_unit(l, nxt(), rb_, rbb, N, lambda j, ps, pb, u2=u2: ev_s(j, ps, pb, u2))
                    flush_s()
                    nxt_pre[0] = pre_s
                else:
                    for u2 in range(8):
                        def ev(j, ps, pb, u2=u2):
                            c = 2 * u2 + j
                            TT(xT[:, c, :N], xT[:, c, :N], ps[:, :N], ALU.add, [xT_b[c], pb], [xT_b[c]])
                        fm_unit(l, nxt(), rb_, rbb, N, ev)
            assert ui[0] == UNITS_PER_LAYER

        def state_out(l, oc, on, om, ocv):
            for h in range(MH):
                DMA("pool", oc[l, h], Cst[l][:, h, :], [Cst_b[l][h]], [], mdma())
                DMA("pool", on[l, h].rearrange("(p o) -> p o", o=1), nst[l][:, h, 0:1], [Cst_b[l][h]], [], mdma(),
                    nonc=True)
            TT(mout[:, 0:MH], carry[l][:, MH:2 * MH], carry[l][:, 0:MH], ALU.subtract, carry_b[l], [mout_b])
            DMA("pool", om[l:l + 1, :], mout[0:1, 0:MH], [mout_b], [], mdma())
            for c in range(4):
                DMA("pool", ocv[l][:, c * 128:(c + 1) * 128].rearrange("j p -> p j"), cst[l][:, c, :], [cst_b[l][c]], [],
                    mdma(), nonc=True)

        z0 = Buf("zinit")
        MSET(stage2[:, :], 0.0, [st2])
        for blk in range(NB):
            DMA("pool", gath_p[blk * 128:(blk + 1) * 128, :], stage2[:, :], [st2], [], mdma())
        DMA("pool", gath_s[0:TS, :], stage2[0:TS, :], [st2], [], mdma())
        gp_b, gs_b, bp_b, bs_b = Buf("gp"), Buf("gs"), Buf("bp"), Buf("bs")
        for i_ in range(len(P.ops) - NB - 1, len(P.ops) - 1):
            gp_b.w[P.key(i_)] = i_
        gs_b.w[P.key(len(P.ops) - 1)] = len(P.ops) - 1
        csem = new_dsem()

        def exchange(bounce, gath, bb, gb):
            i_ap, o_ap = bounce.opt(), gath.opt()
            P.op("pool", lambda e: e.collective_compute("AllGather", ALU.bypass, replica_groups=GROUPS,
                                                        ins=[i_ap], outs=[o_ap]),
                 [bb], [gb], dma=csem, dma_inc=1)

        def reset_states(l):
            for h in range(MH):
                MSET(Cst[l][:, h, :], 0.0, [Cst_b[l][h]])
                MSET(nst[l][:, h, :], 0.0, [Cst_b[l][h]])
            MSET(carry[l][:], 0.0, carry_b[l])
            MSET(cst[l][:], 0.0, cst_b[l])

        def select_in(npart):
            TS_(stage[:npart, :], stage[:npart, :], role[:npart, 0:1], None, ALU.mult, ALU.bypass, [stA, stB],
                [stA, stB])
            STT(stage[:npart, :], stage2[:npart, :], role[:npart, 1:2], stage[:npart, :], ALU.mult, ALU.add,
                [st2, stA, stB], [stA, stB])

        for k in range(NSS):
            DMA("pool", stage[0:TS, :], xs[k], [stA, stB], [stA, stB], mdma())
            DMA("pool", stage2[0:TS, :], gath_s[0:TS, :], [gs_b, st2], [st2], mdma())
            select_in(TS)
            for g in range(4):
                ps, pb = psA()
                for j in range(4):
                    c = 4 * g + j
                    TR(ps[:, j * TS:(j + 1) * TS], stage[0:TS, c * 128:(c + 1) * 128], ident[0:TS, 0:TS],
                       [stA, stB], [pb])
                CP(xT[:, 4 * g:4 * g + 4, 0:TS], ps[:, 0:4 * TS].rearrange("p (j t) -> p j t", j=4), [pb],
                   xT_b[4 * g:4 * g + 4])
            for l in range(NL):
                DMA("pool", Cst[l][:], smc[k, l].rearrange("h p v -> p h v"), Cst_b[l], Cst_b[l], mdma())
                DMA("pool", ncol[:], smn[k, l].rearrange("h p -> p h"), [ncol_b], [ncol_b], mdma(), nonc=True)
                for h in range(MH):
                    TS_(nst[l][:, h, :], ones_f[:, 0:128], ncol[:, h:h + 1], None, ALU.mult, ALU.bypass,
                        [ncol_b, Cst_b[l][h]], [Cst_b[l][h]])
                MSET(carry[l][:, 0:MH], 0.0, carry_b[l])
                DMA("pool", carry[l][:, MH:2 * MH], smm[k, l].partition_broadcast(128), carry_b[l], carry_b[l],
                    mdma(), nonc=True)
                for c in range(4):
                    DMA("pool", cst[l][:, c, :], sconv[k, l][:, c * 128:(c + 1) * 128].rearrange("j p -> p j"),
                        [cst_b[l][c]], [cst_b[l][c]], mdma(), nonc=True)
                DMA("pool", vst[l][:], cv[k, l].rearrange("(b p) f -> p b f", p=128), [vst_b[l]], [vst_b[l]], mdma())
                bigf = big[:, :, :].rearrange("p c t -> p (c t)")
                for blk in range(4):
                    DMA("pool", bigf[:, blk * 1024:(blk + 1) * 1024],
                        ck[k, l, blk * 128:(blk + 1) * 128, :], big_b, big_b, mdma())
                for h in range(NH):
                    ps, pb = psA()
                    psv = ps[:, :].bitcast(BF16)
                    for blk in range(4):
                        TR(psv[:, blk * 128:(blk + 1) * 128], bigf[:, blk * 1024 + h * 128:blk * 1024 + (h + 1) * 128],
                           identb[:], big_b, [pb])
                    CP(kst[l][:, h, :], psv[:, 0:512], [pb], [kst_b[l]])
                marks['sload%d_%d' % (k, l)] = len(P.ops)
                layer(l, TS, 0, True, True, 0, o_sk[k], o_sv[k], next_norm=False)
                marks['slayer%d_%d' % (k, l)] = len(P.ops)
                state_out(l, o_sc[k], o_sn[k], o_sm[k], o_sconv[k])
                reset_states(l)
            for g in range(4):
                ps, pb = psA()
                for j in range(4):
                    c = 4 * g + j
                    TR(ps[0:TS, j * 128:(j + 1) * 128], xT[:, c, 0:TS], ident[:], [xT_b[c]], [pb])
                o_ = stage[0:TS, g * 512:(g + 1) * 512]
                i_ = ps[0:TS, 0:512]
                P.op("dve", lambda e, o_=o_, i_=i_: e.tensor_copy(o_, i_), [pb],
                     [stA, stB] if g == 0 else (), wa=() if g == 0 else [stA, stB])
            DMA("pool", ys[k * TS:(k + 1) * TS, :], stage[0:TS, :], [stA, stB], [], mdma())
            DMA("pool", bounce_s, stage[0:TS, :], [stA, stB], [bs_b], mdma())
            exchange(bounce_s, gath_s, bs_b, gs_b)

        marks['sample'] = len(P.ops)
        for k in range(NT + 1):
            ta = min(k, NT - 1)
            for blk in range(NB):
                DMA("pool", stage[:, :], xp[ta * T + blk * 128:ta * T + (blk + 1) * 128, :], [stA, stB], [stA, stB],
                    mdma())
                DMA("pool", stage2[:, :], gath_p[blk * 128:(blk + 1) * 128, :], [gp_b, st2], [st2], mdma())
                select_in(128)
                for g in range(4):
                    ps, pb = psA()
                    for j in range(4):
                        c = 4 * g + j
                        TR(ps[:, j * 128:(j + 1) * 128], stage[:, c * 128:(c + 1) * 128], ident[:], [stA, stB], [pb])
                    CP(xT[:, 4 * g:4 * g + 4, blk * 128:(blk + 1) * 128],
                       ps[:, 0:512].rearrange("p (j t) -> p j t", j=4), [pb], xT_b[4 * g:4 * g + 4])
            slot = k - (NT - 2)
            for l in range(NL):
                layer(l, T, k, False, slot >= 0, 0, o_pk[max(slot, 0)], o_pv[max(slot, 0)], tt_b=k - 1,
                      next_norm=(l < NL - 1))
            if k == 0:
                for l in range(NL):
                    for h in range(MH):
                        TS_(Cst[l][:, h, :], Cst[l][:, h, :], role[:, 0:1], None, ALU.mult, ALU.bypass,
                            [Cst_b[l][h]], [Cst_b[l][h]])
                        TS_(nst[l][:, h, :], nst[l][:, h, :], role[:, 0:1], None, ALU.mult, ALU.bypass,
                            [Cst_b[l][h]], [Cst_b[l][h]])
                    TS_(carry[l][:], carry[l][:], role[:, 0:1], None, ALU.mult, ALU.bypass, carry_b[l], carry_b[l])
                    TS_(cst[l][:], cst[l][:], role[:, 0:1], None, ALU.mult, ALU.bypass, cst_b[l], cst_b[l])
            for blk in range(NB):
                for g in range(4):
                    ps, pb = psA()
                    for j in range(4):
                        c = 4 * g + j
                        TR(ps[:, j * 128:(j + 1) * 128], xT[:, c, blk * 128:(blk + 1) * 128], ident[:], [xT_b[c]],
                           [pb])
                    o_ = stage[:, g * 512:(g + 1) * 512]
                    i_ = ps[:, 0:512]
                    P.op("dve", lambda e, o_=o_, i_=i_: e.tensor_copy(o_, i_), [pb],
                         [stA, stB] if g == 0 else (), wa=() if g == 0 else [stA, stB])
                DMA("pool", yp[k * T + blk * 128:k * T + (blk + 1) * 128, :], stage[:, :], [stA, stB], [], mdma())
                P.op("pool", (lambda blk: lambda e: e.dma_start(out=bounce_p[blk * 128:(blk + 1) * 128, :],
                                                                in_=stage[:, :]))(blk),
                     [stA, stB], [bp_b] if blk == 0 else (), dma=mdma(), wa=() if blk == 0 else [bp_b])
            if k < NT:
                exchange(bounce_p, gath_p, bp_b, gp_b)
            if k >= NT - 1:
                v_ = k - (NT - 1)
                for l in range(NL):
                    state_out(l, o_pc[v_], o_pn[v_], o_pm[v_], o_pconv[v_])
        fin = Buf("fin")
        for d in list(P.dma_last):
            pass
        import os
        cut = os.environ.get("KPREFIX")
        if cut:
            ncut = marks[cut] if cut in marks else int(cut)
            del P.ops[ncut:]
        lastd = {}
        for i_, o_ in enumerate(P.ops):
            if o_[3] is not None:
                lastd[o_[3]] = i_
        P.ops.append(["pool", lambda e: e.memset(mout[:, 0:1], 0.0), sorted(lastd.values()), None, None, False, 0])
        print("nops", len(P.ops), {k: v for k, v in marks.items()}, flush=True)
        P.emit(nc, stack)
    return nc


def host_params(norm_mix_g, conv_w, q_norm_g, k_norm_g, rel_bias, b_igate, b_fgate, mlstm_norm_g, norm_mlp_g, TS=32):
    NL = norm_mix_g.shape[0]
    par = np.zeros((NL, 128, NPAR), np.float32)
    par[:, :, 0:16] = norm_mix_g.reshape(NL, 16, 128).transpose(0, 2, 1)
    par[:, :, 16:32] = norm_mlp_g.reshape(NL, 16, 128).transpose(0, 2, 1)
    cw = conv_w.reshape(NL, 3, 4, 128)
    par[:, :, 32:44] = cw.transpose(0, 3, 2, 1).reshape(NL, 128, 12)
    par[:, :, 44] = q_norm_g
    par[:, :, 45] = k_norm_g
    par[:, :, 46:50] = mlstm_norm_g.reshape(NL, 4, 128).transpose(0, 2, 1)
    par[:, :, 50:54] = b_igate[:, None, :]
    par[:, :, 54:58] = b_fgate[:, None, :]
    par[:, :, 58:66] = rel_bias[:, None, :, 256]
    k = np.arange(128)[:, None]
    c = np.arange(256)[None, :]
    idx = np.clip(c - k, -128, 128) + 128
    nbp = rel_bias[:, :, idx].transpose(0, 2, 1, 3).reshape(NL, 128, NH * 256)
    c2 = np.arange(TS)[None, :]
    idx3 = np.clip(c2 + 128 - k, -128, 128) + 128
    nbs3 = rel_bias[:, :, idx3].transpose(0, 2, 1, 3).reshape(NL, 128, NH * TS)
    k4 = np.arange(TS)[:, None]
    idx4 = np.clip(c2 - k4, -128, 128) + 128
    nbs4 = rel_bias[:, :, idx4].transpose(0, 2, 1, 3).reshape(NL, TS, NH * TS)
    return (np.ascontiguousarray(par), np.ascontiguousarray(nbp, np.float32), np.ascontiguousarray(nbs3, np.float32),
            np.ascontiguousarray(nbs4, np.float32))


def const_inputs():
    ident = np.eye(128, dtype=np.float32)
    s = np.arange(128)[:, None]
    t = np.arange(128)[None, :]
    negmask = np.where(s <= t, 0.0, -1e30).astype(np.float32)
    return ident, negmask


_T = 256
_NSS = 3


def make_in_maps(inp, NLC, T, TS, n_pairs, NT):
    f = lambda a: np.ascontiguousarray(np.asarray(a), dtype=np.float32)
    g = {k: f(v) for k, v in inp.items()}
    L = NT * T
    SB = g["x_sample"].shape[0]
    ident, negmask = const_inputs()
    maps = []
    for c in range(2 * n_pairs):
        s, st = c // 2, c % 2
        ls = slice(st * NLC, (st + 1) * NLC)
        par, nbp, nbs3, nbs4 = host_params(g["norm_mix_g"][ls], g["conv_w"][ls], g["q_norm_g"][ls], g["k_norm_g"][ls],
                                           g["rel_bias"][ls], g["b_igate"][ls], g["b_fgate"][ls],
                                           g["mlstm_norm_g"][ls], g["norm_mlp_g"][ls], TS)
        sidx = [min(max(2 * s + k - st, 2 * s), 2 * s + 1) % SB for k in range(_NSS)]
        role = np.zeros((128, 2), np.float32)
        role[:, st] = 1.0
        maps.append({
            "xp": g["x_prompt"][s % g["x_prompt"].shape[0], :L],
            "xs": np.ascontiguousarray(g["x_sample"][sidx]),
            "ck": np.ascontiguousarray(g["cache_att_k"][ls][:, sidx].transpose(1, 0, 2, 3, 4).reshape(_NSS, NLC, 512, 1024)),
            "cv": np.ascontiguousarray(g["cache_att_v"][ls][:, sidx].transpose(1, 0, 2, 3, 4).reshape(_NSS, NLC, 512, 1024)),
            "sconv": np.ascontiguousarray(g["state_conv"][ls][:, sidx].transpose(1, 0, 2, 3)),
            "smc": np.ascontiguousarray(g["state_mlstm_c"][ls][:, sidx].transpose(1, 0, 2, 3, 4)),
            "smn": np.ascontiguousarray(g["state_mlstm_n"][ls][:, sidx].transpose(1, 0, 2, 3)),
            "smm": np.ascontiguousarray(g["state_mlstm_m"][ls][:, sidx].transpose(1, 0, 2)),
            "role": role,
            "win": np.ascontiguousarray(g["w_in"][ls]), "wout": np.ascontiguousarray(g["w_out"][ls]),
            "wup": np.ascontiguousarray(g["w_up"][ls]), "wdn": np.ascontiguousarray(g["w_down"][ls]),
            "par": par, "nbp": nbp, "nbs3": nbs3, "nbs4": nbs4, "ident": ident, "negmask": negmask,
        })
    return maps


def assemble(R, NLC, T, TS, n_pairs, NT, SB):
    yp, ys = [], [None] * SB
    P = {k: [[None] * n_pairs for _ in range(2 * NLC)] for k in ("conv", "k", "v", "c", "n", "m")}
    S = {k: [[None] * SB for _ in range(2 * NLC)] for k in ("conv", "k", "v", "c", "n", "m")}
    for s in range(n_pairs):
        A, B = R[2 * s], R[2 * s + 1]
        yp.append(B["yp"][T:(NT + 1) * T])
        for j in range(2):
            if 2 * s + j < SB:
                ys[2 * s + j] = B["ys"][(j + 1) * TS:(j + 2) * TS]
        for st, C in ((0, A), (1, B)):
            for l in range(NLC):
                gl = st * NLC + l
                P["conv"][gl][s] = C["o_pconv"][st, l]
                P["c"][gl][s] = C["o_pc"][st, l]
                P["n"][gl][s] = C["o_pn"][st, l]
                P["m"][gl][s] = C["o_pm"][st, l]
                nsl = min(2, NT)
                P["k"][gl][s] = np.concatenate([C["o_pk"][st + i + (2 - nsl), l] for i in range(nsl)], axis=0)
                P["v"][gl][s] = np.concatenate([C["o_pv"][st + i + (2 - nsl), l] for i in range(nsl)], axis=0)
                for j in range(2):
                    if 2 * s + j < SB:
                        S["conv"][gl][2 * s + j] = C["o_sconv"][j + st, l]
                        S["k"][gl][2 * s + j] = C["o_sk"][j + st, l]
                        S["v"][gl][2 * s + j] = C["o_sv"][j + st, l]
                        S["c"][gl][2 * s + j] = C["o_sc"][j + st, l]
                        S["n"][gl][2 * s + j] = C["o_sn"][j + st, l]
                        S["m"][gl][2 * s + j] = C["o_sm"][j + st, l]
    st_ = lambda d: np.stack([np.stack(x) for x in d])
    KEEP = min(512, NT * T)
    NLT = 2 * NLC
    outs = (np.stack(yp), np.stack(ys), st_(P["conv"]),
            st_(P["k"]).reshape(NLT, n_pairs, KEEP, NH, HD), st_(P["v"]).reshape(NLT, n_pairs, KEEP, NH, HD),
            st_(P["c"]), st_(P["n"]), st_(P["m"]), st_(S["conv"]),
            st_(S["k"]).reshape(NLT, SB, TS, NH, HD), st_(S["v"]).reshape(NLT, SB, TS, NH, HD),
            st_(S["c"]), st_(S["n"]), st_(S["m"]))
    return tuple(np.ascontiguousarray(o, dtype=np.float32) for o in outs)


def kernel(x_prompt, x_sample, cache_att_k, cache_att_v, state_conv, state_mlstm_c, state_mlstm_n, state_mlstm_m,
           norm_mix_g, w_in, conv_w, q_norm_g, k_norm_g, rel_bias, b_igate, b_fgate, mlstm_norm_g, w_out,
           norm_mlp_g, w_up, w_down):
    inp = dict(x_prompt=x_prompt, x_sample=x_sample, cache_att_k=cache_att_k, cache_att_v=cache_att_v,
               state_conv=state_conv, state_mlstm_c=state_mlstm_c, state_mlstm_n=state_mlstm_n,
               state_mlstm_m=state_mlstm_m, norm_mix_g=norm_mix_g, w_in=w_in, conv_w=conv_w, q_norm_g=q_norm_g,
               k_norm_g=k_norm_g, rel_bias=rel_bias, b_igate=b_igate, b_fgate=b_fgate, mlstm_norm_g=mlstm_norm_g,
               w_out=w_out, norm_mlp_g=norm_mlp_g, w_up=w_up, w_down=w_down)
    NLAY = np.asarray(w_in).shape[0]
    B, L, _ = np.asarray(x_prompt).shape
    SB, TS, _ = np.asarray(x_sample).shape
    NLC = NLAY // 2
    NT = L // _T
    nc = build(NLC, NT, _T, TS, _NSS)
    in_maps = make_in_maps(inp, NLC, _T, TS, B, NT)
    res = run_bass_kernel_spmd(nc, in_maps, core_ids=list(range(2 * B)))
    return assemble(res.results, NLC, _T, TS, B, NT, SB)
```

```python
import contextlib
import numpy as np
import concourse.bass as bass
import concourse.mybir as mybir
from concourse.bass_utils import run_bass_kernel_spmd

F32 = mybir.dt.float32
BF16 = mybir.dt.bfloat16
ALU = mybir.AluOpType
AF = mybir.ActivationFunctionType

D = 2048
NCH = 16
HD = 128
NH = 8
MH = 4
DFF = 8192
IN_DIM = 6664
C_XA, C_GB, C_GC, C_Q, C_K, C_V, C_MQ, C_MK, C_MV, C_MO, C_IG, C_FG = (
    0, 512, 1024, 1536, 2560, 3584, 4608, 5120, 5632, 6144, 6656, 6660)
EPS = 1e-6
SCALE = 128 ** -0.5
NPAR = 66
UNITS_PER_LAYER = 104


class Buf:
    __slots__ = ("w", "r", "name")

    def __init__(self, name=""):
        self.w = {}
        self.r = {}
        self.name = name


class Prog:
    ENGS = ("pe", "act", "dve", "pool", "sp")

    def __init__(self):
        self.ops = []
        self.dma_count = {}
        self.dma_last = {}
        self.dma_inc = {}
        self.fold_waits = True

    def key(self, i):
        o = self.ops[i]
        return ("d", o[3]) if o[3] is not None else o[0]

    def op(self, eng, fn, rd=(), wr=(), dma=None, wa=(), dma_inc=16):
        i = len(self.ops)
        raw = set()
        deps = {}

        def need(j):
            k = self.key(j)
            if deps.get(k, -1) < j:
                deps[k] = j

        for b in rd:
            for j in b.w.values():
                raw.add(j)
                need(j)
        for b in wr:
            for j in b.w.values():
                need(j)
            for j in b.r.values():
                need(j)
        for b in wa:
            for j in b.r.values():
                need(j)
        if dma is None:
            if eng in deps and deps[eng] not in raw:
                cand = [j for j in raw if self.key(j) == eng]
                if cand:
                    deps[eng] = max(cand)
                else:
                    del deps[eng]
            mykey = eng
            ordn = None
        else:
            mykey = ("d", dma)
            if dma in self.dma_last:
                need(self.dma_last[dma])
            self.dma_last[dma] = i
            self.dma_inc[dma] = dma_inc
            self.dma_count[dma] = self.dma_count.get(dma, 0) + 1
            ordn = self.dma_count[dma]
        self.ops.append([eng, fn, sorted(deps.values()), dma, ordn, False, 0])
        for b in rd:
            b.r[mykey] = i
        for b in wr:
            b.w = {mykey: i}
            b.r = {}
        for b in wa:
            b.w[mykey] = i
        return i

    def emit(self, nc, stack):
        ops = self.ops
        sems = {e: stack.enter_context(nc.semaphore("s_" + e)) for e in self.ENGS}
        dsems = {d: stack.enter_context(nc.semaphore("d_%d" % d)) for d in sorted(self.dma_count)}
        waited = {e: {} for e in self.ENGS}
        waits = []
        for i, o in enumerate(ops):
            e = o[0]
            wl = []
            for j in o[2]:
                k = self.key(j)
                if waited[e].get(k, -1) >= j:
                    continue
                waited[e][k] = j
                wl.append(j)
                if ops[j][3] is None:
                    ops[j][5] = True
            waits.append(wl)
        cnt = {e: 0 for e in self.ENGS}
        for o in ops:
            if o[3] is None and o[5]:
                cnt[o[0]] += 1
                o[6] = cnt[o[0]]
        per = {e: [] for e in self.ENGS}
        for i, o in enumerate(ops):
            per[o[0]].append(i)

        def run(engname, eng):
            for i in per[engname]:
                o = ops[i]
                wl = [(dsems[ops[j][3]], self.dma_inc[ops[j][3]] * ops[j][4]) if ops[j][3] is not None
                      else (sems[ops[j][0]], ops[j][6]) for j in waits[i]]
                fold = None
                if self.fold_waits and wl and o[3] is None:
                    fold = wl.pop()
                for sm_, v_ in wl:
                    eng.wait_ge(sm_, v_)
                ins = o[1](eng)
                if fold is not None:
                    ins._wait_ge(fold[0], fold[1])
                if o[3] is not None:
                    if self.dma_inc[o[3]] == 1:
                        ins.then_inc(dsems[o[3]])
                    else:
                        ins.then_inc(dsems[o[3]], 16)
                elif o[5]:
                    ins.then_inc(sems[engname], 1)

        with nc.Block() as block:
            @block.tensor
            def _(e):
                run("pe", e)

            @block.scalar
            def _(e):
                run("act", e)

            @block.vector
            def _(e):
                run("dve", e)

            @block.gpsimd
            def _(e):
                run("pool", e)

            @block.sync
            def _(e):
                run("sp", e)


def unit_table():
    u = []
    for h in range(MH):
        u.append(("gate", h))
    for base in (C_MQ, C_MK, C_MO):
        for j in range(2):
            u.append(("fm", "win", 0, base + j * 256, base + j * 256 + 128))
    for base in (C_MK, C_MV):
        for j in range(2):
            u.append(("tm", base + j * 256))
    for c in range(4):
        u.append(("fm", "win", 0, C_GC + c * 128, C_XA + c * 128))
    for j in range(2):
        u.append(("fm", "win", 0, C_GB + j * 256, C_GB + j * 256 + 128))
    for base in (C_Q, C_K):
        for j in range(4):
            u.append(("fm", "win", 0, base + j * 256, base + j * 256 + 128))
    for j in range(4):
        u.append(("tm", C_V + j * 256))
    for j in range(8):
        u.append(("fm", "wout", 0, j * 256, j * 256 + 128))
    for q in range(4):
        for j in range(8):
            u.append(("fm", "wup", 0, q * 2048 + j * 256, q * 2048 + j * 256 + 128))
        for j in range(8):
            u.append(("fm", "wdn", q * 2048, j * 256, j * 256 + 128))
    assert len(u) == UNITS_PER_LAYER
    return u


UNITS = unit_table()


def build(NL, NT, T, TS=32, NSS=3):
    NTOK = NT * T
    NB = T // 128
    NQ = T // 64
    nc = bass.Bass("TRN2", target_bir_lowering=False)
    P = Prog()

    def din(name, shape, dt=F32):
        return nc.dram_tensor(name, list(shape), dt, kind="ExternalInput").ap()

    def dout(name, shape, dt=F32):
        return nc.dram_tensor(name, list(shape), dt, kind="ExternalOutput").ap()

    xp = din("xp", [NTOK, D])
    xs = din("xs", [NSS, TS, D])
    ck = din("ck", [NSS, NL, 512, 1024])
    cv = din("cv", [NSS, NL, 512, 1024])
    sconv = din("sconv", [NSS, NL, 2, 512])
    smc = din("smc", [NSS, NL, MH, 128, 128])
    smn = din("smn", [NSS, NL, MH, 128])
    smm = din("smm", [NSS, NL, MH])
    role_d = din("role", [128, 2])
    win = din("win", [NL, D, IN_DIM])
    wout = din("wout", [NL, D, D])
    wup = din("wup", [NL, D, DFF])
    wdn = din("wdn", [NL, DFF, D])
    par_d = din("par", [NL, 128, NPAR])
    nbp_d = din("nbp", [NL, 128, NH * 256])
    nbs3_d = din("nbs3", [NL, 128, NH * TS])
    nbs4_d = din("nbs4", [NL, TS, NH * TS])
    ident_d = din("ident", [128, 128])
    negmask_d = din("negmask", [128, 128])
    WM = {"win": win, "wout": wout, "wup": wup, "wdn": wdn}

    yp = dout("yp", [(NT + 1) * T, D])
    ys = dout("ys", [NSS * TS, D])
    o_pconv = dout("o_pconv", [2, NL, 2, 512])
    o_pk = dout("o_pk", [3, NL, T, 1024])
    o_pv = dout("o_pv", [3, NL, T, 1024])
    o_pc = dout("o_pc", [2, NL, MH, 128, 128])
    o_pn = dout("o_pn", [2, NL, MH, 128])
    o_pm = dout("o_pm", [2, NL, MH])
    o_sconv = dout("o_sconv", [NSS, NL, 2, 512])
    o_sk = dout("o_sk", [NSS, NL, TS, 1024])
    o_sv = dout("o_sv", [NSS, NL, TS, 1024])
    o_sc = dout("o_sc", [NSS, NL, MH, 128, 128])
    o_sn = dout("o_sn", [NSS, NL, MH, 128])
    o_sm = dout("o_sm", [NSS, NL, MH])

    bounce_p = nc.dram_tensor("bounce_p", [T, D], F32).ap()
    gath_p = nc.dram_tensor("gath_p", [2 * T, D], F32).ap()
    bounce_s = nc.dram_tensor("bounce_s", [TS, D], F32).ap()
    gath_s = nc.dram_tensor("gath_s", [2 * TS, D], F32).ap()
    GROUPS = [[0, 1], [2, 3], [4, 5], [6, 7]]
    wscr_l = [nc.dram_tensor("wscr%d" % l, [UNITS_PER_LAYER, 128, 4096], BF16).ap() for l in range(NL)]
    nbscr = nc.dram_tensor("nbscr", [NL, 128, NH * 256], BF16).ap()
    nbs3scr = nc.dram_tensor("nbs3scr", [NL, 128, NH * TS], BF16).ap()
    nbs4scr = nc.dram_tensor("nbs4scr", [NL, TS, NH * TS], BF16).ap()

    stack = contextlib.ExitStack()
    with stack:
        def sb(name, shape, dt=F32):
            return stack.enter_context(nc.sbuf_tensor("sb_" + name, list(shape), dt))

        banks = [stack.enter_context(nc.psum_tensor("bank%d" % i, [128, 512], F32)) for i in range(8)]
        bank_bufs = [Buf("bank%d" % i) for i in range(8)]
        pool_ctr = {"A": 0, "B": 0}

        def psA():
            i = pool_ctr["A"] % 4
            pool_ctr["A"] += 1
            return banks[i], bank_bufs[i]

        def psB():
            i = 4 + pool_ctr["B"] % 4
            pool_ctr["B"] += 1
            return banks[i], bank_bufs[i]

        xT = sb("xT", [128, NCH, T])
        hT = sb("hT", [128, NCH, T], BF16)
        big = sb("big", [128, NCH, T], BF16)
        NWS = 6
        wb = [sb("wb%d" % i, [128, 4096], BF16) for i in range(NWS)]
        wb_buf = [Buf("wb%d" % i) for i in range(NWS)]
        stage = sb("stage", [128, 2048])
        stA, stB = Buf("stA"), Buf("stB")
        stage2 = sb("stage2", [128, 2048])
        st2 = Buf("st2")
        role = sb("role", [128, 2])
        kst = [sb("kst%d" % l, [128, NH, 512], BF16) for l in range(NL)]
        vst = [sb("vst%d" % l, [128, 4, 1024], BF16) for l in range(NL)]
        kst_b = [Buf() for _ in range(NL)]
        vst_b = [Buf() for _ in range(NL)]
        kcur = sb("kcur", [128, NH, T], BF16)
        vcur = sb("vcur", [128, NB, 1024], BF16)
        qT = sb("qT", [128, NH, T], BF16)
        kcur_b = [Buf() for _ in range(NH)]
        qT_b = [Buf() for _ in range(NH)]
        vcur_b = Buf()
        nbt = sb("nbt", [128, NH, 256], BF16)
        nbt3 = nbt[:, :, 0:TS]
        nbt4 = nbt[:, :, TS:2 * TS]
        nbt_b = Buf()
        g1 = sb("g1", [128, MH, T])
        g2 = sb("g2", [128, MH, T])
        g3 = sb("g3", [128, MH, T])
        g1_b = [Buf() for _ in range(MH)]
        g2_b = [Buf() for _ in range(MH)]
        g3_b = [Buf() for _ in range(MH)]
        mqT = sb("mqT", [128, MH, T], BF16)
        mkT = sb("mkT", [128, MH, T], BF16)
        sigo = sb("sigo", [128, MH, T], BF16)
        mqT_b = [Buf() for _ in range(MH)]
        mkT_b = [Buf() for _ in range(MH)]
        sigo_b = [Buf() for _ in range(MH)]
        mktok = sb("mktok", [128, NB, 512], BF16)
        mvtok = sb("mvtok", [128, NB, 512], BF16)
        mktok_b, mvtok_b = Buf(), Buf()
        uext = [sb("uext%d" % i, [128, T + 2]) for i in range(2)]
        uext_b = [Buf() for _ in range(2)]
        utmp = [sb("utmp%d" % i, [128, T]) for i in range(2)]
        utmp_b = [Buf() for _ in range(2)]
        rstd = [sb("rstd%d" % i, [128, T]) for i in range(2)]
        rstd_b = [Buf() for _ in range(2)]
        sqt = [sb("sqt%d" % i, [128, T], BF16) for i in range(3)]
        sqt_b = [Buf() for _ in range(3)]
        NE = 5
        Eb = [sb("E%d" % i, [128, T], BF16) for i in range(NE)]
        Eb_b = [Buf() for _ in range(NE)]
        par = sb("par", [128, NL, NPAR])
        negbf = sb("negbf", [128, NL, MH])
        negch = sb("negch", [128, NL, NH])
        ident = sb("ident", [128, 128])
        identb = sb("identb", [128, 128], BF16)
        negmask = sb("negmask", [128, 128])
        ones_bf = sb("ones_bf", [128, 128], BF16)
        ones_f = sb("ones_f", [128, T])
        Cst = [sb("Cst%d" % l, [128, MH, 128]) for l in range(NL)]
        Cbf1 = sb("Cbf1", [128, MH, 128], BF16)
        nbf1 = sb("nbf1", [128, MH, 128], BF16)
        Cbf = [Cbf1 for l in range(NL)]
        cbf_b = [Buf() for _ in range(MH)]
        nst = [sb("nst%d" % l, [128, MH, 128]) for l in range(NL)]
        nbf = [nbf1 for l in range(NL)]
        carry = [sb("carry%d" % l, [128, 2 * MH]) for l in range(NL)]
        Cst_b = [[Buf() for _ in range(MH)] for _ in range(NL)]
        carry_b = [[Buf() for _ in range(MH)] for _ in range(NL)]
        cst = [sb("cst%d" % l, [128, 4, 2]) for l in range(NL)]
        cst_b = [[Buf() for _ in range(4)] for _ in range(NL)]
        xT_b = [Buf("xT%d" % c) for c in range(NCH)]
        hT_b = [Buf("hT%d" % c) for c in range(NCH)]
        big_b = [Buf("big%d" % c) for c in range(NCH)]
        NR = 4
        acol = [sb("acol%d" % i, [128, 2]) for i in range(NR)]
        Wt = [sb("Wt%d" % i, [128, 128]) for i in range(NR)]
        PT = [sb("PT%d" % i, [128, 128], BF16) for i in range(NR)]
        eint = [sb("eint%d" % i, [128, 128]) for i in range(NR)]
        qtil = [sb("qtil%d" % i, [128, 128], BF16) for i in range(NR)]
        wcol = [sb("wcol%d" % i, [128, 2]) for i in range(NR)]
        kw = [sb("kw%d" % i, [128, 128], BF16) for i in range(NR)]
        tmp_b = [[Buf() for _ in range(8)] for _ in range(NR)]
        ddt = [sb("ddt%d" % i, [128, MH, 128]) for i in range(1)]
        ddt_b = [Buf() for _ in range(1)]
        wgf = sb("wgf", [128, NCH, 8])
        wgf_b = Buf()
        mout = sb("mout", [128, 2 * MH])
        mout_b = Buf()

        dma_ids = {"n": 0}

        def new_dsem():
            dma_ids["n"] += 1
            return dma_ids["n"] - 1

        wsem = [new_dsem() for _ in range(NWS)]
        psem = [new_dsem() for _ in range(8)]
        msem = [new_dsem() for _ in range(6)]
        ctr = {"p": 0, "m": 0, "w": 0, "rot": 0, "sq": 0, "rs": 0, "E": 0, "u": 0, "t": 0, "dd": 0}

        def pdma():
            ctr["p"] += 1
            return psem[ctr["p"] % 8]

        def mdma():
            ctr["m"] += 1
            return msem[ctr["m"] % 6]

        def MM(out, lhsT, rhs, start, stop, rd, wr):
            P.op("pe", lambda e: e.matmul(out, lhsT, rhs, start=start, stop=stop), rd, wr)

        def TR(out, in_, idn, rd, wr):
            P.op("pe", lambda e: e.transpose(out, in_, idn), rd, wr)

        def ACT(out, in_, func, rd, wr, bias=None, scale=1.0):
            if bias is None:
                P.op("act", lambda e: e.activation(out, in_, func, scale=scale), rd, wr)
            else:
                P.op("act", lambda e: e.activation(out, in_, func, bias=bias, scale=scale), rd, wr)

        def TT(out, a, b, op, rd, wr, eng="dve"):
            P.op(eng, lambda e: e.tensor_tensor(out, a, b, op), rd, wr)

        def STT(out, in0, scalar, in1, op0, op1, rd, wr):
            P.op("dve", lambda e: e.scalar_tensor_tensor(out, in0, scalar, in1, op0, op1), rd, wr)

        def TS_(out, in0, s1, s2, op0, op1, rd, wr, eng="dve"):
            P.op(eng, lambda e: e.tensor_scalar(out, in0, s1, s2, op0, op1), rd, wr)

        def CP(out, in_, rd, wr, eng="dve"):
            P.op(eng, lambda e: e.tensor_copy(out, in_), rd, wr)

        def RCP(out, in_, rd, wr):
            P.op("dve", lambda e: e.reciprocal(out, in_), rd, wr)

        def SCAN(out, d0, d1, init, op0, op1, rd, wr):
            P.op("dve", lambda e: e.tensor_tensor_scan(out, d0, d1, init, op0, op1), rd, wr)

        def MSET(ap, val, wr, eng="pool"):
            P.op(eng, lambda e: e.memset(ap, val), (), wr)

        def DMA(eng, out, in_, rd, wr, sem, nonc=False):
            if nonc:
                def f(e):
                    with nc.allow_non_contiguous_dma(reason="small strided transfer"):
                        return e.dma_start(out=out, in_=in_)
            else:
                def f(e):
                    return e.dma_start(out=out, in_=in_)
            P.op(eng, f, rd, wr, dma=sem)

        marks = {}
        ALLB = Buf("init")
        DMA("pool", ident[:], ident_d, (), [ALLB], mdma())
        DMA("pool", negmask[:], negmask_d, (), [ALLB], mdma())
        DMA("pool", role[:], role_d, (), [ALLB], mdma())
        DMA("pool", par[:], par_d.rearrange("l p n -> p l n"), (), [ALLB], mdma(), nonc=True)
        MSET(ones_bf[:], 1.0, [ALLB])
        MSET(ones_f[:], 1.0, [ALLB])
        CP(identb[:], ident[:], [ALLB], [ALLB], eng="pool")
        for l in range(NL):
            TS_(negbf[:, l, :], par[:, l, 54:58], -1.0, None, ALU.mult, ALU.bypass, [ALLB], [ALLB], eng="pool")
            TS_(negch[:, l, :], par[:, l, 58:66], -1.0, None, ALU.mult, ALU.bypass, [ALLB], [ALLB], eng="pool")

        def pr(l, a, b=None):
            return par[:, l, a:(a + 1 if b is None else b)]

        nbscr_b = [Buf() for _ in range(NL)]
        for l in range(NL):
            DMA("pool", stage[:, 0:NH * 256], nbp_d[l], [ALLB], [stA, stB], mdma())
            for h in range(NH):
                ACT(nbt[:, h, :], stage[:, h * 256:(h + 1) * 256], AF.Exp, [stA, stB, ALLB], [nbt_b],
                    bias=negch[:, l, h:h + 1])
            for h in range(NH):
                MSET(nbt[64:128, h, 0:64], 0.0, [nbt_b])
            DMA("pool", nbscr[l].rearrange("p (h n) -> p h n", h=NH), nbt[:], [nbt_b], [nbscr_b[l]], mdma())
            DMA("pool", stage[:, 0:NH * TS], nbs3_d[l], [nbt_b], [stA, stB], mdma())
            for h in range(NH):
                ACT(nbt3[:, h, :], stage[:, h * TS:(h + 1) * TS], AF.Exp, [stA, stB], [nbt_b],
                    bias=negch[:, l, h:h + 1])
            DMA("pool", nbs3scr[l].rearrange("p (h n) -> p h n", h=NH), nbt3, [nbt_b], [nbscr_b[l]], mdma())
            DMA("pool", stage[0:TS, 0:NH * TS], nbs4_d[l], [nbt_b], [stA, stB], mdma())
            for h in range(NH):
                ACT(nbt4[0:TS, h, :], stage[0:TS, h * TS:(h + 1) * TS], AF.Exp, [stA, stB], [nbt_b],
                    bias=negch[0:TS, l, h:h + 1])
            DMA("pool", nbs4scr[l].rearrange("p (h n) -> p h n", h=NH), nbt4[0:TS], [nbt_b], [nbscr_b[l]], mdma())

        marks['init'] = len(P.ops)
        wscr_b = [[Buf() for _ in range(UNITS_PER_LAYER)] for _ in range(NL)]

        def DMAw(out, in_, b, sem):
            P.op("pool", lambda e: e.dma_start(out=out, in_=in_), (), (), dma=sem, wa=[b])

        for l in range(NL):
            DMA("pool", wgf[:], win[l][:, C_IG:C_IG + 8].rearrange("(kc p) g -> p kc g", p=128),
                [wgf_b], [wgf_b], mdma(), nonc=True)
            for ui, u in enumerate(UNITS):
                dst = wscr_l[l][ui]
                if u[0] == "gate":
                    h = u[1]
                    rb = ui % 2
                    wv = wb[rb][:, :].rearrange("p (j kc m) -> p j kc m", j=2, kc=NCH)
                    for j, col in enumerate((h, 4 + h)):
                        CP(wv[:, j, :, :], wgf[:, :, col:col + 1].to_broadcast([128, NCH, 128]),
                           [wgf_b], [wb_buf[rb]], eng="pool")
                    DMA("pool", dst, wb[rb][:, :], [wb_buf[rb]], [wscr_b[l][ui]], pdma())
                elif u[0] == "fm":
                    W = WM[u[1]][l]
                    r0 = u[2]
                    for j, col in enumerate(u[3:5]):
                        src = W[r0:r0 + 2048, col:col + 128].rearrange("(kc p) m -> p kc m", p=128)
                        DMAw(dst[:, j * 2048:(j + 1) * 2048].rearrange("p (kc m) -> p kc m", m=128), src,
                             wscr_b[l][ui], pdma())
                else:
                    col = u[1]
                    src = win[l][:, col:col + 256].rearrange("(kc p) n -> p kc n", p=128)
                    DMAw(dst.rearrange("p (kc n) -> p kc n", n=256), src, wscr_b[l][ui], pdma())

        marks['prepass'] = len(P.ops)
        def wload(l, ui):
            s = ctr["w"] % NWS
            ctr["w"] += 1
            DMA("sp", wb[s][:], wscr_l[l][ui], [wscr_b[l][ui]], [wb_buf[s]], wsem[s])
            return s

        def fm_unit(l, ui, rhs, rbufs, N, evac):
            s = wload(l, ui)
            for j in range(2):
                ps, pb = psA()
                for kc in range(NCH):
                    MM(ps[:, :N], wb[s][:, (j * NCH + kc) * 128:(j * NCH + kc + 1) * 128], rhs(kc),
                       kc == 0, kc == NCH - 1, [wb_buf[s]] + rbufs(kc), [pb])
                evac(j, ps, pb)

        def tm_unit(l, ui, N, evac):
            s = wload(l, ui)
            bs = min(128, N)
            for blk in range(N // bs):
                ps, pb = psA()
                for kc in range(NCH):
                    MM(ps[:bs, :256], hT[:, kc, blk * bs:(blk + 1) * bs], wb[s][:, kc * 256:(kc + 1) * 256],
                       kc == 0, kc == NCH - 1, [wb_buf[s], hT_b[kc]], [pb])
                evac(blk, ps, pb)

        def rot(name, n):
            i = ctr[name] % n
            ctr[name] += 1
            return i

        def rmsnorm_hT(l, goff, N, pre=None):
            if pre is None:
                ps, pb = psA()
                for c in range(NCH):
                    i = rot("sq", 3)
                    ACT(sqt[i][:, :N], xT[:, c, :N], AF.Square, [xT_b[c]], [sqt_b[i]])
                    MM(ps[:, :N], ones_bf[:], sqt[i][:, :N], c == 0, c == NCH - 1, [sqt_b[i]], [pb])
            else:
                ps, pb = pre
            r = rot("rs", 2)
            ACT(rstd[r][:, :N], ps[:, :N], AF.Sqrt, [pb], [rstd_b[r]], bias=epsc[:, 0:1], scale=1.0 / D)
            RCP(rstd[r][:, :N], rstd[r][:, :N], [rstd_b[r]], [rstd_b[r]])
            for c in range(NCH):
                STT(hT[:, c, :N], xT[:, c, :N], pr(l, goff + c), rstd[r][:, :N], ALU.mult, ALU.mult,
                    [xT_b[c], rstd_b[r]], [hT_b[c]])

        epsc = sb("epsc", [128, 2])
        MSET(epsc[:, 0:1], EPS, [ALLB])
        MSET(epsc[:, 1:2], 1.0, [ALLB])
        m0s = sb("m0s", [128, MH])
        m0s_b = [Buf() for _ in range(MH)]
        ncol = sb("ncol", [128, MH])
        ncol_b = Buf()

        def headnorm(ps, pb, N, gcol, out_ap, out_rd, out_wr, also_f32=None):
            i = rot("sq", 3)
            ACT(sqt[i][:, :N], ps[:, :N], AF.Square, [pb], [sqt_b[i]])
            ps2, pb2 = psA()
            MM(ps2[:, :N], ones_bf[:], sqt[i][:, :N], True, True, [sqt_b[i]], [pb2])
            r = rot("rs", 2)
            ACT(rstd[r][:, :N], ps2[:, :N], AF.Sqrt, [pb2], [rstd_b[r]], bias=epsc[:, 0:1], scale=1.0 / HD)
            RCP(rstd[r][:, :N], rstd[r][:, :N], [rstd_b[r]], [rstd_b[r]])
            STT(out_ap, ps[:, :N], gcol, rstd[r][:, :N], ALU.mult, ALU.mult, [pb, rstd_b[r]] + out_rd, out_wr)
            if also_f32 is not None:
                STT(also_f32[0], ps[:, :N], gcol, rstd[r][:, :N], ALU.mult, ALU.mult, [pb, rstd_b[r]], also_f32[1])

        nxt_pre = [None]

        def layer(l, N, tt, is_sample, emit_kv, kv_row0, okk, ovv, tt_b=None, next_norm=False):
            bs = min(128, N)
            nblk = N // bs
            if is_sample:
                DMA("pool", nbt3, nbs3scr[l].rearrange("p (h n) -> p h n", h=NH), [nbscr_b[l]], [nbt_b], mdma())
                DMA("pool", nbt4[0:TS], nbs4scr[l].rearrange("p (h n) -> p h n", h=NH), [nbscr_b[l]], [nbt_b], mdma())
            else:
                DMA("pool", nbt[:], nbscr[l].rearrange("p (h n) -> p h n", h=NH), [nbscr_b[l]], [nbt_b], mdma())
            rmsnorm_hT(l, 0, N, pre=nxt_pre[0])
            nxt_pre[0] = None
            rh = lambda kc: hT[:, kc, :N]
            rhb = lambda kc: [hT_b[kc]]
            ui = [0]

            def nxt():
                ui[0] += 1
                return ui[0] - 1

            for h in range(MH):
                def ev(j, ps, pb, h=h):
                    if j == 0:
                        ACT(g3[:, h, :N], ps[:, :N], AF.Identity, [pb], [g3_b[h]], bias=pr(l, 50 + h))
                    else:
                        ACT(g1[:, h, :N], ps[:, :N], AF.Exp, [pb], [g1_b[h]], bias=negbf[:, l, h:h + 1], scale=-1.0)
                        ACT(g1[:, h, :N], g1[:, h, :N], AF.Ln, [g1_b[h]], [g1_b[h]], bias=epsc[:, 1:2])
                        SCAN(g2[:, h, :N], ones_f[:, :N], g1[:, h, :N], carry[l][:, h:h + 1], ALU.mult, ALU.add,
                             [g1_b[h], carry_b[l][h]], [g2_b[h]])
                        CP(carry[l][:, h:h + 1], g2[:, h, N - 1:N], [g2_b[h]], [carry_b[l][h]])
                        TT(g3[:, h, :N], g3[:, h, :N], g2[:, h, :N], ALU.add, [g3_b[h], g2_b[h]], [g3_b[h]])
                        CP(m0s[:, h:h + 1], carry[l][:, 4 + h:5 + h], [carry_b[l][h]], [m0s_b[h]])
                        SCAN(g1[:, h, :N], ones_f[:, :N], g3[:, h, :N], m0s[:, h:h + 1], ALU.mult, ALU.max,
                             [g3_b[h], m0s_b[h], g1_b[h]], [g1_b[h]])
                        CP(carry[l][:, 4 + h:5 + h], g1[:, h, N - 1:N], [g1_b[h]], [carry_b[l][h]])
                        TT(g2[:, h, :N], g2[:, h, :N], g1[:, h, :N], ALU.subtract, [g2_b[h], g1_b[h]], [g2_b[h]])
                        ACT(g2[:, h, :N], g2[:, h, :N], AF.Exp, [g2_b[h]], [g2_b[h]])
                fm_unit(l, nxt(), rh, rhb, N, ev)
            for dst, dstb, kind in ((mqT, mqT_b, 0), (mkT, mkT_b, 1), (sigo, sigo_b, 2)):
                for u2 in range(2):
                    def ev(j, ps, pb, u2=u2, dst=dst, dstb=dstb, kind=kind):
                        hh = 2 * u2 + j
                        if kind == 2:
                            ACT(dst[:, hh, :N], ps[:, :N], AF.Sigmoid, [pb], [dstb[hh]])
                        elif kind == 0:
                            ACT(dst[:, hh, :N], ps[:, :N], AF.Copy, [pb], [dstb[hh]])
                        else:
                            CP(dst[:, hh, :N], ps[:, :N], [pb], [dstb[hh]])
                    fm_unit(l, nxt(), rh, rhb, N, ev)
            for dst, dstb in ((mktok, mktok_b), (mvtok, mvtok_b)):
                for u2 in range(2):
                    def ev(blk, ps, pb, u2=u2, dst=dst, dstb=dstb):
                        o = dst[:bs, blk, u2 * 256:(u2 + 1) * 256]
                        i_ = ps[:bs, :256]
                        P.op("dve", lambda e, o=o, i_=i_: e.tensor_copy(o, i_), [pb], (), wa=[dstb])
                    tm_unit(l, nxt(), N, ev)
            cvs = {}
            for c in range(4):
                def ev(j, ps, pb, c=c):
                    if j == 0:
                        i = rot("u", 2)
                        cvs["i"] = i
                        ACT(utmp[i][:, :N], ps[:, :N], AF.Copy, [pb], [utmp_b[i]])
                    else:
                        i = cvs["i"]
                        CP(uext[i][:, 0:2], cst[l][:, c, :], [cst_b[l][c]], [uext_b[i]], eng="pool")
                        TT(uext[i][:, 2:2 + N], utmp[i][:, :N], ps[:, :N], ALU.mult, [utmp_b[i], pb, uext_b[i]],
                           [uext_b[i]])
                        CP(cst[l][:, c, :], uext[i][:, N:N + 2], [uext_b[i]], [cst_b[l][c]], eng="pool")
                        a = stage[:, c * T:c * T + N]
                        ACT(a, uext[i][:, 0:N], AF.Copy, [uext_b[i]], [stA], scale=pr(l, 32 + 3 * c))
                        STT(a, uext[i][:, 1:N + 1], pr(l, 33 + 3 * c), a, ALU.mult, ALU.add, [uext_b[i], stA], [stA])
                        STT(a, uext[i][:, 2:N + 2], pr(l, 34 + 3 * c), a, ALU.mult, ALU.add, [uext_b[i], stA], [stA])
                fm_unit(l, nxt(), rh, rhb, N, ev)
            for u2 in range(2):
                def ev(j, ps, pb, u2=u2):
                    c = 2 * u2 + j
                    TT(big[:, c, :N], ps[:, :N], stage[:, c * T:c * T + N], ALU.mult, [pb, stA], [big_b[c]])
                fm_unit(l, nxt(), rh, rhb, N, ev)
            for u2 in range(4):
                def ev(j, ps, pb, u2=u2):
                    h = 2 * u2 + j
                    headnorm(ps, pb, N, pr(l, 44), qT[:, h, :N], [], [qT_b[h]])
                fm_unit(l, nxt(), rh, rhb, N, ev)
            for u2 in range(4):
                def ev(j, ps, pb, u2=u2):
                    h = 2 * u2 + j
                    if emit_kv:
                        i = rot("u", 2)
                        headnorm(ps, pb, N, pr(l, 45), kcur[:, h, :N], [], [kcur_b[h]],
                                 also_f32=(utmp[i][:, :N], [utmp_b[i]]))
                        for blk in range(nblk):
                            pt, ptb = psA()
                            TR(pt[:bs, 0:128], utmp[i][:, blk * bs:(blk + 1) * bs], ident[:], [utmp_b[i]], [ptb])
                            o = stage[:bs, blk * 1024 + h * 128:blk * 1024 + (h + 1) * 128]
                            i_ = pt[:bs, 0:128]
                            fw = (h == 0 and blk == 0)
                            P.op("act", lambda e, o=o, i_=i_: e.activation(o, i_, AF.Copy), [ptb],
                                 [stA, stB] if fw else (), wa=() if fw else [stA, stB])
                    else:
                        headnorm(ps, pb, N, pr(l, 45), kcur[:, h, :N], [], [kcur_b[h]])
                fm_unit(l, nxt(), rh, rhb, N, ev)
            if emit_kv:
                for blk in range(nblk):
                    DMA("pool", okk[l, kv_row0 + blk * bs:kv_row0 + (blk + 1) * bs, :],
                        stage[:bs, blk * 1024:(blk + 1) * 1024], [stA, stB], [], mdma())
            for u2 in range(4):
                def ev(blk, ps, pb, u2=u2):
                    o = vcur[:bs, blk, u2 * 256:(u2 + 1) * 256]
                    i_ = ps[:bs, :256]
                    P.op("dve", lambda e, o=o, i_=i_: e.tensor_copy(o, i_), [pb], (), wa=[vcur_b])
                    if emit_kv:
                        o2 = stage[:bs, blk * 1024 + u2 * 256:blk * 1024 + (u2 + 1) * 256]
                        fw = (u2 == 0 and blk == 0)
                        P.op("dve", lambda e, o2=o2, i_=i_: e.tensor_copy(o2, i_), [pb],
                             [stA, stB] if fw else (), wa=() if fw else [stA, stB])
                tm_unit(l, nxt(), N, ev)
            if emit_kv:
                for blk in range(nblk):
                    DMA("pool", ovv[l, kv_row0 + blk * bs:kv_row0 + (blk + 1) * bs, :],
                        stage[:bs, blk * 1024:(blk + 1) * 1024], [stA, stB], [], mdma())
            assert ui[0] == 32

            if is_sample:
                ktiles = [("p", jj, 128, 0, N, None, False) for jj in range(4)] + [("c", 0, N, 0, N, None, False)]
            else:
                ktiles = []
                chunk0 = tt * NQ
                for j in range(4 + NB):
                    kc0 = 2 * j - 8
                    if chunk0 + kc0 < 0:
                        continue
                    i0, i1 = max(0, kc0), min(NQ - 1, kc0 + 9)
                    if i0 > i1:
                        continue
                    mk_ = (tt_b is not None) and j < 4 and (tt_b * NQ + kc0 < 0)
                    ktiles.append(("p" if j < 4 else "c", j if j < 4 else j - 4, 128, i0 * 64, (i1 + 1) * 64, kc0, mk_))
            items = [(h, ti) for h in range(NH) for ti in range(len(ktiles))]
            accs = {}
            sinfo = {}
            LA = 3

            def emit_S(idx):
                h, ti = items[idx]
                src, jj, ksz, q0, q1, kc0, mk_ = ktiles[ti]
                nq = q1 - q0
                if src == "p":
                    kap, kb = kst[l][:, h, jj * 128:(jj + 1) * 128], kst_b[l]
                    vap, vb = vst[l][:, jj, h * 128:(h + 1) * 128], vst_b[l]
                else:
                    kap, kb = kcur[:, h, jj * 128:jj * 128 + ksz], kcur_b[h]
                    vap, vb = vcur[:ksz, jj, h * 128:(h + 1) * 128], vcur_b
                ps, pb = psA()
                MM(ps[:ksz, :nq], kap, qT[:, h, q0:q1], True, True, [kb, qT_b[h]], [pb])
                e = rot("E", NE)
                ACT(Eb[e][:ksz, :nq], ps[:ksz, :nq], AF.Exp, [pb], [Eb_b[e]], bias=pr(l, 58 + h)[:ksz],
                    scale=SCALE)
                if is_sample:
                    if src == "p" and jj == 3:
                        TT(Eb[e][:, :nq], Eb[e][:, :nq], nbt3[:, h, :], ALU.mult, [Eb_b[e], nbt_b], [Eb_b[e]])
                    elif src == "c":
                        TT(Eb[e][:ksz, :nq], Eb[e][:ksz, :nq], nbt4[:ksz, h, :], ALU.mult, [Eb_b[e], nbt_b],
                           [Eb_b[e]])
                else:
                    ia, ib = q0 // 64, min(q1 // 64 - 1, kc0 + 3)
                    if ia <= ib:
                        TT(Eb[e][:, ia * 64 - q0:(ib + 1) * 64 - q0], Eb[e][:, ia * 64 - q0:(ib + 1) * 64 - q0],
                           nbt[:, h, (ia - kc0) * 64:(ib - kc0 + 1) * 64], ALU.mult, [Eb_b[e], nbt_b], [Eb_b[e]])
                    if (kc0 + 9) * 64 < q1:
                        c9 = (kc0 + 9) * 64 - q0
                        MSET(Eb[e][0:64, c9:c9 + 64], 0.0, [Eb_b[e]])
                if mk_:
                    TS_(Eb[e][:ksz, :nq], Eb[e][:ksz, :nq], role[:ksz, 0:1], None, ALU.mult, ALU.bypass,
                        [Eb_b[e]], [Eb_b[e]])
                sinfo[idx] = (e, ksz, nq, q0, q1, vap, vb)

            def emit_PV(idx):
                h, ti = items[idx]
                e, ksz, nq, q0, q1, vap, vb = sinfo.pop(idx)
                if ti == 0:
                    accs[h] = psB() + psB()
                pv, pvb, dn, dnb = accs[h]
                first, last = ti == 0, ti == len(ktiles) - 1
                MM(pv[:, q0:q1], vap, Eb[e][:ksz, :nq], first, last, [vb, Eb_b[e]], [pvb])
                MM(dn[:, q0:q1], ones_bf[:ksz, :], Eb[e][:ksz, :nq], first, last, [Eb_b[e]], [dnb])
                if last:
                    r = rot("rs", 2)
                    RCP(rstd[r][:, :N], dn[:, :N], [dnb], [rstd_b[r]])
                    TT(big[:, 4 + h, :N], pv[:, :N], rstd[r][:, :N], ALU.mult, [pvb, rstd_b[r]], [big_b[4 + h]])
                    del accs[h]

            for idx in range(len(items) + LA):
                if idx < len(items):
                    emit_S(idx)
                if idx - LA >= 0:
                    emit_PV(idx - LA)
            if not is_sample:
                if N < 512:
                    for h in range(NH):
                        CP(kst[l][:, h, 0:512 - N], kst[l][:, h, N:512], [kst_b[l]], [kst_b[l]], eng="pool")
                    for jb in range(4 - NB):
                        CP(vst[l][:, jb, :], vst[l][:, jb + NB, :], [vst_b[l]], [vst_b[l]], eng="pool")
                for h in range(NH):
                    CP(kst[l][:, h, 512 - N:512], kcur[:, h, :N], [kcur_b[h], kst_b[l]], [kst_b[l]], eng="pool")
                for jb in range(NB):
                    CP(vst[l][:, 4 - NB + jb, :], vcur[:, jb, :], [vcur_b, vst_b[l]], [vst_b[l]], eng="pool")

            for h in range(MH):
                ACT(Cbf[l][:, h, :], Cst[l][:, h, :], AF.Copy, [Cst_b[l][h]], [cbf_b[h]])
                ACT(nbf[l][:, h, :], nst[l][:, h, :], AF.Copy, [Cst_b[l][h]], [cbf_b[h]])
            hf = stage[:, 1024:1024 + MH * T].rearrange("p (h t) -> p h t", h=MH)
            for r_ in range(nblk):
                c0, c1 = r_ * bs, (r_ + 1) * bs
                nump, numb = psB()
                denp, denb = psB()
                hctx = []
                for h in range(MH):
                    ix = rot("t", NR)
                    tb = tmp_b[ix]
                    M0 = m0s[:, h:h + 1] if r_ == 0 else g1[:, h, c0 - 1:c0]
                    M0b = m0s_b[h] if r_ == 0 else g1_b[h]
                    M1 = g1[:, h, c1 - 1:c1]
                    sb_ = Cst_b[l][h]
                    pt, ptb = psA()
                    TR(pt[:bs, 0:128], g3[:, h, c0:c1], ident[:], [g3_b[h]], [ptb])
                    CP(acol[ix][:bs, 0:1], pt[:bs, 0:1], [ptb], [tb[0]])
                    TT(Wt[ix][:bs, :bs], negmask[:bs, :bs], g1[:bs, h, c0:c1], ALU.subtract, [g1_b[h]], [tb[1]])
                    ACT(Wt[ix][:bs, :bs], Wt[ix][:bs, :bs], AF.Exp, [tb[1], tb[0]], [tb[1]], bias=acol[ix][:bs, 0:1])
                    pss, pssb = psA()
                    MM(pss[:bs, :bs], mkT[:, h, c0:c1], mqT[:, h, c0:c1], True, True, [mkT_b[h], mqT_b[h]], [pssb])
                    STT(PT[ix][:bs, :bs], pss[:bs, :bs], SCALE, Wt[ix][:bs, :bs], ALU.mult, ALU.mult,
                        [pssb, tb[1]], [tb[2]])
                    ACT(eint[ix][:, :bs], g1[:, h, c0:c1], AF.Exp, [g1_b[h], M0b], [tb[3]], bias=M0, scale=-1.0)
                    TT(qtil[ix][:, :bs], mqT[:, h, c0:c1], eint[ix][:, :bs], ALU.mult, [mqT_b[h], tb[3]], [tb[4]])
                    ACT(wcol[ix][:bs, 0:1], M1[:bs], AF.Exp, [g1_b[h], tb[0]], [tb[5]], bias=acol[ix][:bs, 0:1],
                        scale=-1.0)
                    ACT(wcol[ix][:, 1:2], M1, AF.Exp, [g1_b[h], M0b], [tb[6]], bias=M0, scale=-1.0)
                    TS_(kw[ix][:bs, :], mktok[:bs, r_, h * 128:(h + 1) * 128], wcol[ix][:bs, 0:1], SCALE,
                        ALU.mult, ALU.mult, [mktok_b, tb[5]], [tb[7]])
                    hctx.append((ix, tb, sb_))
                for h in range(MH):
                    ix, tb, sb_ = hctx[h]
                    no = nump[:, h * 128:h * 128 + bs]
                    do = denp[:, h * 128:h * 128 + bs]
                    MM(no, Cbf[l][:, h, :], qtil[ix][:, :bs], True, False, [cbf_b[h], tb[4]], [numb])
                    MM(no, mvtok[:bs, r_, h * 128:(h + 1) * 128], PT[ix][:bs, :bs], False, True, [mvtok_b, tb[2]],
                       [numb])
                    MM(do, nbf[l][:, h, :], qtil[ix][:, :bs], True, False, [cbf_b[h], tb[4]], [denb])
                    MM(do, ones_bf[:bs, :], PT[ix][:bs, :bs], False, True, [tb[2]], [denb])
                for h in range(MH):
                    ix, tb, sb_ = hctx[h]
                    pu, pub = psA()
                    MM(pu[:, 0:128], kw[ix][:bs, :], mvtok[:bs, r_, h * 128:(h + 1) * 128], True, True,
                       [tb[7], mvtok_b], [pub])
                    pn, pnb = psA()
                    MM(pn[:, 0:128], kw[ix][:bs, :], ones_bf[:bs, :], True, True, [tb[7]], [pnb])
                    STT(Cst[l][:, h, :], Cst[l][:, h, :], wcol[ix][:, 1:2], pu[:, 0:128], ALU.mult, ALU.add,
                        [pub, tb[6], sb_], [sb_])
                    STT(nst[l][:, h, :], nst[l][:, h, :], wcol[ix][:, 1:2], pn[:, 0:128], ALU.mult, ALU.add,
                        [pnb, tb[6], sb_], [sb_])
                    ACT(Cbf[l][:, h, :], Cst[l][:, h, :], AF.Copy, [sb_], [cbf_b[h]])
                    ACT(nbf[l][:, h, :], nst[l][:, h, :], AF.Copy, [sb_], [cbf_b[h]])
                di = rot("dd", 1)
                nv = nump[:, :].rearrange("p (h n) -> p h n", h=MH)[:, :, :bs]
                dv = denp[:, :].rearrange("p (h n) -> p h n", h=MH)[:, :, :bs]
                ACT(ddt[di][:, :, :bs], dv, AF.Abs, [denb], [ddt_b[di]])
                TT(ddt[di][:, :, :bs], ddt[di][:, :, :bs], g2[:, :, c0:c1], ALU.max, [ddt_b[di]] + g2_b, [ddt_b[di]])
                RCP(ddt[di][:, :, :bs], ddt[di][:, :, :bs], [ddt_b[di]], [ddt_b[di]])
                TT(hf[:, :, c0:c1], nv, ddt[di][:, :, :bs], ALU.mult, [numb, ddt_b[di]], [stB])
            for h in range(MH):
                i = rot("sq", 3)
                ACT(sqt[i][:, :N], hf[:, h, :N], AF.Square, [stB], [sqt_b[i]])
                ps2, pb2 = psA()
                MM(ps2[:, :N], ones_bf[:], sqt[i][:, :N], True, True, [sqt_b[i]], [pb2])
                r = rot("rs", 2)
                ACT(rstd[r][:, :N], ps2[:, :N], AF.Sqrt, [pb2], [rstd_b[r]], bias=epsc[:, 0:1], scale=1.0 / HD)
                RCP(rstd[r][:, :N], rstd[r][:, :N], [rstd_b[r]], [rstd_b[r]])
                iu = rot("u", 2)
                STT(utmp[iu][:, :N], hf[:, h, :N], pr(l, 46 + h), rstd[r][:, :N], ALU.mult, ALU.mult,
                    [stB, rstd_b[r]], [utmp_b[iu]])
                TT(big[:, 12 + h, :N], utmp[iu][:, :N], sigo[:, h, :N], ALU.mult, [utmp_b[iu], sigo_b[h]],
                   [big_b[12 + h]])

            rb_ = lambda kc: big[:, kc, :N]
            rbb = lambda kc: [big_b[kc]]
            def resid_stats():
                st_ps, st_pb = psB()
                pend = []

                def flush():
                    while pend:
                        c_, i_ = pend.pop(0)
                        MM(st_ps[:, :N], ones_bf[:], sqt[i_][:, :N], c_ == 0, c_ == NCH - 1, [sqt_b[i_]], [st_pb])

                def ev(j, ps, pb, u2):
                    c = 2 * u2 + j
                    TT(xT[:, c, :N], xT[:, c, :N], ps[:, :N], ALU.add, [xT_b[c], pb], [xT_b[c]])
                    i = rot("sq", 3)
                    ACT(sqt[i][:, :N], xT[:, c, :N], AF.Square, [xT_b[c]], [sqt_b[i]])
                    flush()
                    pend.append((c, i))
                return ev, flush, (st_ps, st_pb)

            ev_s, flush_s, pre_s = resid_stats()
            for u2 in range(8):
                fm_unit(l, nxt(), rb_, rbb, N, lambda j, ps, pb, u2=u2: ev_s(j, ps, pb, u2))
            flush_s()
            rmsnorm_hT(l, 16, N, pre=pre_s)
            for q in range(4):
                for u2 in range(8):
                    def ev(j, ps, pb, u2=u2):
                        fc = 2 * u2 + j
                        i = rot("u", 2)
                        ACT(utmp[i][:, :N], ps[:, :N], AF.Relu, [pb], [utmp_b[i]])
                        TT(big[:, fc, :N], utmp[i][:, :N], utmp[i][:, :N], ALU.mult, [utmp_b[i]], [big_b[fc]], eng="pool")
                    fm_unit(l, nxt(), rh, rhb, N, ev)
                if q == 3 and next_norm:
                    ev_s, flush_s, pre_s = resid_stats()
                    for u2 in range(8):
                        fm_unit(l, nxt(), rb_, rbb, N, lambda j, ps, pb, u2=u2: ev_s(j, ps, pb, u2))
                    flush_s()
                    nxt_pre[0] = pre_s
                else:
                    for u2 in range(8):
                        def ev(j, ps, pb, u2=u2):
                            c = 2 * u2 + j
                            TT(xT[:, c, :N], xT[:, c, :N], ps[:, :N], ALU.add, [xT_b[c], pb], [xT_b[c]])
                        fm_unit(l, nxt(), rb_, rbb, N, ev)
            assert ui[0] == UNITS_PER_LAYER

        def state_out(l, oc, on, om, ocv):
            for h in range(MH):
                DMA("pool", oc[l, h], Cst[l][:, h, :], [Cst_b[l][h]], [], mdma())
                DMA("pool", on[l, h].rearrange("(p o) -> p o", o=1), nst[l][:, h, 0:1], [Cst_b[l][h]], [], mdma(),
                    nonc=True)
            TT(mout[:, 0:MH], carry[l][:, MH:2 * MH], carry[l][:, 0:MH], ALU.subtract, carry_b[l], [mout_b])
            DMA("pool", om[l:l + 1, :], mout[0:1, 0:MH], [mout_b], [], mdma())
            for c in range(4):
                DMA("pool", ocv[l][:, c * 128:(c + 1) * 128].rearrange("j p -> p j"), cst[l][:, c, :], [cst_b[l][c]], [],
                    mdma(), nonc=True)

        z0 = Buf("zinit")
        MSET(stage2[:, :], 0.0, [st2])
        for blk in range(NB):
            DMA("pool", gath_p[blk * 128:(blk + 1) * 128, :], stage2[:, :], [st2], [], mdma())
        DMA("pool", gath_s[0:TS, :], stage2[0:TS, :], [st2], [], mdma())
        gp_b, gs_b, bp_b, bs_b = Buf("gp"), Buf("gs"), Buf("bp"), Buf("bs")
        for i_ in range(len(P.ops) - NB - 1, len(P.ops) - 1):
            gp_b.w[P.key(i_)] = i_
        gs_b.w[P.key(len(P.ops) - 1)] = len(P.ops) - 1
        csem = new_dsem()

        def exchange(bounce, gath, bb, gb):
            i_ap, o_ap = bounce.opt(), gath.opt()
            P.op("pool", lambda e: e.collective_compute("AllGather", ALU.bypass, replica_groups=GROUPS,
                                                        ins=[i_ap], outs=[o_ap]),
                 [bb], [gb], dma=csem, dma_inc=1)

        def reset_states(l):
            for h in range(MH):
                MSET(Cst[l][:, h, :], 0.0, [Cst_b[l][h]])
                MSET(nst[l][:, h, :], 0.0, [Cst_b[l][h]])
            MSET(carry[l][:], 0.0, carry_b[l])
            MSET(cst[l][:], 0.0, cst_b[l])

        def select_in(npart):
            TS_(stage[:npart, :], stage[:npart, :], role[:npart, 0:1], None, ALU.mult, ALU.bypass, [stA, stB],
                [stA, stB])
            STT(stage[:npart, :], stage2[:npart, :], role[:npart, 1:2], stage[:npart, :], ALU.mult, ALU.add,
                [st2, stA, stB], [stA, stB])

        for k in range(NSS):
            DMA("pool", stage[0:TS, :], xs[k], [stA, stB], [stA, stB], mdma())
            DMA("pool", stage2[0:TS, :], gath_s[0:TS, :], [gs_b, st2], [st2], mdma())
            select_in(TS)
            for g in range(4):
                ps, pb = psA()
                for j in range(4):
                    c = 4 * g + j
                    TR(ps[:, j * TS:(j + 1) * TS], stage[0:TS, c * 128:(c + 1) * 128], ident[0:TS, 0:TS],
                       [stA, stB], [pb])
                CP(xT[:, 4 * g:4 * g + 4, 0:TS], ps[:, 0:4 * TS].rearrange("p (j t) -> p j t", j=4), [pb],
                   xT_b[4 * g:4 * g + 4])
            for l in range(NL):
                DMA("pool", Cst[l][:], smc[k, l].rearrange("h p v -> p h v"), Cst_b[l], Cst_b[l], mdma())
                DMA("pool", ncol[:], smn[k, l].rearrange("h p -> p h"), [ncol_b], [ncol_b], mdma(), nonc=True)
                for h in range(MH):
                    TS_(nst[l][:, h, :], ones_f[:, 0:128], ncol[:, h:h + 1], None, ALU.mult, ALU.bypass,
                        [ncol_b, Cst_b[l][h]], [Cst_b[l][h]])
                MSET(carry[l][:, 0:MH], 0.0, carry_b[l])
                DMA("pool", carry[l][:, MH:2 * MH], smm[k, l].partition_broadcast(128), carry_b[l], carry_b[l],
                    mdma(), nonc=True)
                for c in range(4):
                    DMA("pool", cst[l][:, c, :], sconv[k, l][:, c * 128:(c + 1) * 128].rearrange("j p -> p j"),
                        [cst_b[l][c]], [cst_b[l][c]], mdma(), nonc=True)
                DMA("pool", vst[l][:], cv[k, l].rearrange("(b p) f -> p b f", p=128), [vst_b[l]], [vst_b[l]], mdma())
                bigf = big[:, :, :].rearrange("p c t -> p (c t)")
                for blk in range(4):
                    DMA("pool", bigf[:, blk * 1024:(blk + 1) * 1024],
                        ck[k, l, blk * 128:(blk + 1) * 128, :], big_b, big_b, mdma())
                for h in range(NH):
                    ps, pb = psA()
                    psv = ps[:, :].bitcast(BF16)
                    for blk in range(4):
                        TR(psv[:, blk * 128:(blk + 1) * 128], bigf[:, blk * 1024 + h * 128:blk * 1024 + (h + 1) * 128],
                           identb[:], big_b, [pb])
                    CP(kst[l][:, h, :], psv[:, 0:512], [pb], [kst_b[l]])
                marks['sload%d_%d' % (k, l)] = len(P.ops)
                layer(l, TS, 0, True, True, 0, o_sk[k], o_sv[k], next_norm=False)
                marks['slayer%d_%d' % (k, l)] = len(P.ops)
                state_out(l, o_sc[k], o_sn[k], o_sm[k], o_sconv[k])
                reset_states(l)
            for g in range(4):
                ps, pb = psA()
                for j in range(4):
                    c = 4 * g + j
                    TR(ps[0:TS, j * 128:(j + 1) * 128], xT[:, c, 0:TS], ident[:], [xT_b[c]], [pb])
                o_ = stage[0:TS, g * 512:(g + 1) * 512]
                i_ = ps[0:TS, 0:512]
                P.op("dve", lambda e, o_=o_, i_=i_: e.tensor_copy(o_, i_), [pb],
                     [stA, stB] if g == 0 else (), wa=() if g == 0 else [stA, stB])
            DMA("pool", ys[k * TS:(k + 1) * TS, :], stage[0:TS, :], [stA, stB], [], mdma())
            DMA("pool", bounce_s, stage[0:TS, :], [stA, stB], [bs_b], mdma())
            exchange(bounce_s, gath_s, bs_b, gs_b)

        marks['sample'] = len(P.ops)
        for k in range(NT + 1):
            ta = min(k, NT - 1)
            for blk in range(NB):
                DMA("pool", stage[:, :], xp[ta * T + blk * 128:ta * T + (blk + 1) * 128, :], [stA, stB], [stA, stB],
                    mdma())
                DMA("pool", stage2[:, :], gath_p[blk * 128:(blk + 1) * 128, :], [gp_b, st2], [st2], mdma())
                select_in(128)
                for g in range(4):
                    ps, pb = psA()
                    for j in range(4):
                        c = 4 * g + j
                        TR(ps[:, j * 128:(j + 1) * 128], stage[:, c * 128:(c + 1) * 128], ident[:], [stA, stB], [pb])
                    CP(xT[:, 4 * g:4 * g + 4, blk * 128:(blk + 1) * 128],
                       ps[:, 0:512].rearrange("p (j t) -> p j t", j=4), [pb], xT_b[4 * g:4 * g + 4])
            slot = k - (NT - 2)
            for l in range(NL):
                layer(l, T, k, False, slot >= 0, 0, o_pk[max(slot, 0)], o_pv[max(slot, 0)], tt_b=k - 1,
                      next_norm=(l < NL - 1))
            if k == 0:
                for l in range(NL):
                    for h in range(MH):
                        TS_(Cst[l][:, h, :], Cst[l][:, h, :], role[:, 0:1], None, ALU.mult, ALU.bypass,
                            [Cst_b[l][h]], [Cst_b[l][h]])
                        TS_(nst[l][:, h, :], nst[l][:, h, :], role[:, 0:1], None, ALU.mult, ALU.bypass,
                            [Cst_b[l][h]], [Cst_b[l][h]])
                    TS_(carry[l][:], carry[l][:], role[:, 0:1], None, ALU.mult, ALU.bypass, carry_b[l], carry_b[l])
                    TS_(cst[l][:], cst[l][:], role[:, 0:1], None, ALU.mult, ALU.bypass, cst_b[l], cst_b[l])
            for blk in range(NB):
                sg, sgb = (stage, [stA, stB]) if blk % 2 == 0 else (stage2, [st2])
                for g in range(4):
                    ps, pb = psA()
                    for j in range(4):
                        c = 4 * g + j
                        TR(ps[:, j * 128:(j + 1) * 128], xT[:, c, blk * 128:(blk + 1) * 128], ident[:], [xT_b[c]],
                           [pb])
                    o_ = sg[:, g * 512:(g + 1) * 512]
                    i_ = ps[:, 0:512]
                    P.op("dve", lambda e, o_=o_, i_=i_: e.tensor_copy(o_, i_), [pb],
                         sgb if g == 0 else (), wa=() if g == 0 else sgb)
                DMA("pool", yp[k * T + blk * 128:k * T + (blk + 1) * 128, :], sg[:, :], sgb, [], mdma())
                P.op("pool", (lambda blk, sg: lambda e: e.dma_start(out=bounce_p[blk * 128:(blk + 1) * 128, :],
                                                                    in_=sg[:, :]))(blk, sg),
                     sgb, [bp_b] if blk == 0 else (), dma=mdma(), wa=() if blk == 0 else [bp_b])
            if k < NT:
                exchange(bounce_p, gath_p, bp_b, gp_b)
            if k >= NT - 1:
                v_ = k - (NT - 1)
                for l in range(NL):
                    state_out(l, o_pc[v_], o_pn[v_], o_pm[v_], o_pconv[v_])
        fin = Buf("fin")
        for d in list(P.dma_last):
            pass
        import os
        cut = os.environ.get("KPREFIX")
        if cut:
            ncut = marks[cut] if cut in marks else int(cut)
            del P.ops[ncut:]
        lastd = {}
        for i_, o_ in enumerate(P.ops):
            if o_[3] is not None:
                lastd[o_[3]] = i_
        P.ops.append(["pool", lambda e: e.memset(mout[:, 0:1], 0.0), sorted(lastd.values()), None, None, False, 0])
        print("nops", len(P.ops), {k: v for k, v in marks.items()}, flush=True)
        P.emit(nc, stack)
    return nc


def host_params(norm_mix_g, conv_w, q_norm_g, k_norm_g, rel_bias, b_igate, b_fgate, mlstm_norm_g, norm_mlp_g, TS=32):
    NL = norm_mix_g.shape[0]
    par = np.zeros((NL, 128, NPAR), np.float32)
    par[:, :, 0:16] = norm_mix_g.reshape(NL, 16, 128).transpose(0, 2, 1)
    par[:, :, 16:32] = norm_mlp_g.reshape(NL, 16, 128).transpose(0, 2, 1)
    cw = conv_w.reshape(NL, 3, 4, 128)
    par[:, :, 32:44] = cw.transpose(0, 3, 2, 1).reshape(NL, 128, 12)
    par[:, :, 44] = q_norm_g
    par[:, :, 45] = k_norm_g
    par[:, :, 46:50] = mlstm_norm_g.reshape(NL, 4, 128).transpose(0, 2, 1)
    par[:, :, 50:54] = b_igate[:, None, :]
    par[:, :, 54:58] = b_fgate[:, None, :]
    par[:, :, 58:66] = rel_bias[:, None, :, 256]
    k = np.arange(128)[:, None]
    c = np.arange(256)[None, :]
    idx = np.clip(c - k, -128, 128) + 128
    nbp = rel_bias[:, :, idx].transpose(0, 2, 1, 3).reshape(NL, 128, NH * 256)
    c2 = np.arange(TS)[None, :]
    idx3 = np.clip(c2 + 128 - k, -128, 128) + 128
    nbs3 = rel_bias[:, :, idx3].transpose(0, 2, 1, 3).reshape(NL, 128, NH * TS)
    k4 = np.arange(TS)[:, None]
    idx4 = np.clip(c2 - k4, -128, 128) + 128
    nbs4 = rel_bias[:, :, idx4].transpose(0, 2, 1, 3).reshape(NL, TS, NH * TS)
    return (np.ascontiguousarray(par), np.ascontiguousarray(nbp, np.float32), np.ascontiguousarray(nbs3, np.float32),
            np.ascontiguousarray(nbs4, np.float32))


def const_inputs():
    ident = np.eye(128, dtype=np.float32)
    s = np.arange(128)[:, None]
    t = np.arange(128)[None, :]
    negmask = np.where(s <= t, 0.0, -1e30).astype(np.float32)
    return ident, negmask


_T = 256
_NSS = 3


def make_in_maps(inp, NLC, T, TS, n_pairs, NT):
    f = lambda a: np.ascontiguousarray(np.asarray(a), dtype=np.float32)
    g = {k: f(v) for k, v in inp.items()}
    L = NT * T
    SB = g["x_sample"].shape[0]
    ident, negmask = const_inputs()
    maps = []
    for c in range(2 * n_pairs):
        s, st = c // 2, c % 2
        ls = slice(st * NLC, (st + 1) * NLC)
        par, nbp, nbs3, nbs4 = host_params(g["norm_mix_g"][ls], g["conv_w"][ls], g["q_norm_g"][ls], g["k_norm_g"][ls],
                                           g["rel_bias"][ls], g["b_igate"][ls], g["b_fgate"][ls],
                                           g["mlstm_norm_g"][ls], g["norm_mlp_g"][ls], TS)
        sidx = [min(max(2 * s + k - st, 2 * s), 2 * s + 1) % SB for k in range(_NSS)]
        role = np.zeros((128, 2), np.float32)
        role[:, st] = 1.0
        maps.append({
            "xp": g["x_prompt"][s % g["x_prompt"].shape[0], :L],
            "xs": np.ascontiguousarray(g["x_sample"][sidx]),
            "ck": np.ascontiguousarray(g["cache_att_k"][ls][:, sidx].transpose(1, 0, 2, 3, 4).reshape(_NSS, NLC, 512, 1024)),
            "cv": np.ascontiguousarray(g["cache_att_v"][ls][:, sidx].transpose(1, 0, 2, 3, 4).reshape(_NSS, NLC, 512, 1024)),
            "sconv": np.ascontiguousarray(g["state_conv"][ls][:, sidx].transpose(1, 0, 2, 3)),
            "smc": np.ascontiguousarray(g["state_mlstm_c"][ls][:, sidx].transpose(1, 0, 2, 3, 4)),
            "smn": np.ascontiguousarray(g["state_mlstm_n"][ls][:, sidx].transpose(1, 0, 2, 3)),
            "smm": np.ascontiguousarray(g["state_mlstm_m"][ls][:, sidx].transpose(1, 0, 2)),
            "role": role,
            "win": np.ascontiguousarray(g["w_in"][ls]), "wout": np.ascontiguousarray(g["w_out"][ls]),
            "wup": np.ascontiguousarray(g["w_up"][ls]), "wdn": np.ascontiguousarray(g["w_down"][ls]),
            "par": par, "nbp": nbp, "nbs3": nbs3, "nbs4": nbs4, "ident": ident, "negmask": negmask,
        })
    return maps


def assemble(R, NLC, T, TS, n_pairs, NT, SB):
    yp, ys = [], [None] * SB
    P = {k: [[None] * n_pairs for _ in range(2 * NLC)] for k in ("conv", "k", "v", "c", "n", "m")}
    S = {k: [[None] * SB for _ in range(2 * NLC)] for k in ("conv", "k", "v", "c", "n", "m")}
    for s in range(n_pairs):
        A, B = R[2 * s], R[2 * s + 1]
        yp.append(B["yp"][T:(NT + 1) * T])
        for j in range(2):
            if 2 * s + j < SB:
                ys[2 * s + j] = B["ys"][(j + 1) * TS:(j + 2) * TS]
        for st, C in ((0, A), (1, B)):
            for l in range(NLC):
                gl = st * NLC + l
                P["conv"][gl][s] = C["o_pconv"][st, l]
                P["c"][gl][s] = C["o_pc"][st, l]
                P["n"][gl][s] = C["o_pn"][st, l]
                P["m"][gl][s] = C["o_pm"][st, l]
                nsl = min(2, NT)
                P["k"][gl][s] = np.concatenate([C["o_pk"][st + i + (2 - nsl), l] for i in range(nsl)], axis=0)
                P["v"][gl][s] = np.concatenate([C["o_pv"][st + i + (2 - nsl), l] for i in range(nsl)], axis=0)
                for j in range(2):
                    if 2 * s + j < SB:
                        S["conv"][gl][2 * s + j] = C["o_sconv"][j + st, l]
                        S["k"][gl][2 * s + j] = C["o_sk"][j + st, l]
                        S["v"][gl][2 * s + j] = C["o_sv"][j + st, l]
                        S["c"][gl][2 * s + j] = C["o_sc"][j + st, l]
                        S["n"][gl][2 * s + j] = C["o_sn"][j + st, l]
                        S["m"][gl][2 * s + j] = C["o_sm"][j + st, l]
    st_ = lambda d: np.stack([np.stack(x) for x in d])
    KEEP = min(512, NT * T)
    NLT = 2 * NLC
    outs = (np.stack(yp), np.stack(ys), st_(P["conv"]),
            st_(P["k"]).reshape(NLT, n_pairs, KEEP, NH, HD), st_(P["v"]).reshape(NLT, n_pairs, KEEP, NH, HD),
            st_(P["c"]), st_(P["n"]), st_(P["m"]), st_(S["conv"]),
            st_(S["k"]).reshape(NLT, SB, TS, NH, HD), st_(S["v"]).reshape(NLT, SB, TS, NH, HD),
            st_(S["c"]), st_(S["n"]), st_(S["m"]))
    return tuple(np.ascontiguousarray(o, dtype=np.float32) for o in outs)


def kernel(x_prompt, x_sample, cache_att_k, cache_att_v, state_conv, state_mlstm_c, state_mlstm_n, state_mlstm_m,
           norm_mix_g, w_in, conv_w, q_norm_g, k_norm_g, rel_bias, b_igate, b_fgate, mlstm_norm_g, w_out,
           norm_mlp_g, w_up, w_down):
    inp = dict(x_prompt=x_prompt, x_sample=x_sample, cache_att_k=cache_att_k, cache_att_v=cache_att_v,
               state_conv=state_conv, state_mlstm_c=state_mlstm_c, state_mlstm_n=state_mlstm_n,
               state_mlstm_m=state_mlstm_m, norm_mix_g=norm_mix_g, w_in=w_in, conv_w=conv_w, q_norm_g=q_norm_g,
               k_norm_g=k_norm_g, rel_bias=rel_bias, b_igate=b_igate, b_fgate=b_fgate, mlstm_norm_g=mlstm_norm_g,
               w_out=w_out, norm_mlp_g=norm_mlp_g, w_up=w_up, w_down=w_down)
    NLAY = np.asarray(w_in).shape[0]
    B, L, _ = np.asarray(x_prompt).shape
    SB, TS, _ = np.asarray(x_sample).shape
    NLC = NLAY // 2
    NT = L // _T
    nc = build(NLC, NT, _T, TS, _NSS)
    in_maps = make_in_maps(inp, NLC, _T, TS, B, NT)
    res = run_bass_kernel_spmd(nc, in_maps, core_ids=list(range(2 * B)))
    return assemble(res.results, NLC, _T, TS, B, NT, SB)
```
